# Optimizing a Trainium2 kernel written in Bass

```python
import jax, jax.numpy as jnp
from jax import lax
import numpy as np

D_MODEL = 1024
BATCH = 16
SEQ = 2048
DEPTH = 1

CHUNK = 64
SB_HEADS = 8
SB_HEAD_DIM = 64
SB_WIDTH = SB_HEADS * SB_HEAD_DIM
SG_GROUPS = 8
SG_GROUP_DIM = 64
SG_WIDTH = SG_GROUPS * SG_GROUP_DIM
SG_BLOCK = 128
Q_BLOCK = 128
MIX_WIDTH = SB_WIDTH + SG_WIDTH
IN_WIDTH = 3 * SB_WIDTH + 2 * SG_WIDTH
D_FF = 4 * D_MODEL
EPS = 1e-6

kernel_name = "hybrid_stickbreak_gmlp_block"


def rmsnorm(x, g):
    xf = x.astype(jnp.float32)
    y = xf * lax.rsqrt(jnp.mean(xf * xf, axis=-1, keepdims=True) + EPS)
    return (y * g.astype(jnp.float32)).astype(x.dtype)


def layernorm(x, g, b):
    xf = x.astype(jnp.float32)
    mu = jnp.mean(xf, axis=-1, keepdims=True)
    xc = xf - mu
    var = jnp.mean(xc * xc, axis=-1, keepdims=True)
    y = xc * lax.rsqrt(var + EPS) * g.astype(jnp.float32) + b.astype(jnp.float32)
    return y.astype(x.dtype)


def stick_breaking_attention(q, k, v):
    b, s, h, dh = q.shape
    scale = 1.0 / np.sqrt(dh).astype(np.float32)
    qh = jnp.transpose(q, (0, 2, 1, 3))
    kh = jnp.transpose(k, (0, 2, 1, 3))
    vh = jnp.transpose(v, (0, 2, 1, 3))
    outs = []
    for blk in range(s // Q_BLOCK):
        q0 = blk * Q_BLOCK
        kv_len = q0 + Q_BLOCK
        qb = qh[:, :, q0:kv_len]
        kb = kh[:, :, :kv_len]
        vb = vh[:, :, :kv_len]
        z = jnp.einsum('bhqd,bhkd->bhqk', qb, kb).astype(jnp.float32) * scale
        t_idx = q0 + jnp.arange(Q_BLOCK)[:, None]
        s_idx = jnp.arange(kv_len)[None, :]
        causal = s_idx < t_idx
        log_beta = jax.nn.log_sigmoid(z)
        log_keep = jnp.where(causal, jax.nn.log_sigmoid(-z), 0.0)
        suffix = lax.cumsum(log_keep, axis=3, reverse=True) - log_keep
        a = jnp.where(causal, jnp.exp(log_beta + suffix), 0.0)
        outs.append(jnp.einsum('bhqk,bhkd->bhqd', a.astype(vb.dtype), vb))
    o = jnp.concatenate(outs, axis=2)
    return jnp.transpose(o, (0, 2, 1, 3)).reshape(b, s, h * dh)


def spatial_gating(u, vg, ln_g, ln_b, w_s, b_s):
    b, s, _ = u.shape
    vn = layernorm(vg, ln_g, ln_b)
    vr = vn.reshape(b, s // SG_BLOCK, SG_BLOCK, SG_GROUPS, SG_GROUP_DIM)
    pos = jnp.arange(SG_BLOCK)
    mask = (pos[None, :] // CHUNK) <= (pos[:, None] // CHUNK)
    w = jnp.where(mask[None], w_s, 0.0).astype(vr.dtype)
    mixed = jnp.einsum('gts,bnsgc->bntgc', w, vr) + jnp.transpose(b_s)[None, None, :, :, None]
    return u * mixed.reshape(b, s, SG_WIDTH)


def setup_inputs(seed: int = 0) -> dict:
    key = jax.random.key(seed)
    ks = jax.random.split(key, 14)
    f32 = jnp.float32
    x = jax.random.normal(ks[0], (BATCH, SEQ, D_MODEL), f32)
    norm_mix_g = 1.0 + 0.02 * jax.random.normal(ks[1], (DEPTH, D_MODEL), f32)
    w_in = jax.random.normal(ks[2], (DEPTH, D_MODEL, IN_WIDTH), f32) * D_MODEL ** -0.5
    sg_ln_g = 1.0 + 0.02 * jax.random.normal(ks[3], (DEPTH, SG_WIDTH), f32)
    sg_ln_b = 0.02 * jax.random.normal(ks[4], (DEPTH, SG_WIDTH), f32)
    sg_w = jax.random.normal(ks[5], (DEPTH, SG_GROUPS, SG_BLOCK, SG_BLOCK), f32) * SG_BLOCK ** -0.5
    sg_b = 1.0 + 0.02 * jax.random.normal(ks[6], (DEPTH, SG_GROUPS, SG_BLOCK), f32)
    out_norm_g = 1.0 + 0.02 * jax.random.normal(ks[7], (DEPTH, MIX_WIDTH), f32)
    w_out = jax.random.normal(ks[8], (DEPTH, MIX_WIDTH, D_MODEL), f32) * MIX_WIDTH ** -0.5
    norm_mlp_g = 1.0 + 0.02 * jax.random.normal(ks[9], (DEPTH, D_MODEL), f32)
    w_up = jax.random.normal(ks[10], (DEPTH, D_MODEL, D_FF), f32) * D_MODEL ** -0.5
    w_down = jax.random.normal(ks[11], (DEPTH, D_FF, D_MODEL), f32) * D_FF ** -0.5
    norm_final_g = 1.0 + 0.02 * jax.random.normal(ks[12], (D_MODEL,), f32)
    return {"x": x, "norm_mix_g": norm_mix_g, "w_in": w_in, "sg_ln_g": sg_ln_g,
            "sg_ln_b": sg_ln_b, "sg_w": sg_w, "sg_b": sg_b, "out_norm_g": out_norm_g,
            "w_out": w_out, "norm_mlp_g": norm_mlp_g, "w_up": w_up, "w_down": w_down,
            "norm_final_g": norm_final_g}


def reference(x, norm_mix_g, w_in, sg_ln_g, sg_ln_b, sg_w, sg_b, out_norm_g, w_out,
              norm_mlp_g, w_up, w_down, norm_final_g):
    b, s, _ = x.shape
    split_at = [SB_WIDTH, 2 * SB_WIDTH, 3 * SB_WIDTH, 3 * SB_WIDTH + SG_WIDTH]
    for layer in range(DEPTH):
        h = rmsnorm(x, norm_mix_g[layer])
        proj = jnp.einsum('bsd,de->bse', h, w_in[layer])
        q, k, v, u, vg = jnp.split(proj, split_at, axis=-1)
        a_out = stick_breaking_attention(q.reshape(b, s, SB_HEADS, SB_HEAD_DIM),
                                         k.reshape(b, s, SB_HEADS, SB_HEAD_DIM),
                                         v.reshape(b, s, SB_HEADS, SB_HEAD_DIM))
        b_out = spatial_gating(jax.nn.gelu(u), jax.nn.gelu(vg), sg_ln_g[layer], sg_ln_b[layer],
                               sg_w[layer], sg_b[layer])
        a_out = rmsnorm(a_out, out_norm_g[layer, :SB_WIDTH])
        b_out = rmsnorm(b_out, out_norm_g[layer, SB_WIDTH:])
        mix = jnp.concatenate([a_out, b_out], axis=-1)
        x = x + jnp.einsum('bse,ed->bsd', mix, w_out[layer])
        h = rmsnorm(x, norm_mlp_g[layer])
        act = jnp.square(jax.nn.relu(jnp.einsum('bsd,df->bsf', h, w_up[layer])))
        x = x + jnp.einsum('bsf,fd->bsd', act, w_down[layer])
    return rmsnorm(x, norm_final_g)
```

```python
import numpy as np
import ml_dtypes
from contextlib import ExitStack

import concourse.bass as bass
import concourse.mybir as mybir
from concourse.bass_utils import run_bass_kernel_spmd

F32 = mybir.dt.float32
BF16 = mybir.dt.bfloat16
AF = mybir.ActivationFunctionType
ALU = mybir.AluOpType
AX = mybir.AxisListType

NCORES = 8
BPC = 2
S = 2048
D = 1024
NT = S // 128
INW = 2560
DFF = 4096
EPS = 1e-6


class Tk:
    __slots__ = ("sem", "val")

    def __init__(self, sem, val):
        self.sem = sem
        self.val = val


class Reg:
    __slots__ = ("w", "r", "name")

    def __init__(self, name=""):
        self.w = None
        self.r = {}
        self.name = name


class Chan:
    def __init__(self, sem):
        self.sem = sem
        self.n = 0


class Eng:
    def __init__(self, eng, sem, name):
        self.eng = eng
        self.sem = sem
        self.n = 0
        self.seen = {}
        self.name = name

    def wait(self, tk):
        if tk is None:
            return
        k = id(tk.sem)
        if self.seen.get(k, 0) >= tk.val:
            return
        self.eng.wait_ge(tk.sem, tk.val)
        self.seen[k] = tk.val


class K:
    def __init__(self, nc, es):
        self.nc = nc
        self.es = es
        self.chans = []
        self.E = {}
        for nm, eng in (("pe", nc.tensor), ("act", nc.scalar), ("dve", nc.vector),
                        ("pool", nc.gpsimd), ("sp", nc.sync)):
            sem = es.enter_context(nc.semaphore("sem_" + nm))
            self.E[nm] = Eng(eng, sem, nm)
        self._nm = 0

    def chan(self):
        self._nm += 1
        c = Chan(self.es.enter_context(self.nc.semaphore("ch%d" % self._nm)))
        self.chans.append(c)
        return c

    def _deps(self, E, reads, writes):
        for r in reads:
            if r.w is not None:
                E.wait(r.w)
        pe = E.name == "pe"
        for w in writes:
            if w.w is not None and not (pe and w.w.sem is E.sem):
                E.wait(w.w)
            for tk in w.r.values():
                if not (pe and tk.sem is E.sem):
                    E.wait(tk)

    def _mark(self, tk, reads, writes):
        for r in reads:
            r.r[id(tk.sem)] = tk
        for w in writes:
            w.w = tk
            w.r = {}

    def op(self, e, fn, reads=(), writes=()):
        E = self.E[e]
        self._deps(E, reads, writes)
        inst = fn(E.eng)
        E.n += 1
        inst.then_inc(E.sem, 1)
        tk = Tk(E.sem, E.n)
        self._mark(tk, reads, writes)
        return tk

    def dma(self, q, ch, out, in_, reads=(), writes=()):
        E = self.E[q]
        self._deps(E, reads, writes)
        inst = E.eng.dma_start(out=out, in_=in_)
        ch.n += 1
        inst.then_inc(ch.sem, 16)
        tk = Tk(ch.sem, 16 * ch.n)
        self._mark(tk, reads, writes)
        return tk

    def barrier(self, scratch_ap):
        V = self.E["dve"]
        for nm, E in self.E.items():
            if E is not V and E.n > 0:
                V.wait(Tk(E.sem, E.n))
        for c in self.chans:
            if c.n > 0:
                V.wait(Tk(c.sem, 16 * c.n))
        if V.n > 0:
            V.wait(Tk(V.sem, V.n))
        inst = V.eng.memset(scratch_ap, 0.0)
        V.n += 1
        inst.then_inc(V.sem, 1)
        tk = Tk(V.sem, V.n)
        for nm, E in self.E.items():
            E.wait(tk)


def build():
    nc = bass.Bass("TRN2", target_bir_lowering=False)

    def din(name, shape):
        return nc.dram_tensor(name, list(shape), F32, kind="ExternalInput").ap()

    x = din("x", [BPC, S, D])
    w_in = din("w_in", [D, INW])
    w_out = din("w_out", [D, D])
    w_up = din("w_up", [D, DFF])
    w_down = din("w_down", [DFF, D])
    g_mix = din("g_mix", [128, 8])
    g_out = din("g_out", [128, 8])
    g_mlp = din("g_mlp", [128, 8])
    lng_d = din("lng_full", [128, 512])
    lnb_d = din("lnb_full", [128, 512])
    bias_d = din("bias_full", [128, 512])
    gf_d = din("gf_full", [128, D])
    sgwT_d = din("sgwT", [128, 8, 128])
    ident_d = din("ident", [128, 128])
    ntri_d = din("ntri", [128, 128])
    nones_d = din("nones", [128, 128])
    masks_d = din("masks", [128, 4, 512])
    y = nc.dram_tensor("y", [BPC, S, D], F32, kind="ExternalOutput").ap()

    with ExitStack() as es:
        k = K(nc, es)

        uid = [0]

        def sb(es_, name, shape, dt):
            uid[0] += 1
            return es_.enter_context(nc.sbuf_tensor("s%d_%s" % (uid[0], name), list(shape), dt))

        def ps(es_, name, shape, dt=F32):
            uid[0] += 1
            return es_.enter_context(nc.psum_tensor("p%d_%s" % (uid[0], name), list(shape), dt))

        ident = sb(es, "ident", [128, 128], BF16)
        ntri = sb(es, "ntri", [128, 128], BF16)
        nones = sb(es, "nones", [128, 128], BF16)
        masks = sb(es, "masks", [128, 4, 512], BF16)
        ones32 = sb(es, "ones32", [128, 1], F32)
        eps_t = sb(es, "eps_t", [128, 1], F32)
        one_t = sb(es, "one_t", [128, 1], F32)
        scratch = sb(es, "scratch", [128, 8], F32)
        gmix = sb(es, "gmix", [128, 8], F32)
        gout = sb(es, "gout", [128, 8], F32)
        gmlp = sb(es, "gmlp", [128, 8], F32)
        r_const = Reg("const")

        cch = k.chan()
        cch2 = k.chan()
        for dst, src in ((ident, ident_d), (ntri, ntri_d), (nones, nones_d)):
            k.dma("pool", cch, dst[:], src[:, :], writes=[r_const])
        k.dma("pool", cch, masks[:], masks_d[:, :, :], writes=[r_const])
        for dst, src in ((gmix, g_mix), (gout, g_out), (gmlp, g_mlp)):
            k.dma("sp", cch2, dst[:], src[:, :], writes=[r_const])
        k.op("dve", lambda e: e.memset(ones32[:], 1.0), writes=[r_const])
        k.op("dve", lambda e: e.memset(eps_t[:], EPS), writes=[r_const])
        k.op("dve", lambda e: e.memset(one_t[:], 1.0), writes=[r_const])
        k.barrier(scratch[:, 0:1])

        y_reg = [[Reg("y%d_%d" % (b, i)) for i in range(NT)] for b in range(BPC)]

        with ExitStack() as esA:
            w_in_bf = sb(esA, "w_in_bf", [128, 8, INW], BF16)
            w_out_bf = sb(esA, "w_out_bf", [128, 8, D], BF16)
            sgwT = sb(esA, "sgwT", [128, 8, 128], BF16)
            lng = sb(esA, "lng", [128, 512], F32)
            lnb = sb(esA, "lnb", [128, 512], F32)
            biasf = sb(esA, "biasf", [128, 512], F32)
            QT = sb(esA, "QT", [128, 4, S], BF16)
            KT = sb(esA, "KT", [128, 4, S], BF16)
            Vsb = sb(esA, "Vsb", [128, NT, 512], BF16)
            bT = sb(esA, "bT", [128, 4, S], BF16)
            ssa_sb = sb(esA, "ssa_sb", [128, NT], F32)
            r_w = Reg("wA")

            with ExitStack() as esP:
                stg = [sb(esP, "stg%d" % i, [128, INW], F32) for i in range(2)]
                stg_r = [Reg("stg%d" % i) for i in range(2)]
                stg_ch = [k.chan() for _ in range(2)]
                pch = k.chan()
                r_sg = Reg("sgw")
                sgch = k.chan()
                k.dma("pool", sgch, sgwT[:], sgwT_d[:, :, :], writes=[r_sg])
                k.dma("sp", pch, lng[:], lng_d[:, :], writes=[r_w])
                k.dma("sp", pch, lnb[:], lnb_d[:, :], writes=[r_w])
                k.dma("sp", pch, biasf[:], bias_d[:, :], writes=[r_w])
                n = 0
                for c in range(8):
                    sl = n % 2
                    k.dma("sp", stg_ch[sl], stg[sl][:, :], w_in[c * 128:(c + 1) * 128, :],
                          writes=[stg_r[sl]])
                    eng = "dve" if c % 2 == 0 else "pool"
                    k.op(eng, lambda e, c=c, sl=sl: e.tensor_scalar(
                        out=w_in_bf[:, c, :], in0=stg[sl][:, :], scalar1=gmix[:, c:c + 1],
                        scalar2=None, op0=ALU.mult), reads=[stg_r[sl], r_const], writes=[r_w])
                    n += 1
                for c in range(8):
                    sl = n % 2
                    k.dma("sp", stg_ch[sl], stg[sl][:, 0:D], w_out[c * 128:(c + 1) * 128, :],
                          writes=[stg_r[sl]])
                    eng = "dve" if c % 2 == 0 else "pool"
                    k.op(eng, lambda e, c=c, sl=sl: e.tensor_scalar(
                        out=w_out_bf[:, c, :], in0=stg[sl][:, 0:D], scalar1=gout[:, c:c + 1],
                        scalar2=None, op0=ALU.mult), reads=[stg_r[sl], r_const], writes=[r_w])
                    n += 1
                k.op("pool", lambda e: e.memset(sgwT[64:128, :, 0:64], 0.0), reads=[r_sg], writes=[r_sg])
                k.barrier(scratch[:, 0:1])

            for b in range(BPC):
                with ExitStack() as e1:
                    NXS = 5
                    xt = [sb(e1, "xt%d" % i, [128, D], F32) for i in range(NXS)]
                    xt_r = [Reg() for _ in range(NXS)]
                    xt_ch = [k.chan() for _ in range(NXS)]
                    junk = sb(e1, "junk", [128, D], BF16)
                    junk_r = Reg()
                    xn = [sb(e1, "xn%d" % i, [128, D], BF16) for i in range(2)]
                    xn_r = [Reg() for _ in range(2)]
                    hT = [sb(e1, "hT%d" % i, [128, 8, 512], BF16) for i in range(2)]
                    hT_r = [[Reg() for _ in range(4)] for _ in range(2)]
                    gu = [sb(e1, "gu%d" % i, [128, 512], F32) for i in range(4)]
                    gv = [sb(e1, "gv%d" % i, [128, 512], F32) for i in range(4)]
                    gu_r = [Reg() for _ in range(4)]
                    gv_r = [Reg() for _ in range(4)]
                    braw = gv
                    braw_r = gv_r
                    t1 = sb(e1, "t1", [128, 512], F32)
                    t1_r = Reg()
                    t2 = sb(e1, "t2", [128, 512], F32)
                    t2_r = Reg()
                    vn = [sb(e1, "vn%d" % i, [128, 512], BF16) for i in range(2)]
                    vn_r = [Reg() for _ in range(2)]
                    bn = [sb(e1, "bn%d" % i, [128, 512], BF16) for i in range(2)]
                    bn_r = [Reg() for _ in range(2)]
                    st = sb(e1, "st", [128, 2, 32], F32)
                    st_r = [[Reg() for _ in range(8)] for _ in range(2)]
                    bst = sb(e1, "bst", [128, 4, 6], F32)
                    bst_r = [Reg() for _ in range(4)]
                    mv = sb(e1, "mv", [128, 4, 2], F32)
                    mv_r = Reg()
                    tp_ps = ps(e1, "tp_ps", [128, 8, 128], BF16)
                    tp_r = Reg()
                    qk_ps = [ps(e1, "qk_ps%d" % i, [128, 512]) for i in range(2)]
                    qk_r = [Reg() for _ in range(2)]
                    v_ps = ps(e1, "v_ps", [128, 512])
                    v_r = Reg()
                    u_ps = ps(e1, "u_ps", [128, 512])
                    u_r = Reg()
                    vg_ps = ps(e1, "vg_ps", [128, 512])
                    vg_r = Reg()
                    mix_ps = ps(e1, "mix_ps", [128, 512])
                    mix_r = Reg()
                    bt_ps = ps(e1, "bt_ps", [128, 4, 128], BF16)
                    bt_r = Reg()
                    QT_r = Reg()
                    KT_r = Reg()
                    V_r = Reg()
                    bT_r = Reg()

                    nload = 0
                    nqk = 0
                    for G in range(4):
                        gp = G % 2
                        R = st_r[gp]
                        slots = []
                        for j in range(4):
                            i = 4 * G + j
                            sl = nload % NXS
                            nload += 1
                            slots.append(sl)
                            k.dma("sp", xt_ch[sl], xt[sl][:, :], x[b, i * 128:(i + 1) * 128, :],
                                  writes=[xt_r[sl]])
                            k.op("act", lambda e, sl=sl, j=j: e.activation(
                                out=junk[:, :], in_=xt[sl][:, :], func=AF.Square,
                                accum_out=st[:, gp, j:j + 1]),
                                reads=[xt_r[sl]], writes=[junk_r, R[0]])
                        k.op("act", lambda e: e.activation(
                            out=st[:, gp, 4:8], in_=st[:, gp, 0:4], func=AF.Sqrt,
                            bias=eps_t[:, 0:1], scale=1.0 / D), reads=[R[0]], writes=[R[1]])
                        k.op("dve", lambda e: e.reciprocal(out=st[:, gp, 8:12], in_=st[:, gp, 4:8]),
                             reads=[R[1]], writes=[R[2]])
                        for j in range(4):
                            sl = slots[j]
                            xs = j % 2
                            k.op("dve", lambda e, sl=sl, xs=xs, j=j: e.tensor_scalar(
                                out=xn[xs][:, :], in0=xt[sl][:, :], scalar1=st[:, gp, 8 + j:9 + j],
                                scalar2=None, op0=ALU.mult),
                                reads=[xt_r[sl], R[2]], writes=[xn_r[xs]])
                            for c in range(8):
                                k.op("pe", lambda e, c=c, xs=xs: e.transpose(
                                    tp_ps[:, c, :], xn[xs][:, c * 128:(c + 1) * 128], ident[:]),
                                    reads=[xn_r[xs]], writes=[tp_r])
                            k.op("act", lambda e, j=j: e.copy(
                                out=hT[gp][:, :, j * 128:(j + 1) * 128], in_=tp_ps[:, :, :]),
                                reads=[tp_r], writes=[hT_r[gp][j]])
                        tc0 = G * 512
                        for eb in range(8):
                            qs = nqk % 2
                            nqk += 1
                            for c in range(8):
                                k.op("pe", lambda e, c=c, eb=eb, qs=qs: e.matmul(
                                    qk_ps[qs][:, :], lhsT=w_in_bf[:, c, eb * 128:(eb + 1) * 128],
                                    rhs=hT[gp][:, c, :], start=(c == 0), stop=(c == 7)),
                                    reads=hT_r[gp] + [r_w], writes=[qk_r[qs]])
                            if eb < 4:
                                k.op("dve", lambda e, eb=eb, qs=qs: e.tensor_scalar(
                                    out=QT[:, eb, tc0:tc0 + 512], in0=qk_ps[qs][:, :], scalar1=0.125,
                                    scalar2=None, op0=ALU.mult), reads=[qk_r[qs]], writes=[QT_r])
                            else:
                                k.op("act", lambda e, eb=eb, qs=qs: e.copy(
                                    out=KT[:, eb - 4, tc0:tc0 + 512], in_=qk_ps[qs][:, :]),
                                    reads=[qk_r[qs]], writes=[KT_r])
                        for j in range(4):
                            i = 4 * G + j
                            for (pst, pr, c0) in ((v_ps, v_r, 1024), (u_ps, u_r, 1536), (vg_ps, vg_r, 2048)):
                                for c in range(8):
                                    k.op("pe", lambda e, c=c, pst=pst, c0=c0, j=j: e.matmul(
                                        pst[:, :], lhsT=hT[gp][:, c, j * 128:(j + 1) * 128],
                                        rhs=w_in_bf[:, c, c0:c0 + 512], start=(c == 0), stop=(c == 7)),
                                        reads=[hT_r[gp][j], r_w], writes=[pr])
                            k.op("dve", lambda e, i=i: e.tensor_copy(out=Vsb[:, i, :], in_=v_ps[:, :]),
                                 reads=[v_r], writes=[V_r])
                            k.op("act", lambda e, j=j: e.activation(
                                out=gu[j][:, :], in_=u_ps[:, :], func=AF.Gelu_apprx_tanh),
                                reads=[u_r], writes=[gu_r[j]])
                            k.op("act", lambda e, j=j: e.activation(
                                out=gv[j][:, :], in_=vg_ps[:, :], func=AF.Gelu_apprx_tanh),
                                reads=[vg_r], writes=[gv_r[j]])
                            k.op("dve", lambda e, j=j: e.bn_stats(out=bst[:, j, :], in_=gv[j][:, :]),
                                 reads=[gv_r[j]], writes=[bst_r[j]])
                            k.op("dve", lambda e, j=j: e.bn_aggr(out=mv[:, j, :], in_=bst[:, j, :]),
                                 reads=[bst_r[j]], writes=[mv_r])
                        k.op("act", lambda e: e.activation(
                            out=st[:, gp, 12:16], in_=mv[:, :, 1], func=AF.Sqrt,
                            bias=eps_t[:, 0:1], scale=1.0), reads=[mv_r], writes=[R[3]])
                        k.op("dve", lambda e: e.reciprocal(out=st[:, gp, 16:20], in_=st[:, gp, 12:16]),
                             reads=[R[3]], writes=[R[4]])
                        for j in range(4):
                            vs = j % 2
                            k.op("dve", lambda e, j=j: e.scalar_tensor_tensor(
                                out=t1[:, :], in0=gv[j][:, :], scalar=mv[:, j, 0:1], in1=lng[:, :],
                                op0=ALU.subtract, op1=ALU.mult),
                                reads=[gv_r[j], mv_r, r_w], writes=[t1_r])
                            k.op("dve", lambda e, j=j, vs=vs: e.scalar_tensor_tensor(
                                out=vn[vs][:, :], in0=t1[:, :], scalar=st[:, gp, 16 + j:17 + j], in1=lnb[:, :],
                                op0=ALU.mult, op1=ALU.add),
                                reads=[t1_r, R[4], r_w], writes=[vn_r[vs]])
                            for g in range(8):
                                k.op("pe", lambda e, g=g, vs=vs: e.matmul(
                                    mix_ps[:, g * 64:(g + 1) * 64], lhsT=sgwT[:, g, :],
                                    rhs=vn[vs][:, g * 64:(g + 1) * 64], start=True, stop=True),
                                    reads=[vn_r[vs], r_w], writes=[mix_r])
                            k.op("dve", lambda e: e.tensor_tensor(
                                out=t2[:, :], in0=mix_ps[:, :], in1=biasf[:, :], op=ALU.add),
                                reads=[mix_r, r_w], writes=[t2_r])
                            k.op("pool", lambda e, j=j: e.tensor_tensor(
                                out=braw[j][:, :], in0=t2[:, :], in1=gu[j][:, :], op=ALU.mult),
                                reads=[t2_r, gu_r[j]], writes=[braw_r[j]])
                            k.op("act", lambda e, j=j: e.activation(
                                out=junk[:, 0:512], in_=braw[j][:, :], func=AF.Square,
                                accum_out=st[:, gp, 20 + j:21 + j]),
                                reads=[braw_r[j]], writes=[junk_r, R[5]])
                        k.op("act", lambda e: e.activation(
                            out=st[:, gp, 24:28], in_=st[:, gp, 20:24], func=AF.Sqrt,
                            bias=eps_t[:, 0:1], scale=1.0 / 512), reads=[R[5]], writes=[R[6]])
                        k.op("dve", lambda e: e.reciprocal(out=st[:, gp, 28:32], in_=st[:, gp, 24:28]),
                             reads=[R[6]], writes=[R[7]])
                        for j in range(4):
                            i = 4 * G + j
                            bs = j % 2
                            k.op("dve", lambda e, j=j, bs=bs: e.tensor_scalar(
                                out=bn[bs][:, :], in0=braw[j][:, :], scalar1=st[:, gp, 28 + j:29 + j],
                                scalar2=None, op0=ALU.mult),
                                reads=[braw_r[j], R[7]], writes=[bn_r[bs]])
                            for c in range(4):
                                k.op("pe", lambda e, c=c, bs=bs: e.transpose(
                                    bt_ps[:, c, :], bn[bs][:, c * 128:(c + 1) * 128], ident[:]),
                                    reads=[bn_r[bs]], writes=[bt_r])
                            k.op("dve", lambda e, i=i: e.tensor_copy(
                                out=bT[:, :, i * 128:(i + 1) * 128], in_=bt_ps[:, :, :]),
                                reads=[bt_r], writes=[bT_r])
                    k.barrier(scratch[:, 0:1])

                with ExitStack() as e2:
                    aT = sb(e2, "aT", [128, 4, S], BF16)
                    aT_r = Reg()
                    with ExitStack() as e3:
                        QTz = [[sb(e3, "QTz%d_%d" % (p, i), [128, 512], BF16) for i in range(2)]
                               for p in range(2)]
                        QTz_r = [[Reg() for _ in range(2)] for _ in range(2)]
                        ebuf = [sb(e3, "ebuf%d" % i, [128, 512], F32) for i in range(2)]
                        ebuf_r = [Reg() for _ in range(2)]
                        spb = [sb(e3, "spb%d" % i, [128, 512], BF16) for i in range(4)]
                        spb_r = [Reg() for _ in range(4)]
                        Sb = [sb(e3, "Sb%d" % i, [128, 512], BF16) for i in range(2)]
                        Sb_r = [Reg() for _ in range(2)]
                        Ab = [sb(e3, "Ab%d" % i, [128, 512], BF16) for i in range(3)]
                        Ab_r = [Reg() for _ in range(3)]
                        sq = sb(e3, "sq", [128, 512], F32)
                        sq_r = Reg()
                        z_ps = [ps(e3, "z_ps%d" % i, [128, 512]) for i in range(2)]
                        z_r = [Reg() for _ in range(2)]
                        f_ps = [ps(e3, "f_ps%d" % i, [128, 512]) for i in range(3)]
                        f_r = [Reg() for _ in range(3)]
                        o_ps = [ps(e3, "o_ps%d" % i, [128, 512]) for i in range(2)]
                        o_r = [Reg() for _ in range(2)]
                        ssa_ps = ps(e3, "ssa_ps", [128, 16])
                        ssa_r = Reg()
                        ssa_sb_r = Reg()
                        r_att = Reg()
                        for p in range(2):
                            for i in range(2):
                                k.op("pool", lambda e, p=p, i=i: e.memset(QTz[p][i][:, :], 0.0),
                                     writes=[QTz_r[p][i]])

                        steps = []
                        nhead = 0
                        for qg in range(4):
                            for hp in range(4):
                                for hh in range(2):
                                    nk = 4 * qg + 4
                                    for si in range(nk):
                                        kb = nk - 1 - si
                                        steps.append(dict(qg=qg, hp=hp, hh=hh, kb=kb, first=(si == 0),
                                                          last=(si == nk - 1), j=kb - 4 * qg, hidx=nhead))
                                    nhead += 1
                        state = {"S": None}

                        def stage1(i):
                            s_ = steps[i]
                            hp, hh, kb, qg = s_["hp"], s_["hh"], s_["kb"], s_["qg"]
                            pb = 64 * hh
                            qb_ = (s_["hidx"] // 2) % 2
                            qz, qzr = QTz[hh][qb_], QTz_r[hh][qb_]
                            if s_["first"]:
                                k.op("pool", lambda e: e.tensor_copy(
                                    out=qz[pb:pb + 64, :], in_=QT[pb:pb + 64, hp, qg * 512:(qg + 1) * 512]),
                                    reads=[r_att], writes=[qzr])
                            zb = i % 2
                            fb = i % 3
                            k.op("pe", lambda e: e.matmul(
                                z_ps[zb][:, :], lhsT=KT[:, hp, kb * 128:(kb + 1) * 128], rhs=qz[:, :],
                                start=True, stop=True), reads=[qzr, r_att], writes=[z_r[zb]])
                            k.op("pe", lambda e: e.matmul(
                                f_ps[fb][:, :], lhsT=KT[:, hp, kb * 128:(kb + 1) * 128], rhs=qz[:, :],
                                start=True, stop=False), reads=[qzr, r_att], writes=[f_r[fb]])
                            k.op("act", lambda e: e.activation(
                                out=ebuf[zb][:, :], in_=z_ps[zb][:, :], func=AF.Exp),
                                reads=[z_r[zb]], writes=[ebuf_r[zb]])
                            sb_i = i % 4
                            k.op("act", lambda e: e.activation(
                                out=spb[sb_i][:, :], in_=ebuf[zb][:, :], func=AF.Ln, bias=one_t[:, 0:1],
                                scale=1.0), reads=[ebuf_r[zb]], writes=[spb_r[sb_i]])
                            if s_["j"] >= 0:
                                jj = s_["j"]
                                k.op("pool", lambda e: e.tensor_tensor(
                                    out=spb[sb_i][:, :], in0=spb[sb_i][:, :], in1=masks[:, jj, :], op=ALU.mult),
                                    reads=[spb_r[sb_i]], writes=[spb_r[sb_i]])

                        def stage2(i):
                            s_ = steps[i]
                            fb = i % 3
                            sb_i = i % 4
                            first = s_["first"]
                            k.op("pe", lambda e: e.matmul(
                                f_ps[fb][:, :], lhsT=ntri[:, :], rhs=spb[sb_i][:, :],
                                start=False, stop=first), reads=[spb_r[sb_i]], writes=[f_r[fb]])
                            if not first:
                                S_t, S_rg = state["S"]
                                k.op("pe", lambda e: e.matmul(
                                    f_ps[fb][:, :], lhsT=nones[:, :], rhs=S_t[:, :],
                                    start=False, stop=True), reads=[S_rg], writes=[f_r[fb]])
                            if not s_["last"]:
                                if first:
                                    state["S"] = (spb[sb_i], spb_r[sb_i])
                                    state["Sn"] = 0
                                else:
                                    S_t, S_rg = state["S"]
                                    nb = state["Sn"]
                                    state["Sn"] = 1 - nb
                                    k.op("dve", lambda e: e.tensor_tensor(
                                        out=Sb[nb][:, :], in0=S_t[:, :], in1=spb[sb_i][:, :], op=ALU.add),
                                        reads=[S_rg, spb_r[sb_i]], writes=[Sb_r[nb]])
                                    state["S"] = (Sb[nb], Sb_r[nb])
                            ab = i % 3
                            k.op("act", lambda e: e.activation(
                                out=Ab[ab][:, :], in_=f_ps[fb][:, :], func=AF.Exp),
                                reads=[f_r[fb]], writes=[Ab_r[ab]])
                            if s_["j"] >= 0:
                                jj = s_["j"]
                                k.op("pool", lambda e: e.tensor_tensor(
                                    out=Ab[ab][:, :], in0=Ab[ab][:, :], in1=masks[:, jj, :], op=ALU.mult),
                                    reads=[Ab_r[ab]], writes=[Ab_r[ab]])

                        def stage3(i):
                            s_ = steps[i]
                            hp, hh, kb, qg = s_["hp"], s_["hh"], s_["kb"], s_["qg"]
                            pb = 64 * hh
                            ab = i % 3
                            ob = s_["hidx"] % 2
                            k.op("pe", lambda e: e.matmul(
                                o_ps[ob][:, :], lhsT=Vsb[:, kb, hp * 128:(hp + 1) * 128], rhs=Ab[ab][:, :],
                                start=s_["first"], stop=s_["last"]),
                                reads=[Ab_r[ab], r_att], writes=[o_r[ob]])
                            if s_["last"]:
                                cs = slice(qg * 512, (qg + 1) * 512)
                                k.op("dve", lambda e: e.tensor_copy(
                                    out=aT[pb:pb + 64, hp, cs], in_=o_ps[ob][pb:pb + 64, :]),
                                    reads=[o_r[ob]], writes=[aT_r])
                                k.op("pool", lambda e: e.tensor_tensor(
                                    out=sq[pb:pb + 64, :], in0=aT[pb:pb + 64, hp, cs], in1=aT[pb:pb + 64, hp, cs],
                                    op=ALU.mult), reads=[aT_r], writes=[sq_r])
                                if hh == 1:
                                    for jj in range(4):
                                        k.op("pe", lambda e, jj=jj: e.matmul(
                                            ssa_ps[:, hp * 4 + jj:hp * 4 + jj + 1],
                                            lhsT=sq[:, jj * 128:(jj + 1) * 128], rhs=ones32[:, 0:1],
                                            start=True, stop=True), reads=[sq_r, r_const], writes=[ssa_r])
                                    if hp == 3:
                                        k.op("dve", lambda e: e.tensor_reduce(
                                            out=ssa_sb[:, qg * 4:(qg + 1) * 4],
                                            in_=ssa_ps[:, 0:16].rearrange("p (h j) -> p j h", h=4),
                                            axis=AX.X, op=ALU.add), reads=[ssa_r], writes=[ssa_sb_r])

                        nst = len(steps)
                        for it in range(nst + 2):
                            if it < nst:
                                stage1(it)
                            if 0 <= it - 1 < nst:
                                stage2(it - 1)
                            if 0 <= it - 2 < nst:
                                stage3(it - 2)
                        k.barrier(scratch[:, 0:1])

                    with ExitStack() as e4:
                        xr = [sb(e4, "xr%d" % i, [128, D], F32) for i in range(3)]
                        xr_r = [Reg() for _ in range(3)]
                        xr_ch = [k.chan() for _ in range(3)]
                        xo_ch = [k.chan() for _ in range(3)]
                        sda = sb(e4, "sda", [128, NT], F32)
                        rsa = sb(e4, "rsa", [128, NT], F32)
                        rsa_r = Reg()
                        acc = [[ps(e4, "acc%d_%d" % (i, h), [128, 512]) for h in range(4)] for i in range(2)]
                        acc_r = [[Reg() for _ in range(4)] for _ in range(2)]
                        r_all = Reg()
                        k.op("act", lambda e: e.activation(
                            out=sda[:, :], in_=ssa_sb[:, :], func=AF.Sqrt, bias=eps_t[:, 0:1],
                            scale=1.0 / 512), reads=[r_all], writes=[rsa_r])
                        k.op("dve", lambda e: e.reciprocal(out=rsa[:, :], in_=sda[:, :]),
                             reads=[rsa_r], writes=[rsa_r])
                        for i in range(NT):
                            sl = i % 3
                            ab = i % 2
                            k.dma("sp", xr_ch[sl], xr[sl][:, :], x[b, i * 128:(i + 1) * 128, :],
                                  writes=[xr_r[sl]])
                            for hf in range(2):
                                for c in range(4):
                                    k.op("pe", lambda e, c=c, hf=hf: e.matmul(
                                        acc[ab][hf][:, :], lhsT=aT[:, c, i * 128:(i + 1) * 128],
                                        rhs=w_out_bf[:, c, hf * 512:(hf + 1) * 512],
                                        start=(c == 0), stop=(c == 3)), reads=[r_all], writes=[acc_r[ab][hf]])
                                for c in range(4):
                                    k.op("pe", lambda e, c=c, hf=hf: e.matmul(
                                        acc[ab][2 + hf][:, :], lhsT=bT[:, c, i * 128:(i + 1) * 128],
                                        rhs=w_out_bf[:, 4 + c, hf * 512:(hf + 1) * 512],
                                        start=(c == 0), stop=(c == 3)), reads=[r_all], writes=[acc_r[ab][2 + hf]])
                            for hf in range(2):
                                cs = slice(hf * 512, (hf + 1) * 512)
                                k.op("dve", lambda e, hf=hf, cs=cs: e.scalar_tensor_tensor(
                                    out=xr[sl][:, cs], in0=acc[ab][hf][:, :], scalar=rsa[:, i:i + 1],
                                    in1=xr[sl][:, cs], op0=ALU.mult, op1=ALU.add),
                                    reads=[acc_r[ab][hf], rsa_r, xr_r[sl]], writes=[xr_r[sl]])
                                k.op("dve", lambda e, hf=hf, cs=cs: e.tensor_tensor(
                                    out=xr[sl][:, cs], in0=xr[sl][:, cs], in1=acc[ab][2 + hf][:, :], op=ALU.add),
                                    reads=[acc_r[ab][2 + hf], xr_r[sl]], writes=[xr_r[sl]])
                            k.dma("sp", xo_ch[sl], y[b, i * 128:(i + 1) * 128, :], xr[sl][:, :],
                                  reads=[xr_r[sl]], writes=[y_reg[b][i]])
                        k.barrier(scratch[:, 0:1])

        with ExitStack() as esB:
            w_up_bf = sb(esB, "w_up_bf", [128, 8, DFF], BF16)
            w_dn_bf = sb(esB, "w_dn_bf", [128, 32, D], BF16)
            gf = sb(esB, "gf", [128, D], F32)
            r_wB = Reg()
            with ExitStack() as esP:
                stg = [sb(esP, "stgB%d" % i, [128, DFF], F32) for i in range(2)]
                stg_r = [Reg() for _ in range(2)]
                stg_ch = [k.chan() for _ in range(2)]
                wch = k.chan()
                wch2 = k.chan()
                k.dma("sp", wch2, gf[:], gf_d[:, :], writes=[r_wB])
                for q in range(4):
                    k.dma("pool", wch, w_dn_bf[:, q * 8:(q + 1) * 8, :],
                          w_down[q * 1024:(q + 1) * 1024, :].rearrange("(c p) d -> p c d", p=128),
                          writes=[r_wB])
                for c in range(8):
                    sl = c % 2
                    k.dma("sp", stg_ch[sl], stg[sl][:, :], w_up[c * 128:(c + 1) * 128, :],
                          writes=[stg_r[sl]])
                    k.op("dve", lambda e, c=c, sl=sl: e.tensor_scalar(
                        out=w_up_bf[:, c, 0:2048], in0=stg[sl][:, 0:2048], scalar1=gmlp[:, c:c + 1],
                        scalar2=None, op0=ALU.mult), reads=[stg_r[sl], r_const], writes=[r_wB])
                    k.op("act", lambda e, c=c, sl=sl: e.activation(
                        out=w_up_bf[:, c, 2048:4096], in_=stg[sl][:, 2048:4096], func=AF.Copy,
                        scale=gmlp[:, c:c + 1]), reads=[stg_r[sl], r_const], writes=[r_wB])
                k.barrier(scratch[:, 0:1])

            with ExitStack() as e5:
                x1t = [sb(e5, "x1t%d" % i, [128, D], F32) for i in range(4)]
                x1_r = [Reg() for _ in range(4)]
                x1_ch = [k.chan() for _ in range(4)]
                xo_ch = [k.chan() for _ in range(4)]
                junk = sb(e5, "junkB", [128, D], BF16)
                junk_r = Reg()
                xn = [sb(e5, "xnB%d" % i, [128, D], BF16) for i in range(2)]
                xn_r = [Reg() for _ in range(2)]
                h2T = sb(e5, "h2T", [128, 8, 512], BF16)
                h2_r = [Reg() for _ in range(4)]
                actT = sb(e5, "actT", [128, 32, 512], BF16)
                act_r = [Reg() for _ in range(32)]
                rr = [sb(e5, "rr%d" % i, [128, 512], F32) for i in range(2)]
                rr_r = [Reg() for _ in range(2)]
                st = sb(e5, "stB", [128, 32], F32)
                R = [Reg() for _ in range(8)]
                tp_ps = ps(e5, "tpB", [128, 8, 128], BF16)
                tp_r = Reg()
                up_ps = [ps(e5, "up_ps%d" % i, [128, 512]) for i in range(3)]
                up_r = [Reg() for _ in range(3)]
                dn_ps = [ps(e5, "dn_ps%d" % i, [128, 512]) for i in range(4)]
                dn_r = [Reg() for _ in range(4)]
                out_chs = []
                nup = 0
                ndn = 0
                for blk in range(BPC * 4):
                    b = blk // 4
                    G = blk % 4
                    for j in range(4):
                        i = 4 * G + j
                        k.dma("sp", x1_ch[j], x1t[j][:, :], y[b, i * 128:(i + 1) * 128, :],
                              reads=[y_reg[b][i]], writes=[x1_r[j]])
                        k.op("act", lambda e, j=j: e.activation(
                            out=junk[:, :], in_=x1t[j][:, :], func=AF.Square, accum_out=st[:, j:j + 1]),
                            reads=[x1_r[j]], writes=[junk_r, R[0]])
                    k.op("act", lambda e: e.activation(
                        out=st[:, 4:8], in_=st[:, 0:4], func=AF.Sqrt, bias=eps_t[:, 0:1], scale=1.0 / D),
                        reads=[R[0]], writes=[R[1]])
                    k.op("dve", lambda e: e.reciprocal(out=st[:, 8:12], in_=st[:, 4:8]),
                         reads=[R[1]], writes=[R[2]])
                    for j in range(4):
                        xs = j % 2
                        k.op("dve", lambda e, j=j, xs=xs: e.tensor_scalar(
                            out=xn[xs][:, :], in0=x1t[j][:, :], scalar1=st[:, 8 + j:9 + j], scalar2=None,
                            op0=ALU.mult), reads=[x1_r[j], R[2]], writes=[xn_r[xs]])
                        for c in range(8):
                            k.op("pe", lambda e, c=c, xs=xs: e.transpose(
                                tp_ps[:, c, :], xn[xs][:, c * 128:(c + 1) * 128], ident[:]),
                                reads=[xn_r[xs]], writes=[tp_r])
                        k.op("dve", lambda e, j=j: e.tensor_copy(
                            out=h2T[:, :, j * 128:(j + 1) * 128], in_=tp_ps[:, :, :]),
                            reads=[tp_r], writes=[h2_r[j]])
                    for fc in range(32):
                        ub = nup % 3
                        rb = nup % 2
                        nup += 1
                        for c in range(8):
                            k.op("pe", lambda e, c=c, fc=fc, ub=ub: e.matmul(
                                up_ps[ub][:, :], lhsT=w_up_bf[:, c, fc * 128:(fc + 1) * 128],
                                rhs=h2T[:, c, :], start=(c == 0), stop=(c == 7)),
                                reads=h2_r + [r_wB], writes=[up_r[ub]])
                        k.op("act", lambda e, ub=ub, rb=rb: e.activation(
                            out=rr[rb][:, :], in_=up_ps[ub][:, :], func=AF.Relu),
                            reads=[up_r[ub]], writes=[rr_r[rb]])
                        k.op("pool", lambda e, fc=fc, rb=rb: e.tensor_tensor(
                            out=actT[:, fc, :], in0=rr[rb][:, :], in1=rr[rb][:, :], op=ALU.mult),
                            reads=[rr_r[rb]], writes=[act_r[fc]])
                    for j in range(4):
                        for hf in range(2):
                            db = ndn % 4
                            ndn += 1
                            cs = slice(hf * 512, (hf + 1) * 512)
                            for fc in range(32):
                                k.op("pe", lambda e, fc=fc, db=db, j=j, hf=hf: e.matmul(
                                    dn_ps[db][:, :], lhsT=actT[:, fc, j * 128:(j + 1) * 128],
                                    rhs=w_dn_bf[:, fc, hf * 512:(hf + 1) * 512],
                                    start=(fc == 0), stop=(fc == 31)),
                                    reads=[act_r[fc], r_wB], writes=[dn_r[db]])
                            k.op("dve", lambda e, db=db, j=j, cs=cs: e.tensor_tensor(
                                out=x1t[j][:, cs], in0=x1t[j][:, cs], in1=dn_ps[db][:, :], op=ALU.add),
                                reads=[dn_r[db], x1_r[j]], writes=[x1_r[j]])
                        k.op("act", lambda e, j=j: e.activation(
                            out=junk[:, :], in_=x1t[j][:, :], func=AF.Square, accum_out=st[:, 12 + j:13 + j]),
                            reads=[x1_r[j]], writes=[junk_r, R[3]])
                    k.op("act", lambda e: e.activation(
                        out=st[:, 16:20], in_=st[:, 12:16], func=AF.Sqrt, bias=eps_t[:, 0:1], scale=1.0 / D),
                        reads=[R[3]], writes=[R[4]])
                    k.op("dve", lambda e: e.reciprocal(out=st[:, 20:24], in_=st[:, 16:20]),
                         reads=[R[4]], writes=[R[5]])
                    for j in range(4):
                        i = 4 * G + j
                        k.op("dve", lambda e, j=j: e.scalar_tensor_tensor(
                            out=x1t[j][:, :], in0=x1t[j][:, :], scalar=st[:, 20 + j:21 + j], in1=gf[:, :],
                            op0=ALU.mult, op1=ALU.mult), reads=[x1_r[j], R[5], r_wB], writes=[x1_r[j]])
                        tk = k.dma("sp", xo_ch[j], y[b, i * 128:(i + 1) * 128, :], x1t[j][:, :],
                                   reads=[x1_r[j]], writes=[y_reg[b][i]])
                k.barrier(scratch[:, 0:1])
    return nc


def _host_consts():
    ident = np.eye(128, dtype=np.float32)
    jj = np.arange(128)[:, None]
    ss = np.arange(128)[None, :]
    ntri = -(jj >= ss).astype(np.float32)
    nones = -np.ones((128, 128), np.float32)
    masks = np.zeros((128, 4, 512), np.float32)
    s_idx = np.arange(128)[:, None]
    t_idx = np.arange(128)[None, :]
    strict = (s_idx < t_idx).astype(np.float32)
    for j in range(4):
        for cb in range(4):
            if cb == j:
                masks[:, j, cb * 128:(cb + 1) * 128] = strict
            elif cb > j:
                masks[:, j, cb * 128:(cb + 1) * 128] = 1.0
    return ident, ntri, nones, masks


def kernel(x, norm_mix_g, w_in, sg_ln_g, sg_ln_b, sg_w, sg_b, out_norm_g, w_out,
           norm_mlp_g, w_up, w_down, norm_final_g):
    f = np.float32
    x = np.ascontiguousarray(np.asarray(x, dtype=f))
    ident, ntri, nones, masks = _host_consts()

    def pc(v):
        return np.ascontiguousarray(np.asarray(v, dtype=f).reshape(8, 128).T)

    shared = {
        "w_in": np.ascontiguousarray(np.asarray(w_in, dtype=f)[0]),
        "w_out": np.ascontiguousarray(np.asarray(w_out, dtype=f)[0]),
        "w_up": np.ascontiguousarray(np.asarray(w_up, dtype=f)[0]),
        "w_down": np.ascontiguousarray(np.asarray(w_down, dtype=f)[0]),
        "g_mix": pc(norm_mix_g[0]),
        "g_out": pc(out_norm_g[0]),
        "g_mlp": pc(norm_mlp_g[0]),
        "lng_full": np.ascontiguousarray(np.broadcast_to(np.asarray(sg_ln_g, dtype=f)[0][None, :], (128, 512))),
        "lnb_full": np.ascontiguousarray(np.broadcast_to(np.asarray(sg_ln_b, dtype=f)[0][None, :], (128, 512))),
        "bias_full": np.ascontiguousarray(
            np.broadcast_to(np.asarray(sg_b, dtype=f)[0].T[:, :, None], (128, 8, 64)).reshape(128, 512)),
        "gf_full": np.ascontiguousarray(np.broadcast_to(np.asarray(norm_final_g, dtype=f)[None, :], (128, D))),
        "sgwT": np.ascontiguousarray(np.transpose(np.asarray(sg_w, dtype=f)[0], (2, 0, 1))),
        "ident": ident, "ntri": ntri, "nones": nones, "masks": masks,
    }
    nc = build()
    in_maps = []
    for c in range(NCORES):
        m = dict(shared)
        m["x"] = np.ascontiguousarray(x[c * BPC:(c + 1) * BPC])
        in_maps.append(m)
    res = run_bass_kernel_spmd(nc, in_maps, core_ids=list(range(NCORES)))
    out = np.concatenate([np.asarray(r["y"]) for r in res.results], axis=0)
    return out.astype(np.float32)
```

```python
import numpy as np
import ml_dtypes
from contextlib import ExitStack

import concourse.bass as bass
import concourse.mybir as mybir
from concourse.bass_utils import run_bass_kernel_spmd

F32 = mybir.dt.float32
BF16 = mybir.dt.bfloat16
AF = mybir.ActivationFunctionType
ALU = mybir.AluOpType
AX = mybir.AxisListType

NCORES = 8
BPC = 2
S = 2048
D = 1024
NT = S // 128
INW = 2560
DFF = 4096
EPS = 1e-6


class Tk:
    __slots__ = ("sem", "val")

    def __init__(self, sem, val):
        self.sem = sem
        self.val = val


class Reg:
    __slots__ = ("w", "r", "name")

    def __init__(self, name=""):
        self.w = None
        self.r = {}
        self.name = name


class Chan:
    def __init__(self, sem):
        self.sem = sem
        self.n = 0


class Eng:
    def __init__(self, eng, sem, name):
        self.eng = eng
        self.sem = sem
        self.n = 0
        self.seen = {}
        self.name = name

    def wait(self, tk):
        if tk is None:
            return
        k = id(tk.sem)
        if self.seen.get(k, 0) >= tk.val:
            return
        self.eng.wait_ge(tk.sem, tk.val)
        self.seen[k] = tk.val


class K:
    def __init__(self, nc, es):
        self.nc = nc
        self.es = es
        self.chans = []
        self.E = {}
        for nm, eng in (("pe", nc.tensor), ("act", nc.scalar), ("dve", nc.vector),
                        ("pool", nc.gpsimd), ("sp", nc.sync)):
            sem = es.enter_context(nc.semaphore("sem_" + nm))
            self.E[nm] = Eng(eng, sem, nm)
        self._nm = 0

    def chan(self):
        self._nm += 1
        c = Chan(self.es.enter_context(self.nc.semaphore("ch%d" % self._nm)))
        self.chans.append(c)
        return c

    def _deps(self, E, reads, writes):
        for r in reads:
            if r.w is not None:
                E.wait(r.w)
        pe = E.name == "pe"
        for w in writes:
            if w.w is not None and not (pe and w.w.sem is E.sem):
                E.wait(w.w)
            for tk in w.r.values():
                if not (pe and tk.sem is E.sem):
                    E.wait(tk)

    def _mark(self, tk, reads, writes):
        for r in reads:
            r.r[id(tk.sem)] = tk
        for w in writes:
            w.w = tk
            w.r = {}

    def op(self, e, fn, reads=(), writes=()):
        E = self.E[e]
        self._deps(E, reads, writes)
        inst = fn(E.eng)
        E.n += 1
        inst.then_inc(E.sem, 1)
        tk = Tk(E.sem, E.n)
        self._mark(tk, reads, writes)
        return tk

    def dma(self, q, ch, out, in_, reads=(), writes=()):
        E = self.E[q]
        self._deps(E, reads, writes)
        inst = E.eng.dma_start(out=out, in_=in_)
        ch.n += 1
        inst.then_inc(ch.sem, 16)
        tk = Tk(ch.sem, 16 * ch.n)
        self._mark(tk, reads, writes)
        return tk

    def barrier(self, scratch_ap):
        V = self.E["dve"]
        for nm, E in self.E.items():
            if E is not V and E.n > 0:
                V.wait(Tk(E.sem, E.n))
        for c in self.chans:
            if c.n > 0:
                V.wait(Tk(c.sem, 16 * c.n))
        if V.n > 0:
            V.wait(Tk(V.sem, V.n))
        inst = V.eng.memset(scratch_ap, 0.0)
        V.n += 1
        inst.then_inc(V.sem, 1)
        tk = Tk(V.sem, V.n)
        for nm, E in self.E.items():
            E.wait(tk)


def build():
    nc = bass.Bass("TRN2", target_bir_lowering=False)

    def din(name, shape):
        return nc.dram_tensor(name, list(shape), F32, kind="ExternalInput").ap()

    x = din("x", [BPC, S, D])
    w_in = din("w_in", [D, INW])
    w_out = din("w_out", [D, D])
    w_up = din("w_up", [D, DFF])
    w_down = din("w_down", [DFF, D])
    g_mix = din("g_mix", [128, 8])
    g_out = din("g_out", [128, 8])
    g_mlp = din("g_mlp", [128, 8])
    lng_d = din("lng_full", [128, 512])
    lnb_d = din("lnb_full", [128, 512])
    bias_d = din("bias_full", [128, 512])
    gf_d = din("gf_full", [128, D])
    gmixf_d = din("gmix_full", [128, D])
    gmlpf_d = din("gmlp_full", [128, D])
    sgwT_d = din("sgwT", [128, 8, 128])
    ident_d = din("ident", [128, 128])
    ntri_d = din("ntri", [128, 128])
    nones_d = din("nones", [128, 128])
    masks_d = din("masks", [128, 128])
    y = nc.dram_tensor("y", [BPC, S, D], F32, kind="ExternalOutput").ap()

    with ExitStack() as es:
        k = K(nc, es)

        uid = [0]

        def sb(es_, name, shape, dt):
            uid[0] += 1
            return es_.enter_context(nc.sbuf_tensor("s%d_%s" % (uid[0], name), list(shape), dt))

        def ps(es_, name, shape, dt=F32):
            uid[0] += 1
            return es_.enter_context(nc.psum_tensor("p%d_%s" % (uid[0], name), list(shape), dt))

        ident = sb(es, "ident", [128, 128], BF16)
        ntri = sb(es, "ntri", [128, 128], BF16)
        nones = sb(es, "nones", [128, 128], BF16)
        masks = sb(es, "masks", [128, 128], BF16)
        ones32 = sb(es, "ones32", [128, 1], F32)
        eps_t = sb(es, "eps_t", [128, 1], F32)
        one_t = sb(es, "one_t", [128, 1], F32)
        scratch = sb(es, "scratch", [128, 8], F32)
        gmix = sb(es, "gmix", [128, 8], F32)
        gout = sb(es, "gout", [128, 8], F32)
        gmlp = sb(es, "gmlp", [128, 8], F32)
        r_const = Reg("const")

        cch = k.chan()
        cch2 = k.chan()
        for dst, src in ((ident, ident_d), (ntri, ntri_d), (nones, nones_d)):
            k.dma("pool", cch, dst[:], src[:, :], writes=[r_const])
        k.dma("pool", cch, masks[:], masks_d[:, :], writes=[r_const])
        for dst, src in ((gmix, g_mix), (gout, g_out), (gmlp, g_mlp)):
            k.dma("sp", cch2, dst[:], src[:, :], writes=[r_const])
        k.op("dve", lambda e: e.memset(ones32[:], 1.0), writes=[r_const])
        k.op("dve", lambda e: e.memset(eps_t[:], EPS), writes=[r_const])
        k.op("dve", lambda e: e.memset(one_t[:], 1.0), writes=[r_const])
        k.barrier(scratch[:, 0:1])

        y_reg = [[Reg("y%d_%d" % (b, i)) for i in range(NT)] for b in range(BPC)]

        with ExitStack() as esA:
            w_in_bf = sb(esA, "w_in_bf", [128, 8, INW], BF16)
            w_out_bf = sb(esA, "w_out_bf", [128, 8, D], BF16)
            sgwT = sb(esA, "sgwT", [128, 8, 128], BF16)
            lng = sb(esA, "lng", [128, 512], F32)
            lnb = sb(esA, "lnb", [128, 512], F32)
            biasf = sb(esA, "biasf", [128, 512], F32)
            gmixf = sb(esA, "gmixf", [128, D], F32)
            QT = sb(esA, "QT", [128, 4, S], BF16)
            KT = sb(esA, "KT", [128, 4, S], BF16)
            Vsb = sb(esA, "Vsb", [128, NT, 512], BF16)
            bT = sb(esA, "bT", [128, 4, S], BF16)
            ssa_sb = sb(esA, "ssa_sb", [128, NT], F32)
            r_w = Reg("wA")

            with ExitStack() as esP:
                stg = [sb(esP, "stg%d" % i, [128, D], F32) for i in range(2)]
                stg_r = [Reg("stg%d" % i) for i in range(2)]
                stg_ch = [k.chan() for _ in range(2)]
                pch = k.chan()
                r_sg = Reg("sgw")
                sgch = k.chan()
                wich = k.chan()
                for c in range(8):
                    k.dma("pool", wich, w_in_bf[:, c, :], w_in[c * 128:(c + 1) * 128, :], writes=[r_w])
                k.dma("pool", sgch, sgwT[:], sgwT_d[:, :, :], writes=[r_sg])
                k.dma("sp", pch, lng[:], lng_d[:, :], writes=[r_w])
                k.dma("sp", pch, lnb[:], lnb_d[:, :], writes=[r_w])
                k.dma("sp", pch, biasf[:], bias_d[:, :], writes=[r_w])
                k.dma("sp", pch, gmixf[:], gmixf_d[:, :], writes=[r_w])
                for c in range(8):
                    sl = c % 2
                    k.dma("sp", stg_ch[sl], stg[sl][:, :], w_out[c * 128:(c + 1) * 128, :],
                          writes=[stg_r[sl]])
                    if c % 2 == 0:
                        k.op("dve", lambda e, c=c, sl=sl: e.tensor_scalar(
                            out=w_out_bf[:, c, :], in0=stg[sl][:, :], scalar1=gout[:, c:c + 1],
                            scalar2=None, op0=ALU.mult), reads=[stg_r[sl], r_const], writes=[r_w])
                    else:
                        k.op("act", lambda e, c=c, sl=sl: e.activation(
                            out=w_out_bf[:, c, :], in_=stg[sl][:, :], func=AF.Copy,
                            scale=gout[:, c:c + 1]), reads=[stg_r[sl], r_const], writes=[r_w])
                k.op("pool", lambda e: e.memset(sgwT[64:128, :, 0:64], 0.0), reads=[r_sg], writes=[r_sg])
                k.barrier(scratch[:, 0:1])

            for b in range(BPC):
                with ExitStack() as e1:
                    NXS = 5
                    xt = [sb(e1, "xt%d" % i, [128, D], F32) for i in range(NXS)]
                    xt_r = [Reg() for _ in range(NXS)]
                    xt_ch = [k.chan() for _ in range(NXS)]
                    junk = sb(e1, "junk", [128, D], BF16)
                    junk_r = Reg()
                    xn = [sb(e1, "xn%d" % i, [128, D], BF16) for i in range(2)]
                    xn_r = [Reg() for _ in range(2)]
                    hT = [sb(e1, "hT%d" % i, [128, 8, 512], BF16) for i in range(2)]
                    hT_r = [[Reg() for _ in range(4)] for _ in range(2)]
                    gu = [sb(e1, "gu%d" % i, [128, 512], F32) for i in range(4)]
                    gv = [sb(e1, "gv%d" % i, [128, 512], F32) for i in range(4)]
                    gu_r = [Reg() for _ in range(4)]
                    gv_r = [Reg() for _ in range(4)]
                    braw = gv
                    braw_r = gv_r
                    t1 = sb(e1, "t1", [128, 512], F32)
                    t1_r = Reg()
                    t2 = sb(e1, "t2", [128, 512], F32)
                    t2_r = Reg()
                    vn = [sb(e1, "vn%d" % i, [128, 512], BF16) for i in range(2)]
                    vn_r = [Reg() for _ in range(2)]
                    bn = [sb(e1, "bn%d" % i, [128, 512], BF16) for i in range(2)]
                    bn_r = [Reg() for _ in range(2)]
                    st = sb(e1, "st", [128, 2, 32], F32)
                    st_r = [[Reg() for _ in range(8)] for _ in range(2)]
                    bst = sb(e1, "bst", [128, 4, 6], F32)
                    bst_r = [Reg() for _ in range(4)]
                    mv = sb(e1, "mv", [128, 4, 2], F32)
                    mv_r = Reg()
                    tp_ps = ps(e1, "tp_ps", [128, 8, 128], BF16)
                    tp_r = Reg()
                    qk_ps = [ps(e1, "qk_ps%d" % i, [128, 512]) for i in range(2)]
                    qk_r = [Reg() for _ in range(2)]
                    v_ps = ps(e1, "v_ps", [128, 512])
                    v_r = Reg()
                    u_ps = ps(e1, "u_ps", [128, 512])
                    u_r = Reg()
                    vg_ps = ps(e1, "vg_ps", [128, 512])
                    vg_r = Reg()
                    mix_ps = ps(e1, "mix_ps", [128, 512])
                    mix_r = Reg()
                    bt_ps = ps(e1, "bt_ps", [128, 4, 128], BF16)
                    bt_r = Reg()
                    QT_r = Reg()
                    KT_r = Reg()
                    V_r = Reg()
                    bT_r = Reg()

                    nload = 0
                    nqk = 0
                    for G in range(4):
                        gp = G % 2
                        R = st_r[gp]
                        slots = []
                        for j in range(4):
                            i = 4 * G + j
                            sl = nload % NXS
                            nload += 1
                            slots.append(sl)
                            k.dma("sp", xt_ch[sl], xt[sl][:, :], x[b, i * 128:(i + 1) * 128, :],
                                  writes=[xt_r[sl]])
                            k.op("act", lambda e, sl=sl, j=j: e.activation(
                                out=junk[:, :], in_=xt[sl][:, :], func=AF.Square,
                                accum_out=st[:, gp, j:j + 1]),
                                reads=[xt_r[sl]], writes=[junk_r, R[0]])
                        k.op("act", lambda e: e.activation(
                            out=st[:, gp, 4:8], in_=st[:, gp, 0:4], func=AF.Sqrt,
                            bias=eps_t[:, 0:1], scale=1.0 / D), reads=[R[0]], writes=[R[1]])
                        k.op("dve", lambda e: e.reciprocal(out=st[:, gp, 8:12], in_=st[:, gp, 4:8]),
                             reads=[R[1]], writes=[R[2]])
                        for j in range(4):
                            sl = slots[j]
                            xs = j % 2
                            k.op("dve", lambda e, sl=sl, xs=xs, j=j: e.scalar_tensor_tensor(
                                out=xn[xs][:, :], in0=xt[sl][:, :], scalar=st[:, gp, 8 + j:9 + j],
                                in1=gmixf[:, :], op0=ALU.mult, op1=ALU.mult),
                                reads=[xt_r[sl], R[2], r_w], writes=[xn_r[xs]])
                            for c in range(8):
                                k.op("pe", lambda e, c=c, xs=xs: e.transpose(
                                    tp_ps[:, c, :], xn[xs][:, c * 128:(c + 1) * 128], ident[:]),
                                    reads=[xn_r[xs]], writes=[tp_r])
                            k.op("act", lambda e, j=j: e.copy(
                                out=hT[gp][:, :, j * 128:(j + 1) * 128], in_=tp_ps[:, :, :]),
                                reads=[tp_r], writes=[hT_r[gp][j]])
                        tc0 = G * 512
                        for eb in range(8):
                            qs = nqk % 2
                            nqk += 1
                            for c in range(8):
                                k.op("pe", lambda e, c=c, eb=eb, qs=qs: e.matmul(
                                    qk_ps[qs][:, :], lhsT=w_in_bf[:, c, eb * 128:(eb + 1) * 128],
                                    rhs=hT[gp][:, c, :], start=(c == 0), stop=(c == 7)),
                                    reads=hT_r[gp] + [r_w], writes=[qk_r[qs]])
                            if eb < 4:
                                k.op("dve", lambda e, eb=eb, qs=qs: e.tensor_scalar(
                                    out=QT[:, eb, tc0:tc0 + 512], in0=qk_ps[qs][:, :], scalar1=0.125,
                                    scalar2=None, op0=ALU.mult), reads=[qk_r[qs]], writes=[QT_r])
                            else:
                                k.op("act", lambda e, eb=eb, qs=qs: e.copy(
                                    out=KT[:, eb - 4, tc0:tc0 + 512], in_=qk_ps[qs][:, :]),
                                    reads=[qk_r[qs]], writes=[KT_r])
                        for j in range(4):
                            i = 4 * G + j
                            for (pst, pr, c0) in ((v_ps, v_r, 1024), (u_ps, u_r, 1536), (vg_ps, vg_r, 2048)):
                                for c in range(8):
                                    k.op("pe", lambda e, c=c, pst=pst, c0=c0, j=j: e.matmul(
                                        pst[:, :], lhsT=hT[gp][:, c, j * 128:(j + 1) * 128],
                                        rhs=w_in_bf[:, c, c0:c0 + 512], start=(c == 0), stop=(c == 7)),
                                        reads=[hT_r[gp][j], r_w], writes=[pr])
                            k.op("dve", lambda e, i=i: e.tensor_copy(out=Vsb[:, i, :], in_=v_ps[:, :]),
                                 reads=[v_r], writes=[V_r])
                            k.op("act", lambda e, j=j: e.activation(
                                out=gu[j][:, :], in_=u_ps[:, :], func=AF.Gelu_apprx_tanh),
                                reads=[u_r], writes=[gu_r[j]])
                            k.op("act", lambda e, j=j: e.activation(
                                out=gv[j][:, :], in_=vg_ps[:, :], func=AF.Gelu_apprx_tanh),
                                reads=[vg_r], writes=[gv_r[j]])
                            k.op("dve", lambda e, j=j: e.bn_stats(out=bst[:, j, :], in_=gv[j][:, :]),
                                 reads=[gv_r[j]], writes=[bst_r[j]])
                            k.op("dve", lambda e, j=j: e.bn_aggr(out=mv[:, j, :], in_=bst[:, j, :]),
                                 reads=[bst_r[j]], writes=[mv_r])
                        k.op("act", lambda e: e.activation(
                            out=st[:, gp, 12:16], in_=mv[:, :, 1], func=AF.Sqrt,
                            bias=eps_t[:, 0:1], scale=1.0), reads=[mv_r], writes=[R[3]])
                        k.op("dve", lambda e: e.reciprocal(out=st[:, gp, 16:20], in_=st[:, gp, 12:16]),
                             reads=[R[3]], writes=[R[4]])
                        for j in range(4):
                            vs = j % 2
                            k.op("dve", lambda e, j=j: e.scalar_tensor_tensor(
                                out=t1[:, :], in0=gv[j][:, :], scalar=mv[:, j, 0:1], in1=lng[:, :],
                                op0=ALU.subtract, op1=ALU.mult),
                                reads=[gv_r[j], mv_r, r_w], writes=[t1_r])
                            k.op("dve", lambda e, j=j, vs=vs: e.scalar_tensor_tensor(
                                out=vn[vs][:, :], in0=t1[:, :], scalar=st[:, gp, 16 + j:17 + j], in1=lnb[:, :],
                                op0=ALU.mult, op1=ALU.add),
                                reads=[t1_r, R[4], r_w], writes=[vn_r[vs]])
                            for g in range(8):
                                k.op("pe", lambda e, g=g, vs=vs: e.matmul(
                                    mix_ps[:, g * 64:(g + 1) * 64], lhsT=sgwT[:, g, :],
                                    rhs=vn[vs][:, g * 64:(g + 1) * 64], start=True, stop=True),
                                    reads=[vn_r[vs], r_w], writes=[mix_r])
                            k.op("dve", lambda e: e.tensor_tensor(
                                out=t2[:, :], in0=mix_ps[:, :], in1=biasf[:, :], op=ALU.add),
                                reads=[mix_r, r_w], writes=[t2_r])
                            k.op("pool", lambda e, j=j: e.tensor_tensor(
                                out=braw[j][:, :], in0=t2[:, :], in1=gu[j][:, :], op=ALU.mult),
                                reads=[t2_r, gu_r[j]], writes=[braw_r[j]])
                            k.op("act", lambda e, j=j: e.activation(
                                out=junk[:, 0:512], in_=braw[j][:, :], func=AF.Square,
                                accum_out=st[:, gp, 20 + j:21 + j]),
                                reads=[braw_r[j]], writes=[junk_r, R[5]])
                        k.op("act", lambda e: e.activation(
                            out=st[:, gp, 24:28], in_=st[:, gp, 20:24], func=AF.Sqrt,
                            bias=eps_t[:, 0:1], scale=1.0 / 512), reads=[R[5]], writes=[R[6]])
                        k.op("dve", lambda e: e.reciprocal(out=st[:, gp, 28:32], in_=st[:, gp, 24:28]),
                             reads=[R[6]], writes=[R[7]])
                        for j in range(4):
                            i = 4 * G + j
                            bs = j % 2
                            k.op("dve", lambda e, j=j, bs=bs: e.tensor_scalar(
                                out=bn[bs][:, :], in0=braw[j][:, :], scalar1=st[:, gp, 28 + j:29 + j],
                                scalar2=None, op0=ALU.mult),
                                reads=[braw_r[j], R[7]], writes=[bn_r[bs]])
                            for c in range(4):
                                k.op("pe", lambda e, c=c, bs=bs: e.transpose(
                                    bt_ps[:, c, :], bn[bs][:, c * 128:(c + 1) * 128], ident[:]),
                                    reads=[bn_r[bs]], writes=[bt_r])
                            k.op("dve", lambda e, i=i: e.tensor_copy(
                                out=bT[:, :, i * 128:(i + 1) * 128], in_=bt_ps[:, :, :]),
                                reads=[bt_r], writes=[bT_r])
                    k.barrier(scratch[:, 0:1])

                with ExitStack() as e2:
                    aT = sb(e2, "aT", [128, 4, S], BF16)
                    aT_r = Reg()
                    with ExitStack() as e3:
                        QTz = [[sb(e3, "QTz%d_%d" % (p, i), [128, 512], BF16) for i in range(2)]
                               for p in range(2)]
                        QTz_r = [[Reg() for _ in range(2)] for _ in range(2)]
                        NE = 3
                        ebuf = [sb(e3, "ebuf%d" % i, [128, 512], F32) for i in range(NE)]
                        ebuf_r = [Reg() for _ in range(NE)]
                        NSP = 5
                        spb = [sb(e3, "spb%d" % i, [128, 512], BF16) for i in range(NSP)]
                        spb_r = [Reg() for _ in range(NSP)]
                        Sb = [sb(e3, "Sb%d" % i, [128, 512], BF16) for i in range(2)]
                        Sb_r = [Reg() for _ in range(2)]
                        NA = 3
                        Ab = [sb(e3, "Ab%d" % i, [128, 512], BF16) for i in range(NA)]
                        Ab_r = [Reg() for _ in range(NA)]
                        sq = sb(e3, "sq", [128, 512], F32)
                        sq_r = Reg()
                        NF = 5
                        f_ps = [ps(e3, "f_ps%d" % i, [128, 512]) for i in range(NF)]
                        f_r = [Reg() for _ in range(NF)]
                        o_ps = [ps(e3, "o_ps%d" % i, [128, 512]) for i in range(2)]
                        o_r = [Reg() for _ in range(2)]
                        ssa_ps = ps(e3, "ssa_ps", [128, 16])
                        ssa_r = Reg()
                        ssa_sb_r = Reg()
                        r_att = Reg()
                        for p in range(2):
                            for i in range(2):
                                k.op("pool", lambda e, p=p, i=i: e.memset(QTz[p][i][:, :], 0.0),
                                     writes=[QTz_r[p][i]])

                        steps = []
                        nhead = 0
                        for qg in range(4):
                            for hp in range(4):
                                for hh in range(2):
                                    nk = 4 * qg + 4
                                    prev_c0 = None
                                    for si in range(nk):
                                        kb = nk - 1 - si
                                        j = kb - 4 * qg
                                        c0 = max(0, j) * 128
                                        steps.append(dict(qg=qg, hp=hp, hh=hh, kb=kb, first=(si == 0),
                                                          last=(si == nk - 1), j=j, c0=c0, pc0=prev_c0,
                                                          hidx=nhead))
                                        prev_c0 = c0
                                    nhead += 1
                        state = {"S": None}

                        def s1a(i):
                            s_ = steps[i]
                            hp, hh, kb, qg, c0 = s_["hp"], s_["hh"], s_["kb"], s_["qg"], s_["c0"]
                            pb = 64 * hh
                            qb_ = (s_["hidx"] // 2) % 2
                            qz, qzr = QTz[hh][qb_], QTz_r[hh][qb_]
                            if s_["first"]:
                                k.op("pool", lambda e: e.tensor_copy(
                                    out=qz[pb:pb + 64, :], in_=QT[pb:pb + 64, hp, qg * 512:(qg + 1) * 512]),
                                    reads=[r_att], writes=[qzr])
                            fb = i % NF
                            eb = i % NE
                            k.op("pe", lambda e: e.matmul(
                                f_ps[fb][:, c0:512], lhsT=KT[:, hp, kb * 128:(kb + 1) * 128], rhs=qz[:, c0:512],
                                start=True, stop=True), reads=[qzr, r_att], writes=[f_r[fb]])
                            k.op("act", lambda e: e.activation(
                                out=ebuf[eb][:, c0:512], in_=f_ps[fb][:, c0:512], func=AF.Exp),
                                reads=[f_r[fb]], writes=[ebuf_r[eb]])

                        def s1b(i):
                            s_ = steps[i]
                            c0 = s_["c0"]
                            eb = i % NE
                            sb_i = i % NSP
                            k.op("act", lambda e: e.activation(
                                out=spb[sb_i][:, c0:512], in_=ebuf[eb][:, c0:512], func=AF.Ln,
                                bias=one_t[:, 0:1], scale=1.0), reads=[ebuf_r[eb]], writes=[spb_r[sb_i]])
                            if s_["j"] >= 0:
                                k.op("pool", lambda e: e.tensor_tensor(
                                    out=spb[sb_i][:, c0:c0 + 128], in0=spb[sb_i][:, c0:c0 + 128],
                                    in1=masks[:, :], op=ALU.mult),
                                    reads=[spb_r[sb_i]], writes=[spb_r[sb_i]])

                        def s2(i):
                            s_ = steps[i]
                            fb = i % NF
                            sb_i = i % NSP
                            first = s_["first"]
                            c0, pc0 = s_["c0"], s_["pc0"]
                            k.op("pe", lambda e: e.matmul(
                                f_ps[fb][:, c0:512], lhsT=ntri[:, :], rhs=spb[sb_i][:, c0:512],
                                start=False, stop=first, skip_group_check=True),
                                reads=[spb_r[sb_i]], writes=[f_r[fb]])
                            if not first:
                                S_t, S_rg = state["S"]
                                k.op("pe", lambda e: e.matmul(
                                    f_ps[fb][:, pc0:512], lhsT=nones[:, :], rhs=S_t[:, pc0:512],
                                    start=False, stop=True, skip_group_check=True),
                                    reads=[S_rg], writes=[f_r[fb]])
                            ab = i % NA
                            k.op("act", lambda e: e.activation(
                                out=Ab[ab][:, c0:512], in_=f_ps[fb][:, c0:512], func=AF.Exp),
                                reads=[f_r[fb]], writes=[Ab_r[ab]])
                            if s_["j"] >= 0:
                                k.op("pool", lambda e: e.tensor_tensor(
                                    out=Ab[ab][:, c0:c0 + 128], in0=Ab[ab][:, c0:c0 + 128],
                                    in1=masks[:, :], op=ALU.mult),
                                    reads=[Ab_r[ab]], writes=[Ab_r[ab]])
                            if not s_["last"]:
                                if first:
                                    state["S"] = (spb[sb_i], spb_r[sb_i])
                                    state["Sn"] = 0
                                else:
                                    S_t, S_rg = state["S"]
                                    nb = state["Sn"]
                                    state["Sn"] = 1 - nb
                                    k.op("dve", lambda e: e.tensor_tensor(
                                        out=Sb[nb][:, pc0:512], in0=S_t[:, pc0:512], in1=spb[sb_i][:, pc0:512],
                                        op=ALU.add), reads=[S_rg, spb_r[sb_i]], writes=[Sb_r[nb]])
                                    if c0 < pc0:
                                        k.op("dve", lambda e: e.tensor_copy(
                                            out=Sb[nb][:, c0:pc0], in_=spb[sb_i][:, c0:pc0]),
                                            reads=[spb_r[sb_i]], writes=[Sb_r[nb]])
                                    state["S"] = (Sb[nb], Sb_r[nb])

                        def s3(i):
                            s_ = steps[i]
                            hp, hh, kb, qg, c0 = s_["hp"], s_["hh"], s_["kb"], s_["qg"], s_["c0"]
                            pb = 64 * hh
                            ab = i % NA
                            ob = s_["hidx"] % 2
                            k.op("pe", lambda e: e.matmul(
                                o_ps[ob][:, c0:512], lhsT=Vsb[:, kb, hp * 128:(hp + 1) * 128], rhs=Ab[ab][:, c0:512],
                                start=s_["first"], stop=s_["last"], skip_group_check=True),
                                reads=[Ab_r[ab], r_att], writes=[o_r[ob]])
                            if s_["last"]:
                                cs = slice(qg * 512, (qg + 1) * 512)
                                k.op("dve", lambda e: e.tensor_copy(
                                    out=aT[pb:pb + 64, hp, cs], in_=o_ps[ob][pb:pb + 64, :]),
                                    reads=[o_r[ob]], writes=[aT_r])
                                k.op("pool", lambda e: e.tensor_tensor(
                                    out=sq[pb:pb + 64, :], in0=aT[pb:pb + 64, hp, cs], in1=aT[pb:pb + 64, hp, cs],
                                    op=ALU.mult), reads=[aT_r], writes=[sq_r])
                                if hh == 1:
                                    for jj in range(4):
                                        k.op("pe", lambda e, jj=jj: e.matmul(
                                            ssa_ps[:, hp * 4 + jj:hp * 4 + jj + 1],
                                            lhsT=sq[:, jj * 128:(jj + 1) * 128], rhs=ones32[:, 0:1],
                                            start=True, stop=True), reads=[sq_r, r_const], writes=[ssa_r])
                                    if hp == 3:
                                        k.op("dve", lambda e: e.tensor_reduce(
                                            out=ssa_sb[:, qg * 4:(qg + 1) * 4],
                                            in_=ssa_ps[:, 0:16].rearrange("p (h j) -> p j h", h=4),
                                            axis=AX.X, op=ALU.add), reads=[ssa_r], writes=[ssa_sb_r])

                        nst = len(steps)
                        for it in range(nst + 2):
                            if it < nst:
                                s1a(it)
                            if 0 <= it - 1 < nst:
                                s2(it - 1)
                            if it < nst:
                                s1b(it)
                            if 0 <= it - 2 < nst:
                                s3(it - 2)
                        k.barrier(scratch[:, 0:1])

                    with ExitStack() as e4:
                        xr = [sb(e4, "xr%d" % i, [128, D], F32) for i in range(3)]
                        xr_r = [Reg() for _ in range(3)]
                        xr_ch = [k.chan() for _ in range(3)]
                        xo_ch = [k.chan() for _ in range(3)]
                        sda = sb(e4, "sda", [128, NT], F32)
                        rsa = sb(e4, "rsa", [128, NT], F32)
                        rsa_r = Reg()
                        acc = [[ps(e4, "acc%d_%d" % (i, h), [128, 512]) for h in range(4)] for i in range(2)]
                        acc_r = [[Reg() for _ in range(4)] for _ in range(2)]
                        r_all = Reg()
                        k.op("act", lambda e: e.activation(
                            out=sda[:, :], in_=ssa_sb[:, :], func=AF.Sqrt, bias=eps_t[:, 0:1],
                            scale=1.0 / 512), reads=[r_all], writes=[rsa_r])
                        k.op("dve", lambda e: e.reciprocal(out=rsa[:, :], in_=sda[:, :]),
                             reads=[rsa_r], writes=[rsa_r])
                        for i in range(NT):
                            sl = i % 3
                            ab = i % 2
                            k.dma("sp", xr_ch[sl], xr[sl][:, :], x[b, i * 128:(i + 1) * 128, :],
                                  writes=[xr_r[sl]])
                            for hf in range(2):
                                for c in range(4):
                                    k.op("pe", lambda e, c=c, hf=hf: e.matmul(
                                        acc[ab][hf][:, :], lhsT=aT[:, c, i * 128:(i + 1) * 128],
                                        rhs=w_out_bf[:, c, hf * 512:(hf + 1) * 512],
                                        start=(c == 0), stop=(c == 3)), reads=[r_all], writes=[acc_r[ab][hf]])
                                for c in range(4):
                                    k.op("pe", lambda e, c=c, hf=hf: e.matmul(
                                        acc[ab][2 + hf][:, :], lhsT=bT[:, c, i * 128:(i + 1) * 128],
                                        rhs=w_out_bf[:, 4 + c, hf * 512:(hf + 1) * 512],
                                        start=(c == 0), stop=(c == 3)), reads=[r_all], writes=[acc_r[ab][2 + hf]])
                            for hf in range(2):
                                cs = slice(hf * 512, (hf + 1) * 512)
                                k.op("dve", lambda e, hf=hf, cs=cs: e.scalar_tensor_tensor(
                                    out=xr[sl][:, cs], in0=acc[ab][hf][:, :], scalar=rsa[:, i:i + 1],
                                    in1=xr[sl][:, cs], op0=ALU.mult, op1=ALU.add),
                                    reads=[acc_r[ab][hf], rsa_r, xr_r[sl]], writes=[xr_r[sl]])
                                k.op("dve", lambda e, hf=hf, cs=cs: e.tensor_tensor(
                                    out=xr[sl][:, cs], in0=xr[sl][:, cs], in1=acc[ab][2 + hf][:, :], op=ALU.add),
                                    reads=[acc_r[ab][2 + hf], xr_r[sl]], writes=[xr_r[sl]])
                            k.dma("sp", xo_ch[sl], y[b, i * 128:(i + 1) * 128, :], xr[sl][:, :],
                                  reads=[xr_r[sl]], writes=[y_reg[b][i]])
                        k.barrier(scratch[:, 0:1])

        with ExitStack() as esB:
            w_up_bf = sb(esB, "w_up_bf", [128, 8, DFF], BF16)
            w_dn_bf = sb(esB, "w_dn_bf", [128, 32, D], BF16)
            gf = sb(esB, "gf", [128, D], F32)
            gmlpf = sb(esB, "gmlpf", [128, D], F32)
            r_wB = Reg()
            wch = k.chan()
            wch2 = k.chan()
            k.dma("sp", wch2, gf[:], gf_d[:, :], writes=[r_wB])
            k.dma("sp", wch2, gmlpf[:], gmlpf_d[:, :], writes=[r_wB])
            for c in range(8):
                k.dma("pool", wch, w_up_bf[:, c, :], w_up[c * 128:(c + 1) * 128, :], writes=[r_wB])
            for q in range(4):
                k.dma("pool", wch, w_dn_bf[:, q * 8:(q + 1) * 8, :],
                      w_down[q * 1024:(q + 1) * 1024, :].rearrange("(c p) d -> p c d", p=128),
                      writes=[r_wB])
            k.barrier(scratch[:, 0:1])

            with ExitStack() as e5:
                x1t = [sb(e5, "x1t%d" % i, [128, D], F32) for i in range(4)]
                x1_r = [Reg() for _ in range(4)]
                x1_ch = [k.chan() for _ in range(4)]
                xo_ch = [k.chan() for _ in range(4)]
                junk = sb(e5, "junkB", [128, D], BF16)
                junk_r = Reg()
                xn = [sb(e5, "xnB%d" % i, [128, D], BF16) for i in range(2)]
                xn_r = [Reg() for _ in range(2)]
                h2T = sb(e5, "h2T", [128, 8, 512], BF16)
                h2_r = [Reg() for _ in range(4)]
                actT = sb(e5, "actT", [128, 32, 512], BF16)
                act_r = [Reg() for _ in range(32)]
                rr = [sb(e5, "rr%d" % i, [128, 512], F32) for i in range(2)]
                rr_r = [Reg() for _ in range(2)]
                st = sb(e5, "stB", [128, 32], F32)
                R = [Reg() for _ in range(8)]
                tp_ps = ps(e5, "tpB", [128, 8, 128], BF16)
                tp_r = Reg()
                up_ps = [ps(e5, "up_ps%d" % i, [128, 512]) for i in range(3)]
                up_r = [Reg() for _ in range(3)]
                dn_ps = [ps(e5, "dn_ps%d" % i, [128, 512]) for i in range(4)]
                dn_r = [Reg() for _ in range(4)]
                out_chs = []
                nup = 0
                ndn = 0
                for blk in range(BPC * 4):
                    b = blk // 4
                    G = blk % 4
                    for j in range(4):
                        i = 4 * G + j
                        k.dma("sp", x1_ch[j], x1t[j][:, :], y[b, i * 128:(i + 1) * 128, :],
                              reads=[y_reg[b][i]], writes=[x1_r[j]])
                        k.op("act", lambda e, j=j: e.activation(
                            out=junk[:, :], in_=x1t[j][:, :], func=AF.Square, accum_out=st[:, j:j + 1]),
                            reads=[x1_r[j]], writes=[junk_r, R[0]])
                    k.op("act", lambda e: e.activation(
                        out=st[:, 4:8], in_=st[:, 0:4], func=AF.Sqrt, bias=eps_t[:, 0:1], scale=1.0 / D),
                        reads=[R[0]], writes=[R[1]])
                    k.op("dve", lambda e: e.reciprocal(out=st[:, 8:12], in_=st[:, 4:8]),
                         reads=[R[1]], writes=[R[2]])
                    for j in range(4):
                        xs = j % 2
                        k.op("dve", lambda e, j=j, xs=xs: e.scalar_tensor_tensor(
                            out=xn[xs][:, :], in0=x1t[j][:, :], scalar=st[:, 8 + j:9 + j], in1=gmlpf[:, :],
                            op0=ALU.mult, op1=ALU.mult), reads=[x1_r[j], R[2], r_wB], writes=[xn_r[xs]])
                        for c in range(8):
                            k.op("pe", lambda e, c=c, xs=xs: e.transpose(
                                tp_ps[:, c, :], xn[xs][:, c * 128:(c + 1) * 128], ident[:]),
                                reads=[xn_r[xs]], writes=[tp_r])
                        k.op("dve", lambda e, j=j: e.tensor_copy(
                            out=h2T[:, :, j * 128:(j + 1) * 128], in_=tp_ps[:, :, :]),
                            reads=[tp_r], writes=[h2_r[j]])
                    for fc in range(32):
                        ub = nup % 3
                        rb = nup % 2
                        nup += 1
                        for c in range(8):
                            k.op("pe", lambda e, c=c, fc=fc, ub=ub: e.matmul(
                                up_ps[ub][:, :], lhsT=w_up_bf[:, c, fc * 128:(fc + 1) * 128],
                                rhs=h2T[:, c, :], start=(c == 0), stop=(c == 7)),
                                reads=h2_r + [r_wB], writes=[up_r[ub]])
                        k.op("act", lambda e, ub=ub, rb=rb: e.activation(
                            out=rr[rb][:, :], in_=up_ps[ub][:, :], func=AF.Relu),
                            reads=[up_r[ub]], writes=[rr_r[rb]])
                        k.op("pool", lambda e, fc=fc, rb=rb: e.tensor_tensor(
                            out=actT[:, fc, :], in0=rr[rb][:, :], in1=rr[rb][:, :], op=ALU.mult),
                            reads=[rr_r[rb]], writes=[act_r[fc]])
                    for j in range(4):
                        for hf in range(2):
                            db = ndn % 4
                            ndn += 1
                            cs = slice(hf * 512, (hf + 1) * 512)
                            for fc in range(32):
                                k.op("pe", lambda e, fc=fc, db=db, j=j, hf=hf: e.matmul(
                                    dn_ps[db][:, :], lhsT=actT[:, fc, j * 128:(j + 1) * 128],
                                    rhs=w_dn_bf[:, fc, hf * 512:(hf + 1) * 512],
                                    start=(fc == 0), stop=(fc == 31)),
                                    reads=[act_r[fc], r_wB], writes=[dn_r[db]])
                            k.op("dve", lambda e, db=db, j=j, cs=cs: e.tensor_tensor(
                                out=x1t[j][:, cs], in0=x1t[j][:, cs], in1=dn_ps[db][:, :], op=ALU.add),
                                reads=[dn_r[db], x1_r[j]], writes=[x1_r[j]])
                        k.op("act", lambda e, j=j: e.activation(
                            out=junk[:, :], in_=x1t[j][:, :], func=AF.Square, accum_out=st[:, 12 + j:13 + j]),
                            reads=[x1_r[j]], writes=[junk_r, R[3]])
                    k.op("act", lambda e: e.activation(
                        out=st[:, 16:20], in_=st[:, 12:16], func=AF.Sqrt, bias=eps_t[:, 0:1], scale=1.0 / D),
                        reads=[R[3]], writes=[R[4]])
                    k.op("dve", lambda e: e.reciprocal(out=st[:, 20:24], in_=st[:, 16:20]),
                         reads=[R[4]], writes=[R[5]])
                    for j in range(4):
                        i = 4 * G + j
                        k.op("dve", lambda e, j=j: e.scalar_tensor_tensor(
                            out=x1t[j][:, :], in0=x1t[j][:, :], scalar=st[:, 20 + j:21 + j], in1=gf[:, :],
                            op0=ALU.mult, op1=ALU.mult), reads=[x1_r[j], R[5], r_wB], writes=[x1_r[j]])
                        tk = k.dma("sp", xo_ch[j], y[b, i * 128:(i + 1) * 128, :], x1t[j][:, :],
                                   reads=[x1_r[j]], writes=[y_reg[b][i]])
                k.barrier(scratch[:, 0:1])
    return nc


def _host_consts():
    ident = np.eye(128, dtype=np.float32)
    jj = np.arange(128)[:, None]
    ss = np.arange(128)[None, :]
    ntri = -(jj >= ss).astype(np.float32)
    nones = -np.ones((128, 128), np.float32)
    s_idx = np.arange(128)[:, None]
    t_idx = np.arange(128)[None, :]
    masks = (s_idx < t_idx).astype(np.float32)
    return ident, ntri, nones, masks


def kernel(x, norm_mix_g, w_in, sg_ln_g, sg_ln_b, sg_w, sg_b, out_norm_g, w_out,
           norm_mlp_g, w_up, w_down, norm_final_g):
    f = np.float32
    x = np.ascontiguousarray(np.asarray(x, dtype=f))
    ident, ntri, nones, masks = _host_consts()

    def pc(v):
        return np.ascontiguousarray(np.asarray(v, dtype=f).reshape(8, 128).T)

    shared = {
        "w_in": np.ascontiguousarray(np.asarray(w_in, dtype=f)[0]),
        "w_out": np.ascontiguousarray(np.asarray(w_out, dtype=f)[0]),
        "w_up": np.ascontiguousarray(np.asarray(w_up, dtype=f)[0]),
        "w_down": np.ascontiguousarray(np.asarray(w_down, dtype=f)[0]),
        "g_mix": pc(norm_mix_g[0]),
        "g_out": pc(out_norm_g[0]),
        "g_mlp": pc(norm_mlp_g[0]),
        "lng_full": np.ascontiguousarray(np.broadcast_to(np.asarray(sg_ln_g, dtype=f)[0][None, :], (128, 512))),
        "lnb_full": np.ascontiguousarray(np.broadcast_to(np.asarray(sg_ln_b, dtype=f)[0][None, :], (128, 512))),
        "bias_full": np.ascontiguousarray(
            np.broadcast_to(np.asarray(sg_b, dtype=f)[0].T[:, :, None], (128, 8, 64)).reshape(128, 512)),
        "gmix_full": np.ascontiguousarray(np.broadcast_to(np.asarray(norm_mix_g, dtype=f)[0][None, :], (128, D))),
        "gmlp_full": np.ascontiguousarray(np.broadcast_to(np.asarray(norm_mlp_g, dtype=f)[0][None, :], (128, D))),
        "gf_full": np.ascontiguousarray(np.broadcast_to(np.asarray(norm_final_g, dtype=f)[None, :], (128, D))),
        "sgwT": np.ascontiguousarray(np.transpose(np.asarray(sg_w, dtype=f)[0], (2, 0, 1))),
        "ident": ident, "ntri": ntri, "nones": nones, "masks": masks,
    }
    nc = build()
    in_maps = []
    for c in range(NCORES):
        m = dict(shared)
        m["x"] = np.ascontiguousarray(x[c * BPC:(c + 1) * BPC])
        in_maps.append(m)
    res = run_bass_kernel_spmd(nc, in_maps, core_ids=list(range(NCORES)))
    out = np.concatenate([np.asarray(r["y"]) for r in res.results], axis=0)
    return out.astype(np.float32)
```

```python
import numpy as np
import ml_dtypes
from contextlib import ExitStack

import concourse.bass as bass
import concourse.mybir as mybir
from concourse.bass_utils import run_bass_kernel_spmd

F32 = mybir.dt.float32
BF16 = mybir.dt.bfloat16
AF = mybir.ActivationFunctionType
ALU = mybir.AluOpType
AX = mybir.AxisListType

NCORES = 8
BPC = 2
S = 2048
D = 1024
NT = S // 128
INW = 2560
DFF = 4096
EPS = 1e-6


class Tk:
    __slots__ = ("sem", "val")

    def __init__(self, sem, val):
        self.sem = sem
        self.val = val


class Reg:
    __slots__ = ("w", "r", "name")

    def __init__(self, name=""):
        self.w = None
        self.r = {}
        self.name = name


class Chan:
    def __init__(self, sem):
        self.sem = sem
        self.n = 0


class Eng:
    def __init__(self, eng, sem, name):
        self.eng = eng
        self.sem = sem
        self.n = 0
        self.seen = {}
        self.name = name

    def wait(self, tk):
        if tk is None:
            return
        k = id(tk.sem)
        if self.seen.get(k, 0) >= tk.val:
            return
        self.eng.wait_ge(tk.sem, tk.val)
        self.seen[k] = tk.val


class K:
    def __init__(self, nc, es):
        self.nc = nc
        self.es = es
        self.chans = []
        self.E = {}
        for nm, eng in (("pe", nc.tensor), ("act", nc.scalar), ("dve", nc.vector),
                        ("pool", nc.gpsimd), ("sp", nc.sync)):
            sem = es.enter_context(nc.semaphore("sem_" + nm))
            self.E[nm] = Eng(eng, sem, nm)
        self._nm = 0

    def chan(self):
        self._nm += 1
        c = Chan(self.es.enter_context(self.nc.semaphore("ch%d" % self._nm)))
        self.chans.append(c)
        return c

    def _deps(self, E, reads, writes):
        for r in reads:
            if r.w is not None:
                E.wait(r.w)
        pe = E.name == "pe"
        for w in writes:
            if w.w is not None and not (pe and w.w.sem is E.sem):
                E.wait(w.w)
            for tk in w.r.values():
                if not (pe and tk.sem is E.sem):
                    E.wait(tk)

    def _mark(self, tk, reads, writes):
        for r in reads:
            r.r[id(tk.sem)] = tk
        for w in writes:
            w.w = tk
            w.r = {}

    def op(self, e, fn, reads=(), writes=()):
        E = self.E[e]
        self._deps(E, reads, writes)
        inst = fn(E.eng)
        E.n += 1
        inst.then_inc(E.sem, 1)
        tk = Tk(E.sem, E.n)
        self._mark(tk, reads, writes)
        return tk

    def dma(self, q, ch, out, in_, reads=(), writes=()):
        E = self.E[q]
        self._deps(E, reads, writes)
        inst = E.eng.dma_start(out=out, in_=in_)
        ch.n += 1
        inst.then_inc(ch.sem, 16)
        tk = Tk(ch.sem, 16 * ch.n)
        self._mark(tk, reads, writes)
        return tk

    def barrier(self, scratch_ap):
        V = self.E["dve"]
        for nm, E in self.E.items():
            if E is not V and E.n > 0:
                V.wait(Tk(E.sem, E.n))
        for c in self.chans:
            if c.n > 0:
                V.wait(Tk(c.sem, 16 * c.n))
        if V.n > 0:
            V.wait(Tk(V.sem, V.n))
        inst = V.eng.memset(scratch_ap, 0.0)
        V.n += 1
        inst.then_inc(V.sem, 1)
        tk = Tk(V.sem, V.n)
        for nm, E in self.E.items():
            E.wait(tk)


def build():
    nc = bass.Bass("TRN2", target_bir_lowering=False)

    def din(name, shape):
        return nc.dram_tensor(name, list(shape), F32, kind="ExternalInput").ap()

    x = din("x", [BPC, S, D])
    w_in = din("w_in", [D, INW])
    w_out = din("w_out", [D, D])
    w_up = din("w_up", [D, DFF])
    w_down = din("w_down", [DFF, D])
    g_mix = din("g_mix", [128, 8])
    g_out = din("g_out", [128, 8])
    g_mlp = din("g_mlp", [128, 8])
    lng_d = din("lng_full", [128, 512])
    lnb_d = din("lnb_full", [128, 512])
    bias_d = din("bias_full", [128, 512])
    gf_d = din("gf_full", [128, D])
    gmixf_d = din("gmix_full", [128, D])
    gmlpf_d = din("gmlp_full", [128, D])
    sgwT_d = din("sgwT", [128, 8, 128])
    ident_d = din("ident", [128, 128])
    ntri_d = din("ntri", [128, 128])
    nones_d = din("nones", [128, 128])
    masks_d = din("masks", [128, 128])
    y = nc.dram_tensor("y", [BPC, S, D], F32, kind="ExternalOutput").ap()

    with ExitStack() as es:
        k = K(nc, es)

        uid = [0]

        def sb(es_, name, shape, dt):
            uid[0] += 1
            return es_.enter_context(nc.sbuf_tensor("s%d_%s" % (uid[0], name), list(shape), dt))

        def ps(es_, name, shape, dt=F32):
            uid[0] += 1
            return es_.enter_context(nc.psum_tensor("p%d_%s" % (uid[0], name), list(shape), dt))

        ident = sb(es, "ident", [128, 128], BF16)
        ntri = sb(es, "ntri", [128, 128], BF16)
        nones = sb(es, "nones", [128, 128], BF16)
        masks = sb(es, "masks", [128, 128], BF16)
        ones32 = sb(es, "ones32", [128, 1], F32)
        eps_t = sb(es, "eps_t", [128, 1], F32)
        one_t = sb(es, "one_t", [128, 1], F32)
        scratch = sb(es, "scratch", [128, 8], F32)
        gmix = sb(es, "gmix", [128, 8], F32)
        gout = sb(es, "gout", [128, 8], F32)
        gmlp = sb(es, "gmlp", [128, 8], F32)
        r_const = Reg("const")

        cch = k.chan()
        cch2 = k.chan()
        for dst, src in ((ident, ident_d), (ntri, ntri_d), (nones, nones_d)):
            k.dma("pool", cch, dst[:], src[:, :], writes=[r_const])
        k.dma("pool", cch, masks[:], masks_d[:, :], writes=[r_const])
        for dst, src in ((gmix, g_mix), (gout, g_out), (gmlp, g_mlp)):
            k.dma("sp", cch2, dst[:], src[:, :], writes=[r_const])
        k.op("dve", lambda e: e.memset(ones32[:], 1.0), writes=[r_const])
        k.op("dve", lambda e: e.memset(eps_t[:], EPS), writes=[r_const])
        k.op("dve", lambda e: e.memset(one_t[:], 1.0), writes=[r_const])
        k.barrier(scratch[:, 0:1])

        y_reg = [[Reg("y%d_%d" % (b, i)) for i in range(NT)] for b in range(BPC)]

        with ExitStack() as esA:
            w_in_bf = sb(esA, "w_in_bf", [128, 8, INW], BF16)
            w_out_bf = sb(esA, "w_out_bf", [128, 8, D], BF16)
            sgwT = sb(esA, "sgwT", [128, 8, 128], BF16)
            lng = sb(esA, "lng", [128, 512], F32)
            lnb = sb(esA, "lnb", [128, 512], F32)
            biasf = sb(esA, "biasf", [128, 512], F32)
            gmixf = sb(esA, "gmixf", [128, D], F32)
            QT = sb(esA, "QT", [128, 4, S], BF16)
            KT = sb(esA, "KT", [128, 4, S], BF16)
            Vsb = sb(esA, "Vsb", [128, NT, 512], BF16)
            bT = sb(esA, "bT", [128, 4, S], BF16)
            ssa_sb = sb(esA, "ssa_sb", [128, NT], F32)
            r_w = Reg("wA")

            with ExitStack() as esP:
                stg = [sb(esP, "stg%d" % i, [128, D], F32) for i in range(2)]
                stg_r = [Reg("stg%d" % i) for i in range(2)]
                stg_ch = [k.chan() for _ in range(2)]
                pch = k.chan()
                r_sg = Reg("sgw")
                sgch = k.chan()
                wich = k.chan()
                for c in range(8):
                    k.dma("pool", wich, w_in_bf[:, c, :], w_in[c * 128:(c + 1) * 128, :], writes=[r_w])
                k.dma("pool", sgch, sgwT[:], sgwT_d[:, :, :], writes=[r_sg])
                k.dma("sp", pch, lng[:], lng_d[:, :], writes=[r_w])
                k.dma("sp", pch, lnb[:], lnb_d[:, :], writes=[r_w])
                k.dma("sp", pch, biasf[:], bias_d[:, :], writes=[r_w])
                k.dma("sp", pch, gmixf[:], gmixf_d[:, :], writes=[r_w])
                for c in range(8):
                    sl = c % 2
                    k.dma("sp", stg_ch[sl], stg[sl][:, :], w_out[c * 128:(c + 1) * 128, :],
                          writes=[stg_r[sl]])
                    if c % 2 == 0:
                        k.op("dve", lambda e, c=c, sl=sl: e.tensor_scalar(
                            out=w_out_bf[:, c, :], in0=stg[sl][:, :], scalar1=gout[:, c:c + 1],
                            scalar2=None, op0=ALU.mult), reads=[stg_r[sl], r_const], writes=[r_w])
                    else:
                        k.op("act", lambda e, c=c, sl=sl: e.activation(
                            out=w_out_bf[:, c, :], in_=stg[sl][:, :], func=AF.Copy,
                            scale=gout[:, c:c + 1]), reads=[stg_r[sl], r_const], writes=[r_w])
                k.op("pool", lambda e: e.memset(sgwT[64:128, :, 0:64], 0.0), reads=[r_sg], writes=[r_sg])
                k.barrier(scratch[:, 0:1])

            for b in range(BPC):
                with ExitStack() as e1:
                    NXS = 5
                    xt = [sb(e1, "xt%d" % i, [128, D], F32) for i in range(NXS)]
                    xt_r = [Reg() for _ in range(NXS)]
                    xt_ch = [k.chan() for _ in range(NXS)]
                    junk = sb(e1, "junk", [128, D], BF16)
                    junk_r = Reg()
                    xn = [sb(e1, "xn%d" % i, [128, D], BF16) for i in range(2)]
                    xn_r = [Reg() for _ in range(2)]
                    hT = [sb(e1, "hT%d" % i, [128, 8, 512], BF16) for i in range(2)]
                    hT_r = [[Reg() for _ in range(4)] for _ in range(2)]
                    gu = [sb(e1, "gu%d" % i, [128, 512], F32) for i in range(4)]
                    gv = [sb(e1, "gv%d" % i, [128, 512], F32) for i in range(4)]
                    gu_r = [Reg() for _ in range(4)]
                    gv_r = [Reg() for _ in range(4)]
                    braw = gv
                    braw_r = gv_r
                    t1 = sb(e1, "t1", [128, 512], F32)
                    t1_r = Reg()
                    t2 = sb(e1, "t2", [128, 512], F32)
                    t2_r = Reg()
                    vn = [sb(e1, "vn%d" % i, [128, 512], BF16) for i in range(2)]
                    vn_r = [Reg() for _ in range(2)]
                    bn = [sb(e1, "bn%d" % i, [128, 512], BF16) for i in range(2)]
                    bn_r = [Reg() for _ in range(2)]
                    st = sb(e1, "st", [128, 2, 32], F32)
                    st_r = [[Reg() for _ in range(8)] for _ in range(2)]
                    bst = sb(e1, "bst", [128, 4, 6], F32)
                    bst_r = [Reg() for _ in range(4)]
                    mv = sb(e1, "mv", [128, 4, 2], F32)
                    mv_r = Reg()
                    tp_ps = ps(e1, "tp_ps", [128, 8, 128], BF16)
                    tp_r = Reg()
                    qk_ps = [ps(e1, "qk_ps%d" % i, [128, 512]) for i in range(2)]
                    qk_r = [Reg() for _ in range(2)]
                    v_ps = ps(e1, "v_ps", [128, 512])
                    v_r = Reg()
                    u_ps = ps(e1, "u_ps", [128, 512])
                    u_r = Reg()
                    vg_ps = ps(e1, "vg_ps", [128, 512])
                    vg_r = Reg()
                    mix_ps = ps(e1, "mix_ps", [128, 512])
                    mix_r = Reg()
                    bt_ps = ps(e1, "bt_ps", [128, 4, 128], BF16)
                    bt_r = Reg()
                    QT_r = Reg()
                    KT_r = Reg()
                    V_r = Reg()
                    bT_r = Reg()

                    nload = 0
                    nqk = 0
                    for G in range(4):
                        gp = G % 2
                        R = st_r[gp]
                        slots = []
                        for j in range(4):
                            i = 4 * G + j
                            sl = nload % NXS
                            nload += 1
                            slots.append(sl)
                            k.dma("sp", xt_ch[sl], xt[sl][:, :], x[b, i * 128:(i + 1) * 128, :],
                                  writes=[xt_r[sl]])
                            k.op("act", lambda e, sl=sl, j=j: e.activation(
                                out=junk[:, :], in_=xt[sl][:, :], func=AF.Square,
                                accum_out=st[:, gp, j:j + 1]),
                                reads=[xt_r[sl]], writes=[junk_r, R[0]])
                        k.op("act", lambda e: e.activation(
                            out=st[:, gp, 4:8], in_=st[:, gp, 0:4], func=AF.Sqrt,
                            bias=eps_t[:, 0:1], scale=1.0 / D), reads=[R[0]], writes=[R[1]])
                        k.op("dve", lambda e: e.reciprocal(out=st[:, gp, 8:12], in_=st[:, gp, 4:8]),
                             reads=[R[1]], writes=[R[2]])
                        for j in range(4):
                            sl = slots[j]
                            xs = j % 2
                            k.op("dve", lambda e, sl=sl, xs=xs, j=j: e.scalar_tensor_tensor(
                                out=xn[xs][:, :], in0=xt[sl][:, :], scalar=st[:, gp, 8 + j:9 + j],
                                in1=gmixf[:, :], op0=ALU.mult, op1=ALU.mult),
                                reads=[xt_r[sl], R[2], r_w], writes=[xn_r[xs]])
                            for c in range(8):
                                k.op("pe", lambda e, c=c, xs=xs: e.transpose(
                                    tp_ps[:, c, :], xn[xs][:, c * 128:(c + 1) * 128], ident[:]),
                                    reads=[xn_r[xs]], writes=[tp_r])
                            k.op("act", lambda e, j=j: e.copy(
                                out=hT[gp][:, :, j * 128:(j + 1) * 128], in_=tp_ps[:, :, :]),
                                reads=[tp_r], writes=[hT_r[gp][j]])
                        tc0 = G * 512
                        for eb in range(8):
                            qs = nqk % 2
                            nqk += 1
                            for c in range(8):
                                k.op("pe", lambda e, c=c, eb=eb, qs=qs: e.matmul(
                                    qk_ps[qs][:, :], lhsT=w_in_bf[:, c, eb * 128:(eb + 1) * 128],
                                    rhs=hT[gp][:, c, :], start=(c == 0), stop=(c == 7)),
                                    reads=hT_r[gp] + [r_w], writes=[qk_r[qs]])
                            if eb < 4:
                                k.op("dve", lambda e, eb=eb, qs=qs: e.tensor_scalar(
                                    out=QT[:, eb, tc0:tc0 + 512], in0=qk_ps[qs][:, :], scalar1=0.125,
                                    scalar2=None, op0=ALU.mult), reads=[qk_r[qs]], writes=[QT_r])
                            else:
                                k.op("act", lambda e, eb=eb, qs=qs: e.copy(
                                    out=KT[:, eb - 4, tc0:tc0 + 512], in_=qk_ps[qs][:, :]),
                                    reads=[qk_r[qs]], writes=[KT_r])
                        for j in range(4):
                            i = 4 * G + j
                            for (pst, pr, c0) in ((v_ps, v_r, 1024), (u_ps, u_r, 1536), (vg_ps, vg_r, 2048)):
                                for c in range(8):
                                    k.op("pe", lambda e, c=c, pst=pst, c0=c0, j=j: e.matmul(
                                        pst[:, :], lhsT=hT[gp][:, c, j * 128:(j + 1) * 128],
                                        rhs=w_in_bf[:, c, c0:c0 + 512], start=(c == 0), stop=(c == 7)),
                                        reads=[hT_r[gp][j], r_w], writes=[pr])
                            k.op("dve", lambda e, i=i: e.tensor_copy(out=Vsb[:, i, :], in_=v_ps[:, :]),
                                 reads=[v_r], writes=[V_r])
                            k.op("act", lambda e, j=j: e.activation(
                                out=gu[j][:, :], in_=u_ps[:, :], func=AF.Gelu_apprx_tanh),
                                reads=[u_r], writes=[gu_r[j]])
                            k.op("act", lambda e, j=j: e.activation(
                                out=gv[j][:, :], in_=vg_ps[:, :], func=AF.Gelu_apprx_tanh),
                                reads=[vg_r], writes=[gv_r[j]])
                            k.op("dve", lambda e, j=j: e.bn_stats(out=bst[:, j, :], in_=gv[j][:, :]),
                                 reads=[gv_r[j]], writes=[bst_r[j]])
                            k.op("dve", lambda e, j=j: e.bn_aggr(out=mv[:, j, :], in_=bst[:, j, :]),
                                 reads=[bst_r[j]], writes=[mv_r])
                        k.op("act", lambda e: e.activation(
                            out=st[:, gp, 12:16], in_=mv[:, :, 1], func=AF.Sqrt,
                            bias=eps_t[:, 0:1], scale=1.0), reads=[mv_r], writes=[R[3]])
                        k.op("dve", lambda e: e.reciprocal(out=st[:, gp, 16:20], in_=st[:, gp, 12:16]),
                             reads=[R[3]], writes=[R[4]])
                        for j in range(4):
                            vs = j % 2
                            k.op("dve", lambda e, j=j: e.scalar_tensor_tensor(
                                out=t1[:, :], in0=gv[j][:, :], scalar=mv[:, j, 0:1], in1=lng[:, :],
                                op0=ALU.subtract, op1=ALU.mult),
                                reads=[gv_r[j], mv_r, r_w], writes=[t1_r])
                            k.op("dve", lambda e, j=j, vs=vs: e.scalar_tensor_tensor(
                                out=vn[vs][:, :], in0=t1[:, :], scalar=st[:, gp, 16 + j:17 + j], in1=lnb[:, :],
                                op0=ALU.mult, op1=ALU.add),
                                reads=[t1_r, R[4], r_w], writes=[vn_r[vs]])
                            for g in range(8):
                                k.op("pe", lambda e, g=g, vs=vs: e.matmul(
                                    mix_ps[:, g * 64:(g + 1) * 64], lhsT=sgwT[:, g, :],
                                    rhs=vn[vs][:, g * 64:(g + 1) * 64], start=True, stop=True),
                                    reads=[vn_r[vs], r_w], writes=[mix_r])
                            k.op("dve", lambda e: e.tensor_tensor(
                                out=t2[:, :], in0=mix_ps[:, :], in1=biasf[:, :], op=ALU.add),
                                reads=[mix_r, r_w], writes=[t2_r])
                            k.op("pool", lambda e, j=j: e.tensor_tensor(
                                out=braw[j][:, :], in0=t2[:, :], in1=gu[j][:, :], op=ALU.mult),
                                reads=[t2_r, gu_r[j]], writes=[braw_r[j]])
                            k.op("act", lambda e, j=j: e.activation(
                                out=junk[:, 0:512], in_=braw[j][:, :], func=AF.Square,
                                accum_out=st[:, gp, 20 + j:21 + j]),
                                reads=[braw_r[j]], writes=[junk_r, R[5]])
                        k.op("act", lambda e: e.activation(
                            out=st[:, gp, 24:28], in_=st[:, gp, 20:24], func=AF.Sqrt,
                            bias=eps_t[:, 0:1], scale=1.0 / 512), reads=[R[5]], writes=[R[6]])
                        k.op("dve", lambda e: e.reciprocal(out=st[:, gp, 28:32], in_=st[:, gp, 24:28]),
                             reads=[R[6]], writes=[R[7]])
                        for j in range(4):
                            i = 4 * G + j
                            bs = j % 2
                            k.op("dve", lambda e, j=j, bs=bs: e.tensor_scalar(
                                out=bn[bs][:, :], in0=braw[j][:, :], scalar1=st[:, gp, 28 + j:29 + j],
                                scalar2=None, op0=ALU.mult),
                                reads=[braw_r[j], R[7]], writes=[bn_r[bs]])
                            for c in range(4):
                                k.op("pe", lambda e, c=c, bs=bs: e.transpose(
                                    bt_ps[:, c, :], bn[bs][:, c * 128:(c + 1) * 128], ident[:]),
                                    reads=[bn_r[bs]], writes=[bt_r])
                            k.op("dve", lambda e, i=i: e.tensor_copy(
                                out=bT[:, :, i * 128:(i + 1) * 128], in_=bt_ps[:, :, :]),
                                reads=[bt_r], writes=[bT_r])
                    k.barrier(scratch[:, 0:1])

                with ExitStack() as e2:
                    aT = sb(e2, "aT", [128, 4, S], BF16)
                    aT_r = Reg()
                    with ExitStack() as e3:
                        QTz = [[sb(e3, "QTz%d_%d" % (p, i), [128, 512], BF16) for i in range(2)]
                               for p in range(2)]
                        QTz_r = [[Reg() for _ in range(2)] for _ in range(2)]
                        NE = 3
                        ebuf = [sb(e3, "ebuf%d" % i, [128, 512], F32) for i in range(NE)]
                        ebuf_r = [Reg() for _ in range(NE)]
                        NSP = 6
                        spb = [sb(e3, "spb%d" % i, [128, 512], BF16) for i in range(NSP)]
                        spb_r = [Reg() for _ in range(NSP)]
                        Sb = [sb(e3, "Sb%d" % i, [128, 512], BF16) for i in range(2)]
                        Sb_r = [Reg() for _ in range(2)]
                        NA = 4
                        Ab = [sb(e3, "Ab%d" % i, [128, 512], BF16) for i in range(NA)]
                        Ab_r = [Reg() for _ in range(NA)]
                        sq = sb(e3, "sq", [128, 512], F32)
                        sq_r = Reg()
                        NF = 5
                        f_ps = [ps(e3, "f_ps%d" % i, [128, 512]) for i in range(NF)]
                        f_r = [Reg() for _ in range(NF)]
                        o_ps = [ps(e3, "o_ps%d" % i, [128, 512]) for i in range(2)]
                        o_r = [Reg() for _ in range(2)]
                        ssa_ps = ps(e3, "ssa_ps", [128, 16])
                        ssa_r = Reg()
                        ssa_sb_r = Reg()
                        r_att = Reg()
                        for p in range(2):
                            for i in range(2):
                                k.op("pool", lambda e, p=p, i=i: e.memset(QTz[p][i][:, :], 0.0),
                                     writes=[QTz_r[p][i]])

                        steps = []
                        nhead = 0
                        for qg in range(4):
                            for hp in range(4):
                                for hh in range(2):
                                    nk = 4 * qg + 4
                                    prev_c0 = None
                                    for si in range(nk):
                                        kb = nk - 1 - si
                                        j = kb - 4 * qg
                                        c0 = max(0, j) * 128
                                        steps.append(dict(qg=qg, hp=hp, hh=hh, kb=kb, first=(si == 0),
                                                          last=(si == nk - 1), j=j, c0=c0, pc0=prev_c0,
                                                          hidx=nhead))
                                        prev_c0 = c0
                                    nhead += 1
                        state = {"S": None}

                        def s1a(i):
                            s_ = steps[i]
                            hp, hh, kb, qg, c0 = s_["hp"], s_["hh"], s_["kb"], s_["qg"], s_["c0"]
                            pb = 64 * hh
                            qb_ = (s_["hidx"] // 2) % 2
                            qz, qzr = QTz[hh][qb_], QTz_r[hh][qb_]
                            if s_["first"]:
                                k.op("pool", lambda e: e.tensor_copy(
                                    out=qz[pb:pb + 64, :], in_=QT[pb:pb + 64, hp, qg * 512:(qg + 1) * 512]),
                                    reads=[r_att], writes=[qzr])
                            fb = i % NF
                            eb = i % NE
                            k.op("pe", lambda e: e.matmul(
                                f_ps[fb][:, c0:512], lhsT=KT[:, hp, kb * 128:(kb + 1) * 128], rhs=qz[:, c0:512],
                                start=True, stop=True), reads=[qzr, r_att], writes=[f_r[fb]])
                            k.op("act", lambda e: e.activation(
                                out=ebuf[eb][:, c0:512], in_=f_ps[fb][:, c0:512], func=AF.Exp),
                                reads=[f_r[fb]], writes=[ebuf_r[eb]])

                        def s1b(i):
                            s_ = steps[i]
                            c0 = s_["c0"]
                            eb = i % NE
                            sb_i = i % NSP
                            k.op("act", lambda e: e.activation(
                                out=spb[sb_i][:, c0:512], in_=ebuf[eb][:, c0:512], func=AF.Ln,
                                bias=one_t[:, 0:1], scale=1.0), reads=[ebuf_r[eb]], writes=[spb_r[sb_i]])
                            if s_["j"] >= 0:
                                k.op("pool", lambda e: e.tensor_tensor(
                                    out=spb[sb_i][:, c0:c0 + 128], in0=spb[sb_i][:, c0:c0 + 128],
                                    in1=masks[:, :], op=ALU.mult),
                                    reads=[spb_r[sb_i]], writes=[spb_r[sb_i]])

                        def s2p(i):
                            s_ = steps[i]
                            fb = i % NF
                            sb_i = i % NSP
                            first = s_["first"]
                            c0, pc0 = s_["c0"], s_["pc0"]
                            k.op("pe", lambda e: e.matmul(
                                f_ps[fb][:, c0:512], lhsT=ntri[:, :], rhs=spb[sb_i][:, c0:512],
                                start=False, stop=first, skip_group_check=True),
                                reads=[spb_r[sb_i]], writes=[f_r[fb]])
                            if not first:
                                S_t, S_rg = state["S"]
                                k.op("pe", lambda e: e.matmul(
                                    f_ps[fb][:, pc0:512], lhsT=nones[:, :], rhs=S_t[:, pc0:512],
                                    start=False, stop=True, skip_group_check=True),
                                    reads=[S_rg], writes=[f_r[fb]])
                            if not s_["last"]:
                                if first:
                                    state["S"] = (spb[sb_i], spb_r[sb_i])
                                    state["Sn"] = 0
                                else:
                                    S_t, S_rg = state["S"]
                                    nb = state["Sn"]
                                    state["Sn"] = 1 - nb
                                    k.op("dve", lambda e: e.tensor_tensor(
                                        out=Sb[nb][:, pc0:512], in0=S_t[:, pc0:512], in1=spb[sb_i][:, pc0:512],
                                        op=ALU.add), reads=[S_rg, spb_r[sb_i]], writes=[Sb_r[nb]])
                                    if c0 < pc0:
                                        k.op("dve", lambda e: e.tensor_copy(
                                            out=Sb[nb][:, c0:pc0], in_=spb[sb_i][:, c0:pc0]),
                                            reads=[spb_r[sb_i]], writes=[Sb_r[nb]])
                                    state["S"] = (Sb[nb], Sb_r[nb])

                        def s2a(i):
                            s_ = steps[i]
                            fb = i % NF
                            c0 = s_["c0"]
                            ab = i % NA
                            k.op("act", lambda e: e.activation(
                                out=Ab[ab][:, c0:512], in_=f_ps[fb][:, c0:512], func=AF.Exp),
                                reads=[f_r[fb]], writes=[Ab_r[ab]])
                            if s_["j"] >= 0:
                                k.op("pool", lambda e: e.tensor_tensor(
                                    out=Ab[ab][:, c0:c0 + 128], in0=Ab[ab][:, c0:c0 + 128],
                                    in1=masks[:, :], op=ALU.mult),
                                    reads=[Ab_r[ab]], writes=[Ab_r[ab]])

                        def s3(i):
                            s_ = steps[i]
                            hp, hh, kb, qg, c0 = s_["hp"], s_["hh"], s_["kb"], s_["qg"], s_["c0"]
                            pb = 64 * hh
                            ab = i % NA
                            ob = s_["hidx"] % 2
                            k.op("pe", lambda e: e.matmul(
                                o_ps[ob][:, c0:512], lhsT=Vsb[:, kb, hp * 128:(hp + 1) * 128], rhs=Ab[ab][:, c0:512],
                                start=s_["first"], stop=s_["last"], skip_group_check=True),
                                reads=[Ab_r[ab], r_att], writes=[o_r[ob]])
                            if s_["last"]:
                                cs = slice(qg * 512, (qg + 1) * 512)
                                k.op("dve", lambda e: e.tensor_copy(
                                    out=aT[pb:pb + 64, hp, cs], in_=o_ps[ob][pb:pb + 64, :]),
                                    reads=[o_r[ob]], writes=[aT_r])
                                k.op("pool", lambda e: e.tensor_tensor(
                                    out=sq[pb:pb + 64, :], in0=aT[pb:pb + 64, hp, cs], in1=aT[pb:pb + 64, hp, cs],
                                    op=ALU.mult), reads=[aT_r], writes=[sq_r])
                                if hh == 1:
                                    for jj in range(4):
                                        k.op("pe", lambda e, jj=jj: e.matmul(
                                            ssa_ps[:, hp * 4 + jj:hp * 4 + jj + 1],
                                            lhsT=sq[:, jj * 128:(jj + 1) * 128], rhs=ones32[:, 0:1],
                                            start=True, stop=True), reads=[sq_r, r_const], writes=[ssa_r])
                                    if hp == 3:
                                        k.op("dve", lambda e: e.tensor_reduce(
                                            out=ssa_sb[:, qg * 4:(qg + 1) * 4],
                                            in_=ssa_ps[:, 0:16].rearrange("p (h j) -> p j h", h=4),
                                            axis=AX.X, op=ALU.add), reads=[ssa_r], writes=[ssa_sb_r])

                        nst = len(steps)
                        for it in range(nst + 3):
                            if it < nst:
                                s1a(it)
                            if 0 <= it - 1 < nst:
                                s2p(it - 1)
                            if 0 <= it - 2 < nst:
                                s2a(it - 2)
                            if it < nst:
                                s1b(it)
                            if 0 <= it - 3 < nst:
                                s3(it - 3)
                        k.barrier(scratch[:, 0:1])

                    with ExitStack() as e4:
                        xr = [sb(e4, "xr%d" % i, [128, D], F32) for i in range(3)]
                        xr_r = [Reg() for _ in range(3)]
                        xr_ch = [k.chan() for _ in range(3)]
                        xo_ch = [k.chan() for _ in range(3)]
                        sda = sb(e4, "sda", [128, NT], F32)
                        rsa = sb(e4, "rsa", [128, NT], F32)
                        rsa_r = Reg()
                        acc = [[ps(e4, "acc%d_%d" % (i, h), [128, 512]) for h in range(4)] for i in range(2)]
                        acc_r = [[Reg() for _ in range(4)] for _ in range(2)]
                        r_all = Reg()
                        k.op("act", lambda e: e.activation(
                            out=sda[:, :], in_=ssa_sb[:, :], func=AF.Sqrt, bias=eps_t[:, 0:1],
                            scale=1.0 / 512), reads=[r_all], writes=[rsa_r])
                        k.op("dve", lambda e: e.reciprocal(out=rsa[:, :], in_=sda[:, :]),
                             reads=[rsa_r], writes=[rsa_r])
                        for i in range(NT):
                            sl = i % 3
                            ab = i % 2
                            k.dma("sp", xr_ch[sl], xr[sl][:, :], x[b, i * 128:(i + 1) * 128, :],
                                  writes=[xr_r[sl]])
                            for hf in range(2):
                                for c in range(4):
                                    k.op("pe", lambda e, c=c, hf=hf: e.matmul(
                                        acc[ab][hf][:, :], lhsT=aT[:, c, i * 128:(i + 1) * 128],
                                        rhs=w_out_bf[:, c, hf * 512:(hf + 1) * 512],
                                        start=(c == 0), stop=(c == 3)), reads=[r_all], writes=[acc_r[ab][hf]])
                                for c in range(4):
                                    k.op("pe", lambda e, c=c, hf=hf: e.matmul(
                                        acc[ab][2 + hf][:, :], lhsT=bT[:, c, i * 128:(i + 1) * 128],
                                        rhs=w_out_bf[:, 4 + c, hf * 512:(hf + 1) * 512],
                                        start=(c == 0), stop=(c == 3)), reads=[r_all], writes=[acc_r[ab][2 + hf]])
                            for hf in range(2):
                                cs = slice(hf * 512, (hf + 1) * 512)
                                k.op("dve", lambda e, hf=hf, cs=cs: e.scalar_tensor_tensor(
                                    out=xr[sl][:, cs], in0=acc[ab][hf][:, :], scalar=rsa[:, i:i + 1],
                                    in1=xr[sl][:, cs], op0=ALU.mult, op1=ALU.add),
                                    reads=[acc_r[ab][hf], rsa_r, xr_r[sl]], writes=[xr_r[sl]])
                                k.op("dve", lambda e, hf=hf, cs=cs: e.tensor_tensor(
                                    out=xr[sl][:, cs], in0=xr[sl][:, cs], in1=acc[ab][2 + hf][:, :], op=ALU.add),
                                    reads=[acc_r[ab][2 + hf], xr_r[sl]], writes=[xr_r[sl]])
                            k.dma("sp", xo_ch[sl], y[b, i * 128:(i + 1) * 128, :], xr[sl][:, :],
                                  reads=[xr_r[sl]], writes=[y_reg[b][i]])
                        k.barrier(scratch[:, 0:1])

        with ExitStack() as esB:
            w_up_bf = sb(esB, "w_up_bf", [128, 8, DFF], BF16)
            w_dn_bf = sb(esB, "w_dn_bf", [128, 32, D], BF16)
            gf = sb(esB, "gf", [128, D], F32)
            gmlpf = sb(esB, "gmlpf", [128, D], F32)
            r_wB = Reg()
            wch = k.chan()
            wch2 = k.chan()
            k.dma("sp", wch2, gf[:], gf_d[:, :], writes=[r_wB])
            k.dma("sp", wch2, gmlpf[:], gmlpf_d[:, :], writes=[r_wB])
            for c in range(8):
                k.dma("pool", wch, w_up_bf[:, c, :], w_up[c * 128:(c + 1) * 128, :], writes=[r_wB])
            for q in range(4):
                k.dma("pool", wch, w_dn_bf[:, q * 8:(q + 1) * 8, :],
                      w_down[q * 1024:(q + 1) * 1024, :].rearrange("(c p) d -> p c d", p=128),
                      writes=[r_wB])
            k.barrier(scratch[:, 0:1])

            with ExitStack() as e5:
                x1t = [sb(e5, "x1t%d" % i, [128, D], F32) for i in range(4)]
                x1_r = [Reg() for _ in range(4)]
                x1_ch = [k.chan() for _ in range(4)]
                xo_ch = [k.chan() for _ in range(4)]
                junk = sb(e5, "junkB", [128, D], BF16)
                junk_r = Reg()
                xn = [sb(e5, "xnB%d" % i, [128, D], BF16) for i in range(2)]
                xn_r = [Reg() for _ in range(2)]
                h2T = sb(e5, "h2T", [128, 8, 512], BF16)
                h2_r = [Reg() for _ in range(4)]
                actT = sb(e5, "actT", [128, 32, 512], BF16)
                act_r = [Reg() for _ in range(32)]
                rr = [sb(e5, "rr%d" % i, [128, 512], F32) for i in range(2)]
                rr_r = [Reg() for _ in range(2)]
                st = sb(e5, "stB", [128, 32], F32)
                R = [Reg() for _ in range(8)]
                tp_ps = ps(e5, "tpB", [128, 8, 128], BF16)
                tp_r = Reg()
                up_ps = [ps(e5, "up_ps%d" % i, [128, 512]) for i in range(3)]
                up_r = [Reg() for _ in range(3)]
                dn_ps = [ps(e5, "dn_ps%d" % i, [128, 512]) for i in range(4)]
                dn_r = [Reg() for _ in range(4)]
                out_chs = []
                nup = 0
                ndn = 0
                for blk in range(BPC * 4):
                    b = blk // 4
                    G = blk % 4
                    for j in range(4):
                        i = 4 * G + j
                        k.dma("sp", x1_ch[j], x1t[j][:, :], y[b, i * 128:(i + 1) * 128, :],
                              reads=[y_reg[b][i]], writes=[x1_r[j]])
                        k.op("act", lambda e, j=j: e.activation(
                            out=junk[:, :], in_=x1t[j][:, :], func=AF.Square, accum_out=st[:, j:j + 1]),
                            reads=[x1_r[j]], writes=[junk_r, R[0]])
                    k.op("act", lambda e: e.activation(
                        out=st[:, 4:8], in_=st[:, 0:4], func=AF.Sqrt, bias=eps_t[:, 0:1], scale=1.0 / D),
                        reads=[R[0]], writes=[R[1]])
                    k.op("dve", lambda e: e.reciprocal(out=st[:, 8:12], in_=st[:, 4:8]),
                         reads=[R[1]], writes=[R[2]])
                    for j in range(4):
                        xs = j % 2
                        k.op("dve", lambda e, j=j, xs=xs: e.scalar_tensor_tensor(
                            out=xn[xs][:, :], in0=x1t[j][:, :], scalar=st[:, 8 + j:9 + j], in1=gmlpf[:, :],
                            op0=ALU.mult, op1=ALU.mult), reads=[x1_r[j], R[2], r_wB], writes=[xn_r[xs]])
                        for c in range(8):
                            k.op("pe", lambda e, c=c, xs=xs: e.transpose(
                                tp_ps[:, c, :], xn[xs][:, c * 128:(c + 1) * 128], ident[:]),
                                reads=[xn_r[xs]], writes=[tp_r])
                        k.op("dve", lambda e, j=j: e.tensor_copy(
                            out=h2T[:, :, j * 128:(j + 1) * 128], in_=tp_ps[:, :, :]),
                            reads=[tp_r], writes=[h2_r[j]])
                    for fc in range(32):
                        ub = nup % 3
                        rb = nup % 2
                        nup += 1
                        for c in range(8):
                            k.op("pe", lambda e, c=c, fc=fc, ub=ub: e.matmul(
                                up_ps[ub][:, :], lhsT=w_up_bf[:, c, fc * 128:(fc + 1) * 128],
                                rhs=h2T[:, c, :], start=(c == 0), stop=(c == 7)),
                                reads=h2_r + [r_wB], writes=[up_r[ub]])
                        k.op("act", lambda e, ub=ub, rb=rb: e.activation(
                            out=rr[rb][:, :], in_=up_ps[ub][:, :], func=AF.Relu),
                            reads=[up_r[ub]], writes=[rr_r[rb]])
                        k.op("pool", lambda e, fc=fc, rb=rb: e.tensor_tensor(
                            out=actT[:, fc, :], in0=rr[rb][:, :], in1=rr[rb][:, :], op=ALU.mult),
                            reads=[rr_r[rb]], writes=[act_r[fc]])
                    for j in range(4):
                        for hf in range(2):
                            db = ndn % 4
                            ndn += 1
                            cs = slice(hf * 512, (hf + 1) * 512)
                            for fc in range(32):
                                k.op("pe", lambda e, fc=fc, db=db, j=j, hf=hf: e.matmul(
                                    dn_ps[db][:, :], lhsT=actT[:, fc, j * 128:(j + 1) * 128],
                                    rhs=w_dn_bf[:, fc, hf * 512:(hf + 1) * 512],
                                    start=(fc == 0), stop=(fc == 31)),
                                    reads=[act_r[fc], r_wB], writes=[dn_r[db]])
                            k.op("dve", lambda e, db=db, j=j, cs=cs: e.tensor_tensor(
                                out=x1t[j][:, cs], in0=x1t[j][:, cs], in1=dn_ps[db][:, :], op=ALU.add),
                                reads=[dn_r[db], x1_r[j]], writes=[x1_r[j]])
                        k.op("act", lambda e, j=j: e.activation(
                            out=junk[:, :], in_=x1t[j][:, :], func=AF.Square, accum_out=st[:, 12 + j:13 + j]),
                            reads=[x1_r[j]], writes=[junk_r, R[3]])
                    k.op("act", lambda e: e.activation(
                        out=st[:, 16:20], in_=st[:, 12:16], func=AF.Sqrt, bias=eps_t[:, 0:1], scale=1.0 / D),
                        reads=[R[3]], writes=[R[4]])
                    k.op("dve", lambda e: e.reciprocal(out=st[:, 20:24], in_=st[:, 16:20]),
                         reads=[R[4]], writes=[R[5]])
                    for j in range(4):
                        i = 4 * G + j
                        k.op("dve", lambda e, j=j: e.scalar_tensor_tensor(
                            out=x1t[j][:, :], in0=x1t[j][:, :], scalar=st[:, 20 + j:21 + j], in1=gf[:, :],
                            op0=ALU.mult, op1=ALU.mult), reads=[x1_r[j], R[5], r_wB], writes=[x1_r[j]])
                        tk = k.dma("sp", xo_ch[j], y[b, i * 128:(i + 1) * 128, :], x1t[j][:, :],
                                   reads=[x1_r[j]], writes=[y_reg[b][i]])
                k.barrier(scratch[:, 0:1])
    return nc


def _host_consts():
    ident = np.eye(128, dtype=np.float32)
    jj = np.arange(128)[:, None]
    ss = np.arange(128)[None, :]
    ntri = -(jj >= ss).astype(np.float32)
    nones = -np.ones((128, 128), np.float32)
    s_idx = np.arange(128)[:, None]
    t_idx = np.arange(128)[None, :]
    masks = (s_idx < t_idx).astype(np.float32)
    return ident, ntri, nones, masks


def kernel(x, norm_mix_g, w_in, sg_ln_g, sg_ln_b, sg_w, sg_b, out_norm_g, w_out,
           norm_mlp_g, w_up, w_down, norm_final_g):
    f = np.float32
    x = np.ascontiguousarray(np.asarray(x, dtype=f))
    ident, ntri, nones, masks = _host_consts()

    def pc(v):
        return np.ascontiguousarray(np.asarray(v, dtype=f).reshape(8, 128).T)

    shared = {
        "w_in": np.ascontiguousarray(np.asarray(w_in, dtype=f)[0]),
        "w_out": np.ascontiguousarray(np.asarray(w_out, dtype=f)[0]),
        "w_up": np.ascontiguousarray(np.asarray(w_up, dtype=f)[0]),
        "w_down": np.ascontiguousarray(np.asarray(w_down, dtype=f)[0]),
        "g_mix": pc(norm_mix_g[0]),
        "g_out": pc(out_norm_g[0]),
        "g_mlp": pc(norm_mlp_g[0]),
        "lng_full": np.ascontiguousarray(np.broadcast_to(np.asarray(sg_ln_g, dtype=f)[0][None, :], (128, 512))),
        "lnb_full": np.ascontiguousarray(np.broadcast_to(np.asarray(sg_ln_b, dtype=f)[0][None, :], (128, 512))),
        "bias_full": np.ascontiguousarray(
            np.broadcast_to(np.asarray(sg_b, dtype=f)[0].T[:, :, None], (128, 8, 64)).reshape(128, 512)),
        "gmix_full": np.ascontiguousarray(np.broadcast_to(np.asarray(norm_mix_g, dtype=f)[0][None, :], (128, D))),
        "gmlp_full": np.ascontiguousarray(np.broadcast_to(np.asarray(norm_mlp_g, dtype=f)[0][None, :], (128, D))),
        "gf_full": np.ascontiguousarray(np.broadcast_to(np.asarray(norm_final_g, dtype=f)[None, :], (128, D))),
        "sgwT": np.ascontiguousarray(np.transpose(np.asarray(sg_w, dtype=f)[0], (2, 0, 1))),
        "ident": ident, "ntri": ntri, "nones": nones, "masks": masks,
    }
    nc = build()
    in_maps = []
    for c in range(NCORES):
        m = dict(shared)
        m["x"] = np.ascontiguousarray(x[c * BPC:(c + 1) * BPC])
        in_maps.append(m)
    res = run_bass_kernel_spmd(nc, in_maps, core_ids=list(range(NCORES)))
    out = np.concatenate([np.asarray(r["y"]) for r in res.results], axis=0)
    return out.astype(np.float32)
```

```python
import numpy as np
import ml_dtypes
from contextlib import ExitStack

import concourse.bass as bass
import concourse.mybir as mybir
from concourse.bass_utils import run_bass_kernel_spmd

F32 = mybir.dt.float32
BF16 = mybir.dt.bfloat16
AF = mybir.ActivationFunctionType
ALU = mybir.AluOpType
AX = mybir.AxisListType

NCORES = 8
BPC = 2
S = 2048
D = 1024
NT = S // 128
INW = 2560
DFF = 4096
EPS = 1e-6


class Tk:
    __slots__ = ("sem", "val")

    def __init__(self, sem, val):
        self.sem = sem
        self.val = val


class Reg:
    __slots__ = ("w", "r", "name")

    def __init__(self, name=""):
        self.w = None
        self.r = {}
        self.name = name


class Chan:
    def __init__(self, sem):
        self.sem = sem
        self.n = 0


class Eng:
    def __init__(self, eng, sem, name):
        self.eng = eng
        self.sem = sem
        self.n = 0
        self.seen = {}
        self.name = name

    def wait(self, tk):
        if tk is None:
            return
        k = id(tk.sem)
        if self.seen.get(k, 0) >= tk.val:
            return
        self.eng.wait_ge(tk.sem, tk.val)
        self.seen[k] = tk.val


class K:
    def __init__(self, nc, es):
        self.nc = nc
        self.es = es
        self.chans = []
        self.E = {}
        for nm, eng in (("pe", nc.tensor), ("act", nc.scalar), ("dve", nc.vector),
                        ("pool", nc.gpsimd), ("sp", nc.sync)):
            sem = es.enter_context(nc.semaphore("sem_" + nm))
            self.E[nm] = Eng(eng, sem, nm)
        self._nm = 0

    def chan(self):
        self._nm += 1
        c = Chan(self.es.enter_context(self.nc.semaphore("ch%d" % self._nm)))
        self.chans.append(c)
        return c

    def _deps(self, E, reads, writes):
        for r in reads:
            if r.w is not None:
                E.wait(r.w)
        pe = E.name == "pe"
        for w in writes:
            if w.w is not None and not (pe and w.w.sem is E.sem):
                E.wait(w.w)
            for tk in w.r.values():
                if not (pe and tk.sem is E.sem):
                    E.wait(tk)

    def _mark(self, tk, reads, writes):
        for r in reads:
            r.r[id(tk.sem)] = tk
        for w in writes:
            w.w = tk
            w.r = {}

    def op(self, e, fn, reads=(), writes=()):
        E = self.E[e]
        self._deps(E, reads, writes)
        inst = fn(E.eng)
        E.n += 1
        inst.then_inc(E.sem, 1)
        tk = Tk(E.sem, E.n)
        self._mark(tk, reads, writes)
        return tk

    def dma(self, q, ch, out, in_, reads=(), writes=()):
        E = self.E[q]
        self._deps(E, reads, writes)
        inst = E.eng.dma_start(out=out, in_=in_)
        ch.n += 1
        inst.then_inc(ch.sem, 16)
        tk = Tk(ch.sem, 16 * ch.n)
        self._mark(tk, reads, writes)
        return tk

    def barrier(self, scratch_ap):
        V = self.E["dve"]
        for nm, E in self.E.items():
            if E is not V and E.n > 0:
                V.wait(Tk(E.sem, E.n))
        for c in self.chans:
            if c.n > 0:
                V.wait(Tk(c.sem, 16 * c.n))
        if V.n > 0:
            V.wait(Tk(V.sem, V.n))
        inst = V.eng.memset(scratch_ap, 0.0)
        V.n += 1
        inst.then_inc(V.sem, 1)
        tk = Tk(V.sem, V.n)
        for nm, E in self.E.items():
            E.wait(tk)


def build():
    nc = bass.Bass("TRN2", target_bir_lowering=False)

    def din(name, shape):
        return nc.dram_tensor(name, list(shape), F32, kind="ExternalInput").ap()

    x = din("x", [BPC, S, D])
    w_in = din("w_in", [D, INW])
    w_out = din("w_out", [D, D])
    w_up = din("w_up", [D, DFF])
    w_down = din("w_down", [DFF, D])
    g_mix = din("g_mix", [128, 8])
    g_out = din("g_out", [128, 8])
    g_mlp = din("g_mlp", [128, 8])
    lng_d = din("lng_full", [128, 512])
    lnb_d = din("lnb_full", [128, 512])
    bias_d = din("bias_full", [128, 512])
    gf_d = din("gf_full", [128, D])
    gmixf_d = din("gmix_full", [128, D])
    gmlpf_d = din("gmlp_full", [128, D])
    sgwT_d = din("sgwT", [128, 8, 128])
    ident_d = din("ident", [128, 128])
    ntri_d = din("ntri", [128, 128])
    nones_d = din("nones", [128, 128])
    masks_d = din("masks", [128, 128])
    y = nc.dram_tensor("y", [BPC, S, D], F32, kind="ExternalOutput").ap()

    with ExitStack() as es:
        k = K(nc, es)

        uid = [0]

        def sb(es_, name, shape, dt):
            uid[0] += 1
            return es_.enter_context(nc.sbuf_tensor("s%d_%s" % (uid[0], name), list(shape), dt))

        def ps(es_, name, shape, dt=F32):
            uid[0] += 1
            return es_.enter_context(nc.psum_tensor("p%d_%s" % (uid[0], name), list(shape), dt))

        ident = sb(es, "ident", [128, 128], BF16)
        ntri = sb(es, "ntri", [128, 128], BF16)
        nones = sb(es, "nones", [128, 128], BF16)
        masks = sb(es, "masks", [128, 128], BF16)
        ones32 = sb(es, "ones32", [128, 1], F32)
        eps_t = sb(es, "eps_t", [128, 1], F32)
        one_t = sb(es, "one_t", [128, 1], F32)
        scratch = sb(es, "scratch", [128, 8], F32)
        gmix = sb(es, "gmix", [128, 8], F32)
        gout = sb(es, "gout", [128, 8], F32)
        gmlp = sb(es, "gmlp", [128, 8], F32)
        r_const = Reg("const")

        cch = k.chan()
        cch2 = k.chan()
        for dst, src in ((ident, ident_d), (ntri, ntri_d), (nones, nones_d)):
            k.dma("pool", cch, dst[:], src[:, :], writes=[r_const])
        k.dma("pool", cch, masks[:], masks_d[:, :], writes=[r_const])
        for dst, src in ((gmix, g_mix), (gout, g_out), (gmlp, g_mlp)):
            k.dma("sp", cch2, dst[:], src[:, :], writes=[r_const])
        k.op("dve", lambda e: e.memset(ones32[:], 1.0), writes=[r_const])
        k.op("dve", lambda e: e.memset(eps_t[:], EPS), writes=[r_const])
        k.op("dve", lambda e: e.memset(one_t[:], 1.0), writes=[r_const])
        k.barrier(scratch[:, 0:1])

        y_reg = [[Reg("y%d_%d" % (b, i)) for i in range(NT)] for b in range(BPC)]

        with ExitStack() as esA:
            w_in_bf = sb(esA, "w_in_bf", [128, 8, INW], BF16)
            w_out_bf = sb(esA, "w_out_bf", [128, 8, D], BF16)
            sgwT = sb(esA, "sgwT", [128, 8, 128], BF16)
            lng = sb(esA, "lng", [128, 512], F32)
            lnb = sb(esA, "lnb", [128, 512], F32)
            biasf = sb(esA, "biasf", [128, 512], F32)
            gmixf = sb(esA, "gmixf", [128, D], F32)
            QT = sb(esA, "QT", [128, 4, S], BF16)
            KT = sb(esA, "KT", [128, 4, S], BF16)
            Vsb = sb(esA, "Vsb", [128, NT, 512], BF16)
            bT = sb(esA, "bT", [128, 4, S], BF16)
            ssa_sb = sb(esA, "ssa_sb", [128, NT], F32)
            r_w = Reg("wA")

            with ExitStack() as esP:
                stg = [sb(esP, "stg%d" % i, [128, D], F32) for i in range(2)]
                stg_r = [Reg("stg%d" % i) for i in range(2)]
                stg_ch = [k.chan() for _ in range(2)]
                pch = k.chan()
                r_sg = Reg("sgw")
                sgch = k.chan()
                wich = k.chan()
                for c in range(8):
                    k.dma("pool", wich, w_in_bf[:, c, :], w_in[c * 128:(c + 1) * 128, :], writes=[r_w])
                k.dma("pool", sgch, sgwT[:], sgwT_d[:, :, :], writes=[r_sg])
                k.dma("sp", pch, lng[:], lng_d[:, :], writes=[r_w])
                k.dma("sp", pch, lnb[:], lnb_d[:, :], writes=[r_w])
                k.dma("sp", pch, biasf[:], bias_d[:, :], writes=[r_w])
                k.dma("sp", pch, gmixf[:], gmixf_d[:, :], writes=[r_w])
                for c in range(8):
                    sl = c % 2
                    k.dma("sp", stg_ch[sl], stg[sl][:, :], w_out[c * 128:(c + 1) * 128, :],
                          writes=[stg_r[sl]])
                    if c % 2 == 0:
                        k.op("dve", lambda e, c=c, sl=sl: e.tensor_scalar(
                            out=w_out_bf[:, c, :], in0=stg[sl][:, :], scalar1=gout[:, c:c + 1],
                            scalar2=None, op0=ALU.mult), reads=[stg_r[sl], r_const], writes=[r_w])
                    else:
                        k.op("act", lambda e, c=c, sl=sl: e.activation(
                            out=w_out_bf[:, c, :], in_=stg[sl][:, :], func=AF.Copy,
                            scale=gout[:, c:c + 1]), reads=[stg_r[sl], r_const], writes=[r_w])
                k.op("pool", lambda e: e.memset(sgwT[64:128, :, 0:64], 0.0), reads=[r_sg], writes=[r_sg])
                k.barrier(scratch[:, 0:1])

            for b in range(BPC):
                with ExitStack() as e1:
                    NXS = 5
                    xt = [sb(e1, "xt%d" % i, [128, D], F32) for i in range(NXS)]
                    xt_r = [Reg() for _ in range(NXS)]
                    xt_ch = [k.chan() for _ in range(NXS)]
                    junk = sb(e1, "junk", [128, D], BF16)
                    junk_r = Reg()
                    xn = [sb(e1, "xn%d" % i, [128, D], BF16) for i in range(2)]
                    xn_r = [Reg() for _ in range(2)]
                    hT = [sb(e1, "hT%d" % i, [128, 8, 512], BF16) for i in range(2)]
                    hT_r = [[Reg() for _ in range(4)] for _ in range(2)]
                    gu = [sb(e1, "gu%d" % i, [128, 512], F32) for i in range(4)]
                    gv = [sb(e1, "gv%d" % i, [128, 512], F32) for i in range(4)]
                    gu_r = [Reg() for _ in range(4)]
                    gv_r = [Reg() for _ in range(4)]
                    braw = gv
                    braw_r = gv_r
                    t1 = sb(e1, "t1", [128, 512], F32)
                    t1_r = Reg()
                    t2 = sb(e1, "t2", [128, 512], F32)
                    t2_r = Reg()
                    vn = [sb(e1, "vn%d" % i, [128, 512], BF16) for i in range(2)]
                    vn_r = [Reg() for _ in range(2)]
                    bn = [sb(e1, "bn%d" % i, [128, 512], BF16) for i in range(2)]
                    bn_r = [Reg() for _ in range(2)]
                    st = sb(e1, "st", [128, 2, 32], F32)
                    st_r = [[Reg() for _ in range(8)] for _ in range(2)]
                    bst = sb(e1, "bst", [128, 4, 6], F32)
                    bst_r = [Reg() for _ in range(4)]
                    mv = sb(e1, "mv", [128, 4, 2], F32)
                    mv_r = Reg()
                    tp_ps = ps(e1, "tp_ps", [128, 8, 128], BF16)
                    tp_r = Reg()
                    qk_ps = [ps(e1, "qk_ps%d" % i, [128, 512]) for i in range(2)]
                    qk_r = [Reg() for _ in range(2)]
                    v_ps = ps(e1, "v_ps", [128, 512])
                    v_r = Reg()
                    u_ps = ps(e1, "u_ps", [128, 512])
                    u_r = Reg()
                    vg_ps = ps(e1, "vg_ps", [128, 512])
                    vg_r = Reg()
                    mix_ps = ps(e1, "mix_ps", [128, 512])
                    mix_r = Reg()
                    bt_ps = ps(e1, "bt_ps", [128, 4, 128], BF16)
                    bt_r = Reg()
                    QT_r = Reg()
                    KT_r = Reg()
                    V_r = Reg()
                    bT_r = Reg()

                    nload = 0
                    nqk = 0
                    for G in range(4):
                        gp = G % 2
                        R = st_r[gp]
                        slots = []
                        for j in range(4):
                            i = 4 * G + j
                            sl = nload % NXS
                            nload += 1
                            slots.append(sl)
                            k.dma("sp", xt_ch[sl], xt[sl][:, :], x[b, i * 128:(i + 1) * 128, :],
                                  writes=[xt_r[sl]])
                            k.op("act", lambda e, sl=sl, j=j: e.activation(
                                out=junk[:, :], in_=xt[sl][:, :], func=AF.Square,
                                accum_out=st[:, gp, j:j + 1]),
                                reads=[xt_r[sl]], writes=[junk_r, R[0]])
                        k.op("act", lambda e: e.activation(
                            out=st[:, gp, 4:8], in_=st[:, gp, 0:4], func=AF.Sqrt,
                            bias=eps_t[:, 0:1], scale=1.0 / D), reads=[R[0]], writes=[R[1]])
                        k.op("dve", lambda e: e.reciprocal(out=st[:, gp, 8:12], in_=st[:, gp, 4:8]),
                             reads=[R[1]], writes=[R[2]])
                        for j in range(4):
                            sl = slots[j]
                            xs = j % 2
                            k.op("dve", lambda e, sl=sl, xs=xs, j=j: e.scalar_tensor_tensor(
                                out=xn[xs][:, :], in0=xt[sl][:, :], scalar=st[:, gp, 8 + j:9 + j],
                                in1=gmixf[:, :], op0=ALU.mult, op1=ALU.mult),
                                reads=[xt_r[sl], R[2], r_w], writes=[xn_r[xs]])
                            for c in range(8):
                                k.op("pe", lambda e, c=c, xs=xs: e.transpose(
                                    tp_ps[:, c, :], xn[xs][:, c * 128:(c + 1) * 128], ident[:]),
                                    reads=[xn_r[xs]], writes=[tp_r])
                            k.op("act", lambda e, j=j: e.copy(
                                out=hT[gp][:, :, j * 128:(j + 1) * 128], in_=tp_ps[:, :, :]),
                                reads=[tp_r], writes=[hT_r[gp][j]])
                        tc0 = G * 512
                        for eb in range(8):
                            qs = nqk % 2
                            nqk += 1
                            for c in range(8):
                                k.op("pe", lambda e, c=c, eb=eb, qs=qs: e.matmul(
                                    qk_ps[qs][:, :], lhsT=w_in_bf[:, c, eb * 128:(eb + 1) * 128],
                                    rhs=hT[gp][:, c, :], start=(c == 0), stop=(c == 7)),
                                    reads=hT_r[gp] + [r_w], writes=[qk_r[qs]])
                            if eb < 4:
                                k.op("dve", lambda e, eb=eb, qs=qs: e.tensor_scalar(
                                    out=QT[:, eb, tc0:tc0 + 512], in0=qk_ps[qs][:, :], scalar1=0.125,
                                    scalar2=None, op0=ALU.mult), reads=[qk_r[qs]], writes=[QT_r])
                            else:
                                k.op("act", lambda e, eb=eb, qs=qs: e.copy(
                                    out=KT[:, eb - 4, tc0:tc0 + 512], in_=qk_ps[qs][:, :]),
                                    reads=[qk_r[qs]], writes=[KT_r])
                        for j in range(4):
                            i = 4 * G + j
                            for (pst, pr, c0) in ((v_ps, v_r, 1024), (u_ps, u_r, 1536), (vg_ps, vg_r, 2048)):
                                for c in range(8):
                                    k.op("pe", lambda e, c=c, pst=pst, c0=c0, j=j: e.matmul(
                                        pst[:, :], lhsT=hT[gp][:, c, j * 128:(j + 1) * 128],
                                        rhs=w_in_bf[:, c, c0:c0 + 512], start=(c == 0), stop=(c == 7)),
                                        reads=[hT_r[gp][j], r_w], writes=[pr])
                            k.op("dve", lambda e, i=i: e.tensor_copy(out=Vsb[:, i, :], in_=v_ps[:, :]),
                                 reads=[v_r], writes=[V_r])
                            k.op("act", lambda e, j=j: e.activation(
                                out=gu[j][:, :], in_=u_ps[:, :], func=AF.Gelu_apprx_tanh),
                                reads=[u_r], writes=[gu_r[j]])
                            k.op("act", lambda e, j=j: e.activation(
                                out=gv[j][:, :], in_=vg_ps[:, :], func=AF.Gelu_apprx_tanh),
                                reads=[vg_r], writes=[gv_r[j]])
                            k.op("dve", lambda e, j=j: e.bn_stats(out=bst[:, j, :], in_=gv[j][:, :]),
                                 reads=[gv_r[j]], writes=[bst_r[j]])
                            k.op("dve", lambda e, j=j: e.bn_aggr(out=mv[:, j, :], in_=bst[:, j, :]),
                                 reads=[bst_r[j]], writes=[mv_r])
                        k.op("act", lambda e: e.activation(
                            out=st[:, gp, 12:16], in_=mv[:, :, 1], func=AF.Sqrt,
                            bias=eps_t[:, 0:1], scale=1.0), reads=[mv_r], writes=[R[3]])
                        k.op("dve", lambda e: e.reciprocal(out=st[:, gp, 16:20], in_=st[:, gp, 12:16]),
                             reads=[R[3]], writes=[R[4]])
                        for j in range(4):
                            vs = j % 2
                            k.op("dve", lambda e, j=j: e.scalar_tensor_tensor(
                                out=t1[:, :], in0=gv[j][:, :], scalar=mv[:, j, 0:1], in1=lng[:, :],
                                op0=ALU.subtract, op1=ALU.mult),
                                reads=[gv_r[j], mv_r, r_w], writes=[t1_r])
                            k.op("dve", lambda e, j=j, vs=vs: e.scalar_tensor_tensor(
                                out=vn[vs][:, :], in0=t1[:, :], scalar=st[:, gp, 16 + j:17 + j], in1=lnb[:, :],
                                op0=ALU.mult, op1=ALU.add),
                                reads=[t1_r, R[4], r_w], writes=[vn_r[vs]])
                            for g in range(8):
                                k.op("pe", lambda e, g=g, vs=vs: e.matmul(
                                    mix_ps[:, g * 64:(g + 1) * 64], lhsT=sgwT[:, g, :],
                                    rhs=vn[vs][:, g * 64:(g + 1) * 64], start=True, stop=True),
                                    reads=[vn_r[vs], r_w], writes=[mix_r])
                            k.op("dve", lambda e: e.tensor_tensor(
                                out=t2[:, :], in0=mix_ps[:, :], in1=biasf[:, :], op=ALU.add),
                                reads=[mix_r, r_w], writes=[t2_r])
                            k.op("pool", lambda e, j=j: e.tensor_tensor(
                                out=braw[j][:, :], in0=t2[:, :], in1=gu[j][:, :], op=ALU.mult),
                                reads=[t2_r, gu_r[j]], writes=[braw_r[j]])
                            k.op("act", lambda e, j=j: e.activation(
                                out=junk[:, 0:512], in_=braw[j][:, :], func=AF.Square,
                                accum_out=st[:, gp, 20 + j:21 + j]),
                                reads=[braw_r[j]], writes=[junk_r, R[5]])
                        k.op("act", lambda e: e.activation(
                            out=st[:, gp, 24:28], in_=st[:, gp, 20:24], func=AF.Sqrt,
                            bias=eps_t[:, 0:1], scale=1.0 / 512), reads=[R[5]], writes=[R[6]])
                        k.op("dve", lambda e: e.reciprocal(out=st[:, gp, 28:32], in_=st[:, gp, 24:28]),
                             reads=[R[6]], writes=[R[7]])
                        for j in range(4):
                            i = 4 * G + j
                            bs = j % 2
                            k.op("dve", lambda e, j=j, bs=bs: e.tensor_scalar(
                                out=bn[bs][:, :], in0=braw[j][:, :], scalar1=st[:, gp, 28 + j:29 + j],
                                scalar2=None, op0=ALU.mult),
                                reads=[braw_r[j], R[7]], writes=[bn_r[bs]])
                            for c in range(4):
                                k.op("pe", lambda e, c=c, bs=bs: e.transpose(
                                    bt_ps[:, c, :], bn[bs][:, c * 128:(c + 1) * 128], ident[:]),
                                    reads=[bn_r[bs]], writes=[bt_r])
                            k.op("dve", lambda e, i=i: e.tensor_copy(
                                out=bT[:, :, i * 128:(i + 1) * 128], in_=bt_ps[:, :, :]),
                                reads=[bt_r], writes=[bT_r])
                    k.barrier(scratch[:, 0:1])

                with ExitStack() as e2:
                    aT = sb(e2, "aT", [128, 4, S], BF16)
                    aT_r = Reg()
                    with ExitStack() as e3:
                        QTz = [[sb(e3, "QTz%d_%d" % (p, i), [128, 512], BF16) for i in range(2)]
                               for p in range(2)]
                        QTz_r = [[Reg() for _ in range(2)] for _ in range(2)]
                        NE = 3
                        ebuf = [sb(e3, "ebuf%d" % i, [128, 512], F32) for i in range(NE)]
                        ebuf_r = [Reg() for _ in range(NE)]
                        NSP = 6
                        spb = [sb(e3, "spb%d" % i, [128, 512], BF16) for i in range(NSP)]
                        spb_r = [Reg() for _ in range(NSP)]
                        Sb = [sb(e3, "Sb%d" % i, [128, 512], BF16) for i in range(2)]
                        Sb_r = [Reg() for _ in range(2)]
                        NA = 4
                        Ab = [sb(e3, "Ab%d" % i, [128, 512], BF16) for i in range(NA)]
                        Ab_r = [Reg() for _ in range(NA)]
                        sq = sb(e3, "sq", [128, 512], F32)
                        sq_r = Reg()
                        NF = 5
                        f_ps = [ps(e3, "f_ps%d" % i, [128, 512]) for i in range(NF)]
                        f_r = [Reg() for _ in range(NF)]
                        o_ps = [ps(e3, "o_ps%d" % i, [128, 512]) for i in range(2)]
                        o_r = [Reg() for _ in range(2)]
                        ssa_ps = ps(e3, "ssa_ps", [128, 16])
                        ssa_r = Reg()
                        ssa_sb_r = Reg()
                        r_att = Reg()
                        for p in range(2):
                            for i in range(2):
                                k.op("pool", lambda e, p=p, i=i: e.memset(QTz[p][i][:, :], 0.0),
                                     writes=[QTz_r[p][i]])

                        steps = []
                        nhead = 0
                        for qg in range(4):
                            for hp in range(4):
                                for hh in range(2):
                                    nk = 4 * qg + 4
                                    prev_c0 = None
                                    for si in range(nk):
                                        kb = nk - 1 - si
                                        j = kb - 4 * qg
                                        c0 = max(0, j) * 128
                                        steps.append(dict(qg=qg, hp=hp, hh=hh, kb=kb, first=(si == 0),
                                                          last=(si == nk - 1), j=j, c0=c0, pc0=prev_c0,
                                                          hidx=nhead))
                                        prev_c0 = c0
                                    nhead += 1
                        state = {"S": None}

                        def s1p(i):
                            s_ = steps[i]
                            hp, hh, kb, qg, c0 = s_["hp"], s_["hh"], s_["kb"], s_["qg"], s_["c0"]
                            pb = 64 * hh
                            qb_ = (s_["hidx"] // 2) % 2
                            qz, qzr = QTz[hh][qb_], QTz_r[hh][qb_]
                            if s_["first"]:
                                k.op("pool", lambda e: e.tensor_copy(
                                    out=qz[pb:pb + 64, :], in_=QT[pb:pb + 64, hp, qg * 512:(qg + 1) * 512]),
                                    reads=[r_att], writes=[qzr])
                            fb = i % NF
                            k.op("pe", lambda e: e.matmul(
                                f_ps[fb][:, c0:512], lhsT=KT[:, hp, kb * 128:(kb + 1) * 128], rhs=qz[:, c0:512],
                                start=True, stop=True), reads=[qzr, r_att], writes=[f_r[fb]])

                        def s1a(i):
                            s_ = steps[i]
                            c0 = s_["c0"]
                            fb = i % NF
                            eb = i % NE
                            k.op("act", lambda e: e.activation(
                                out=ebuf[eb][:, c0:512], in_=f_ps[fb][:, c0:512], func=AF.Exp),
                                reads=[f_r[fb]], writes=[ebuf_r[eb]])

                        def s1b(i):
                            s_ = steps[i]
                            c0 = s_["c0"]
                            eb = i % NE
                            sb_i = i % NSP
                            k.op("act", lambda e: e.activation(
                                out=spb[sb_i][:, c0:512], in_=ebuf[eb][:, c0:512], func=AF.Ln,
                                bias=one_t[:, 0:1], scale=1.0), reads=[ebuf_r[eb]], writes=[spb_r[sb_i]])
                            if s_["j"] >= 0:
                                k.op("pool", lambda e: e.tensor_tensor(
                                    out=spb[sb_i][:, c0:c0 + 128], in0=spb[sb_i][:, c0:c0 + 128],
                                    in1=masks[:, :], op=ALU.mult),
                                    reads=[spb_r[sb_i]], writes=[spb_r[sb_i]])

                        def s2p(i):
                            s_ = steps[i]
                            fb = i % NF
                            sb_i = i % NSP
                            first = s_["first"]
                            c0, pc0 = s_["c0"], s_["pc0"]
                            k.op("pe", lambda e: e.matmul(
                                f_ps[fb][:, c0:512], lhsT=ntri[:, :], rhs=spb[sb_i][:, c0:512],
                                start=False, stop=first, skip_group_check=True),
                                reads=[spb_r[sb_i]], writes=[f_r[fb]])
                            if not first:
                                S_t, S_rg = state["S"]
                                k.op("pe", lambda e: e.matmul(
                                    f_ps[fb][:, pc0:512], lhsT=nones[:, :], rhs=S_t[:, pc0:512],
                                    start=False, stop=True, skip_group_check=True),
                                    reads=[S_rg], writes=[f_r[fb]])
                            if not s_["last"]:
                                if first:
                                    state["S"] = (spb[sb_i], spb_r[sb_i])
                                    state["Sn"] = 0
                                else:
                                    S_t, S_rg = state["S"]
                                    nb = state["Sn"]
                                    state["Sn"] = 1 - nb
                                    k.op("dve", lambda e: e.tensor_tensor(
                                        out=Sb[nb][:, pc0:512], in0=S_t[:, pc0:512], in1=spb[sb_i][:, pc0:512],
                                        op=ALU.add), reads=[S_rg, spb_r[sb_i]], writes=[Sb_r[nb]])
                                    if c0 < pc0:
                                        k.op("dve", lambda e: e.tensor_copy(
                                            out=Sb[nb][:, c0:pc0], in_=spb[sb_i][:, c0:pc0]),
                                            reads=[spb_r[sb_i]], writes=[Sb_r[nb]])
                                    state["S"] = (Sb[nb], Sb_r[nb])

                        def s2a(i):
                            s_ = steps[i]
                            fb = i % NF
                            c0 = s_["c0"]
                            ab = i % NA
                            k.op("act", lambda e: e.activation(
                                out=Ab[ab][:, c0:512], in_=f_ps[fb][:, c0:512], func=AF.Exp),
                                reads=[f_r[fb]], writes=[Ab_r[ab]])
                            if s_["j"] >= 0:
                                k.op("pool", lambda e: e.tensor_tensor(
                                    out=Ab[ab][:, c0:c0 + 128], in0=Ab[ab][:, c0:c0 + 128],
                                    in1=masks[:, :], op=ALU.mult),
                                    reads=[Ab_r[ab]], writes=[Ab_r[ab]])

                        def s3(i):
                            s_ = steps[i]
                            hp, hh, kb, qg, c0 = s_["hp"], s_["hh"], s_["kb"], s_["qg"], s_["c0"]
                            pb = 64 * hh
                            ab = i % NA
                            ob = s_["hidx"] % 2
                            k.op("pe", lambda e: e.matmul(
                                o_ps[ob][:, c0:512], lhsT=Vsb[:, kb, hp * 128:(hp + 1) * 128], rhs=Ab[ab][:, c0:512],
                                start=s_["first"], stop=s_["last"], skip_group_check=True),
                                reads=[Ab_r[ab], r_att], writes=[o_r[ob]])
                            if s_["last"]:
                                cs = slice(qg * 512, (qg + 1) * 512)
                                k.op("dve", lambda e: e.tensor_copy(
                                    out=aT[pb:pb + 64, hp, cs], in_=o_ps[ob][pb:pb + 64, :]),
                                    reads=[o_r[ob]], writes=[aT_r])
                                k.op("pool", lambda e: e.tensor_tensor(
                                    out=sq[pb:pb + 64, :], in0=aT[pb:pb + 64, hp, cs], in1=aT[pb:pb + 64, hp, cs],
                                    op=ALU.mult), reads=[aT_r], writes=[sq_r])
                                if hh == 1:
                                    for jj in range(4):
                                        k.op("pe", lambda e, jj=jj: e.matmul(
                                            ssa_ps[:, hp * 4 + jj:hp * 4 + jj + 1],
                                            lhsT=sq[:, jj * 128:(jj + 1) * 128], rhs=ones32[:, 0:1],
                                            start=True, stop=True), reads=[sq_r, r_const], writes=[ssa_r])
                                    if hp == 3:
                                        k.op("dve", lambda e: e.tensor_reduce(
                                            out=ssa_sb[:, qg * 4:(qg + 1) * 4],
                                            in_=ssa_ps[:, 0:16].rearrange("p (h j) -> p j h", h=4),
                                            axis=AX.X, op=ALU.add), reads=[ssa_r], writes=[ssa_sb_r])

                        nst = len(steps)
                        s1p(0)
                        for it in range(nst + 3):
                            if it < nst:
                                s1a(it)
                            if 0 <= it - 1 < nst:
                                s2p(it - 1)
                            if 0 <= it - 3 < nst:
                                s3(it - 3)
                            if it + 1 < nst:
                                s1p(it + 1)
                            if 0 <= it - 2 < nst:
                                s2a(it - 2)
                            if it < nst:
                                s1b(it)
                        k.barrier(scratch[:, 0:1])

                    with ExitStack() as e4:
                        xr = [sb(e4, "xr%d" % i, [128, D], F32) for i in range(3)]
                        xr_r = [Reg() for _ in range(3)]
                        xr_ch = [k.chan() for _ in range(3)]
                        xo_ch = [k.chan() for _ in range(3)]
                        sda = sb(e4, "sda", [128, NT], F32)
                        rsa = sb(e4, "rsa", [128, NT], F32)
                        rsa_r = Reg()
                        acc = [[ps(e4, "acc%d_%d" % (i, h), [128, 512]) for h in range(4)] for i in range(2)]
                        acc_r = [[Reg() for _ in range(4)] for _ in range(2)]
                        r_all = Reg()
                        k.op("act", lambda e: e.activation(
                            out=sda[:, :], in_=ssa_sb[:, :], func=AF.Sqrt, bias=eps_t[:, 0:1],
                            scale=1.0 / 512), reads=[r_all], writes=[rsa_r])
                        k.op("dve", lambda e: e.reciprocal(out=rsa[:, :], in_=sda[:, :]),
                             reads=[rsa_r], writes=[rsa_r])
                        for i in range(NT):
                            sl = i % 3
                            ab = i % 2
                            k.dma("sp", xr_ch[sl], xr[sl][:, :], x[b, i * 128:(i + 1) * 128, :],
                                  writes=[xr_r[sl]])
                            for hf in range(2):
                                for c in range(4):
                                    k.op("pe", lambda e, c=c, hf=hf: e.matmul(
                                        acc[ab][hf][:, :], lhsT=aT[:, c, i * 128:(i + 1) * 128],
                                        rhs=w_out_bf[:, c, hf * 512:(hf + 1) * 512],
                                        start=(c == 0), stop=(c == 3)), reads=[r_all], writes=[acc_r[ab][hf]])
                                for c in range(4):
                                    k.op("pe", lambda e, c=c, hf=hf: e.matmul(
                                        acc[ab][2 + hf][:, :], lhsT=bT[:, c, i * 128:(i + 1) * 128],
                                        rhs=w_out_bf[:, 4 + c, hf * 512:(hf + 1) * 512],
                                        start=(c == 0), stop=(c == 3)), reads=[r_all], writes=[acc_r[ab][2 + hf]])
                            for hf in range(2):
                                cs = slice(hf * 512, (hf + 1) * 512)
                                k.op("dve", lambda e, hf=hf, cs=cs: e.scalar_tensor_tensor(
                                    out=xr[sl][:, cs], in0=acc[ab][hf][:, :], scalar=rsa[:, i:i + 1],
                                    in1=xr[sl][:, cs], op0=ALU.mult, op1=ALU.add),
                                    reads=[acc_r[ab][hf], rsa_r, xr_r[sl]], writes=[xr_r[sl]])
                                k.op("dve", lambda e, hf=hf, cs=cs: e.tensor_tensor(
                                    out=xr[sl][:, cs], in0=xr[sl][:, cs], in1=acc[ab][2 + hf][:, :], op=ALU.add),
                                    reads=[acc_r[ab][2 + hf], xr_r[sl]], writes=[xr_r[sl]])
                            k.dma("sp", xo_ch[sl], y[b, i * 128:(i + 1) * 128, :], xr[sl][:, :],
                                  reads=[xr_r[sl]], writes=[y_reg[b][i]])
                        k.barrier(scratch[:, 0:1])

        with ExitStack() as esB:
            w_up_bf = sb(esB, "w_up_bf", [128, 8, DFF], BF16)
            w_dn_bf = sb(esB, "w_dn_bf", [128, 32, D], BF16)
            gf = sb(esB, "gf", [128, D], F32)
            gmlpf = sb(esB, "gmlpf", [128, D], F32)
            r_wB = Reg()
            wch2 = k.chan()
            k.dma("sp", wch2, gf[:], gf_d[:, :], writes=[r_wB])
            k.dma("sp", wch2, gmlpf[:], gmlpf_d[:, :], writes=[r_wB])
            wup_ch = k.chan()
            wdn_ch = k.chan()
            r_wup1 = Reg()
            r_wdn1 = Reg()
            for c in range(8):
                k.dma("pool", wup_ch, w_up_bf[:, c, :], w_up[c * 128:(c + 1) * 128, :], writes=[r_wup1])
            for q in range(4):
                k.dma("pool", wdn_ch, w_dn_bf[:, q * 8:(q + 1) * 8, :],
                      w_down[q * 1024:(q + 1) * 1024, :].rearrange("(c p) d -> p c d", p=128),
                      writes=[r_wdn1])
            r_wup = [r_wup1] * 4
            r_wdn = [r_wdn1] * 4

            with ExitStack() as e5:
                NPS = 2
                xp = [sb(e5, "xp%d" % i, [128, D], F32) for i in range(NPS)]
                xp_r = [Reg() for _ in range(NPS)]
                xp_ch = [k.chan() for _ in range(NPS)]
                NES = 2
                xe = [sb(e5, "xe%d" % i, [128, D], F32) for i in range(NES)]
                xe_r = [Reg() for _ in range(NES)]
                xe_ch = [k.chan() for _ in range(NES)]
                xo_ch = [k.chan() for _ in range(NES)]
                xn = [sb(e5, "xnB%d" % i, [128, D], BF16) for i in range(2)]
                xn_r = [Reg() for _ in range(2)]
                h2T = sb(e5, "h2T", [128, 8, 512], BF16)
                h2_r = [Reg() for _ in range(4)]
                actT = sb(e5, "actT", [128, 32, 512], BF16)
                act_r = [Reg() for _ in range(32)]
                rr = [sb(e5, "rr%d" % i, [128, 512], F32) for i in range(2)]
                rr_r = [Reg() for _ in range(2)]
                NST = 4
                stp = sb(e5, "stp", [128, NST, 4], F32)
                stp_r = [Reg() for _ in range(NST)]
                ste = sb(e5, "ste", [128, NST, 4], F32)
                ste_r = [Reg() for _ in range(NST)]
                tp_ps = ps(e5, "tpB", [128, 8, 128], BF16)
                tp_r = Reg()
                up_ps = [ps(e5, "up_ps%d" % i, [128, 512]) for i in range(3)]
                up_r = [Reg() for _ in range(3)]
                dn_ps = [ps(e5, "dn_ps%d" % i, [128, 512]) for i in range(4)]
                dn_r = [Reg() for _ in range(4)]
                junkB = sb(e5, "junkB", [128, D], BF16)
                junkB_r = Reg()
                cnt = {"p": 0, "e": 0, "up": 0, "dn": 0}

                def prologue(blk):
                    b = blk // 4
                    G = blk % 4
                    for j in range(4):
                        i = 4 * G + j
                        n = cnt["p"]
                        cnt["p"] += 1
                        sl = n % NPS
                        xs = n % 2
                        q = n % NST
                        k.dma("sp", xp_ch[sl], xp[sl][:, :], y[b, i * 128:(i + 1) * 128, :],
                              reads=[y_reg[b][i]], writes=[xp_r[sl]])
                        k.op("act", lambda e, sl=sl, xs=xs, q=q: e.activation(
                            out=xn[xs][:, :], in_=xp[sl][:, :], func=AF.Square, accum_out=stp[:, q, 0:1]),
                            reads=[xp_r[sl]], writes=[xn_r[xs], stp_r[q]])
                        k.op("act", lambda e, q=q: e.activation(
                            out=stp[:, q, 1:2], in_=stp[:, q, 0:1], func=AF.Sqrt, bias=eps_t[:, 0:1],
                            scale=1.0 / D), reads=[stp_r[q]], writes=[stp_r[q]])
                        k.op("dve", lambda e, q=q: e.reciprocal(out=stp[:, q, 2:3], in_=stp[:, q, 1:2]),
                             reads=[stp_r[q]], writes=[stp_r[q]])
                        k.op("dve", lambda e, sl=sl, xs=xs, q=q: e.scalar_tensor_tensor(
                            out=xn[xs][:, :], in0=xp[sl][:, :], scalar=stp[:, q, 2:3], in1=gmlpf[:, :],
                            op0=ALU.mult, op1=ALU.mult), reads=[xp_r[sl], stp_r[q], r_wB], writes=[xn_r[xs]])
                        for c in range(8):
                            k.op("pe", lambda e, c=c, xs=xs: e.transpose(
                                tp_ps[:, c, :], xn[xs][:, c * 128:(c + 1) * 128], ident[:]),
                                reads=[xn_r[xs]], writes=[tp_r])
                        k.op("act", lambda e, j=j: e.copy(
                            out=h2T[:, :, j * 128:(j + 1) * 128], in_=tp_ps[:, :, :]),
                            reads=[tp_r], writes=[h2_r[j]])

                def up(blk):
                    for fc in range(32):
                        ub = cnt["up"] % 3
                        rb = cnt["up"] % 2
                        cnt["up"] += 1
                        for c in range(8):
                            k.op("pe", lambda e, c=c, fc=fc, ub=ub: e.matmul(
                                up_ps[ub][:, :], lhsT=w_up_bf[:, c, fc * 128:(fc + 1) * 128],
                                rhs=h2T[:, c, :], start=(c == 0), stop=(c == 7)),
                                reads=h2_r + [r_wup[fc // 8]], writes=[up_r[ub]])
                        k.op("act", lambda e, ub=ub, rb=rb: e.activation(
                            out=rr[rb][:, :], in_=up_ps[ub][:, :], func=AF.Relu),
                            reads=[up_r[ub]], writes=[rr_r[rb]])
                        k.op("pool", lambda e, fc=fc, rb=rb: e.tensor_tensor(
                            out=actT[:, fc, :], in0=rr[rb][:, :], in1=rr[rb][:, :], op=ALU.mult),
                            reads=[rr_r[rb]], writes=[act_r[fc]])

                def down_epi(blk):
                    b = blk // 4
                    G = blk % 4
                    for j in range(4):
                        i = 4 * G + j
                        n = cnt["e"]
                        cnt["e"] += 1
                        sl = n % NES
                        q = n % NST
                        k.dma("sp", xe_ch[sl], xe[sl][:, :], y[b, i * 128:(i + 1) * 128, :],
                              reads=[y_reg[b][i]], writes=[xe_r[sl]])
                        for hf in range(2):
                            db = cnt["dn"] % 4
                            cnt["dn"] += 1
                            cs = slice(hf * 512, (hf + 1) * 512)
                            for fc in range(32):
                                k.op("pe", lambda e, fc=fc, db=db, j=j, hf=hf: e.matmul(
                                    dn_ps[db][:, :], lhsT=actT[:, fc, j * 128:(j + 1) * 128],
                                    rhs=w_dn_bf[:, fc, hf * 512:(hf + 1) * 512],
                                    start=(fc == 0), stop=(fc == 31)),
                                    reads=[act_r[fc], r_wdn[fc // 8]], writes=[dn_r[db]])
                            k.op("dve", lambda e, db=db, sl=sl, cs=cs: e.tensor_tensor(
                                out=xe[sl][:, cs], in0=xe[sl][:, cs], in1=dn_ps[db][:, :], op=ALU.add),
                                reads=[dn_r[db], xe_r[sl]], writes=[xe_r[sl]])
                        xs = n % 2
                        k.op("act", lambda e, sl=sl, q=q: e.activation(
                            out=junkB[:, :], in_=xe[sl][:, :], func=AF.Square,
                            accum_out=ste[:, q, 0:1]),
                            reads=[xe_r[sl]], writes=[junkB_r, ste_r[q]])
                        k.op("act", lambda e, q=q: e.activation(
                            out=ste[:, q, 1:2], in_=ste[:, q, 0:1], func=AF.Sqrt, bias=eps_t[:, 0:1],
                            scale=1.0 / D), reads=[ste_r[q]], writes=[ste_r[q]])
                        k.op("dve", lambda e, q=q: e.reciprocal(out=ste[:, q, 2:3], in_=ste[:, q, 1:2]),
                             reads=[ste_r[q]], writes=[ste_r[q]])
                        k.op("dve", lambda e, sl=sl, q=q: e.scalar_tensor_tensor(
                            out=xe[sl][:, :], in0=xe[sl][:, :], scalar=ste[:, q, 2:3], in1=gf[:, :],
                            op0=ALU.mult, op1=ALU.mult), reads=[xe_r[sl], ste_r[q], r_wB], writes=[xe_r[sl]])
                        k.dma("sp", xo_ch[sl], y[b, i * 128:(i + 1) * 128, :], xe[sl][:, :],
                              reads=[xe_r[sl]], writes=[y_reg[b][i]])

                NBLK = BPC * 4
                prologue(0)
                for blk in range(NBLK):
                    up(blk)
                    if blk + 1 < NBLK:
                        prologue(blk + 1)
                    down_epi(blk)
                k.barrier(scratch[:, 0:1])
    return nc


def _host_consts():
    ident = np.eye(128, dtype=np.float32)
    jj = np.arange(128)[:, None]
    ss = np.arange(128)[None, :]
    ntri = -(jj >= ss).astype(np.float32)
    nones = -np.ones((128, 128), np.float32)
    s_idx = np.arange(128)[:, None]
    t_idx = np.arange(128)[None, :]
    masks = (s_idx < t_idx).astype(np.float32)
    return ident, ntri, nones, masks


def kernel(x, norm_mix_g, w_in, sg_ln_g, sg_ln_b, sg_w, sg_b, out_norm_g, w_out,
           norm_mlp_g, w_up, w_down, norm_final_g):
    f = np.float32
    x = np.ascontiguousarray(np.asarray(x, dtype=f))
    ident, ntri, nones, masks = _host_consts()

    def pc(v):
        return np.ascontiguousarray(np.asarray(v, dtype=f).reshape(8, 128).T)

    shared = {
        "w_in": np.ascontiguousarray(np.asarray(w_in, dtype=f)[0]),
        "w_out": np.ascontiguousarray(np.asarray(w_out, dtype=f)[0]),
        "w_up": np.ascontiguousarray(np.asarray(w_up, dtype=f)[0]),
        "w_down": np.ascontiguousarray(np.asarray(w_down, dtype=f)[0]),
        "g_mix": pc(norm_mix_g[0]),
        "g_out": pc(out_norm_g[0]),
        "g_mlp": pc(norm_mlp_g[0]),
        "lng_full": np.ascontiguousarray(np.broadcast_to(np.asarray(sg_ln_g, dtype=f)[0][None, :], (128, 512))),
        "lnb_full": np.ascontiguousarray(np.broadcast_to(np.asarray(sg_ln_b, dtype=f)[0][None, :], (128, 512))),
        "bias_full": np.ascontiguousarray(
            np.broadcast_to(np.asarray(sg_b, dtype=f)[0].T[:, :, None], (128, 8, 64)).reshape(128, 512)),
        "gmix_full": np.ascontiguousarray(np.broadcast_to(np.asarray(norm_mix_g, dtype=f)[0][None, :], (128, D))),
        "gmlp_full": np.ascontiguousarray(np.broadcast_to(np.asarray(norm_mlp_g, dtype=f)[0][None, :], (128, D))),
        "gf_full": np.ascontiguousarray(np.broadcast_to(np.asarray(norm_final_g, dtype=f)[None, :], (128, D))),
        "sgwT": np.ascontiguousarray(np.transpose(np.asarray(sg_w, dtype=f)[0], (2, 0, 1))),
        "ident": ident, "ntri": ntri, "nones": nones, "masks": masks,
    }
    nc = build()
    in_maps = []
    for c in range(NCORES):
        m = dict(shared)
        m["x"] = np.ascontiguousarray(x[c * BPC:(c + 1) * BPC])
        in_maps.append(m)
    res = run_bass_kernel_spmd(nc, in_maps, core_ids=list(range(NCORES)))
    out = np.concatenate([np.asarray(r["y"]) for r in res.results], axis=0)
    return out.astype(np.float32)
```

```python
import numpy as np
import ml_dtypes
from contextlib import ExitStack

import concourse.bass as bass
import concourse.mybir as mybir
from concourse.bass_utils import run_bass_kernel_spmd

F32 = mybir.dt.float32
BF16 = mybir.dt.bfloat16
AF = mybir.ActivationFunctionType
ALU = mybir.AluOpType
AX = mybir.AxisListType

NCORES = 8
BPC = 2
S = 2048
D = 1024
NT = S // 128
INW = 2560
DFF = 4096
EPS = 1e-6


class Tk:
    __slots__ = ("sem", "val")

    def __init__(self, sem, val):
        self.sem = sem
        self.val = val


class Reg:
    __slots__ = ("w", "r", "name")

    def __init__(self, name=""):
        self.w = None
        self.r = {}
        self.name = name


class Chan:
    def __init__(self, sem):
        self.sem = sem
        self.n = 0


class Eng:
    def __init__(self, eng, sem, name):
        self.eng = eng
        self.sem = sem
        self.n = 0
        self.seen = {}
        self.name = name

    def wait(self, tk):
        if tk is None:
            return
        k = id(tk.sem)
        if self.seen.get(k, 0) >= tk.val:
            return
        self.eng.wait_ge(tk.sem, tk.val)
        self.seen[k] = tk.val


class K:
    def __init__(self, nc, es):
        self.nc = nc
        self.es = es
        self.chans = []
        self.E = {}
        for nm, eng in (("pe", nc.tensor), ("act", nc.scalar), ("dve", nc.vector),
                        ("pool", nc.gpsimd), ("sp", nc.sync)):
            sem = es.enter_context(nc.semaphore("sem_" + nm))
            self.E[nm] = Eng(eng, sem, nm)
        self._nm = 0

    def chan(self):
        self._nm += 1
        c = Chan(self.es.enter_context(self.nc.semaphore("ch%d" % self._nm)))
        self.chans.append(c)
        return c

    def _deps(self, E, reads, writes):
        for r in reads:
            if r.w is not None:
                E.wait(r.w)
        pe = E.name == "pe"
        for w in writes:
            if w.w is not None and not (pe and w.w.sem is E.sem):
                E.wait(w.w)
            for tk in w.r.values():
                if not (pe and tk.sem is E.sem):
                    E.wait(tk)

    def _mark(self, tk, reads, writes):
        for r in reads:
            r.r[id(tk.sem)] = tk
        for w in writes:
            w.w = tk
            w.r = {}

    def op(self, e, fn, reads=(), writes=()):
        E = self.E[e]
        self._deps(E, reads, writes)
        inst = fn(E.eng)
        E.n += 1
        inst.then_inc(E.sem, 1)
        tk = Tk(E.sem, E.n)
        self._mark(tk, reads, writes)
        return tk

    def dma(self, q, ch, out, in_, reads=(), writes=()):
        E = self.E[q]
        self._deps(E, reads, writes)
        inst = E.eng.dma_start(out=out, in_=in_)
        ch.n += 1
        inst.then_inc(ch.sem, 16)
        tk = Tk(ch.sem, 16 * ch.n)
        self._mark(tk, reads, writes)
        return tk

    def barrier(self, scratch_ap):
        V = self.E["dve"]
        for nm, E in self.E.items():
            if E is not V and E.n > 0:
                V.wait(Tk(E.sem, E.n))
        for c in self.chans:
            if c.n > 0:
                V.wait(Tk(c.sem, 16 * c.n))
        if V.n > 0:
            V.wait(Tk(V.sem, V.n))
        inst = V.eng.memset(scratch_ap, 0.0)
        V.n += 1
        inst.then_inc(V.sem, 1)
        tk = Tk(V.sem, V.n)
        for nm, E in self.E.items():
            E.wait(tk)


def build():
    nc = bass.Bass("TRN2", target_bir_lowering=False)

    def din(name, shape):
        return nc.dram_tensor(name, list(shape), F32, kind="ExternalInput").ap()

    x = din("x", [BPC, S, D])
    w_in = din("w_in", [D, INW])
    w_out = din("w_out", [D, D])
    w_up = din("w_up", [D, DFF])
    w_down = din("w_down", [DFF, D])
    g_mix = din("g_mix", [128, 8])
    g_out = din("g_out", [128, 8])
    g_mlp = din("g_mlp", [128, 8])
    lng_d = din("lng_full", [128, 512])
    lnb_d = din("lnb_full", [128, 512])
    bias_d = din("bias_full", [128, 512])
    gf_d = din("gf_full", [128, D])
    gmixf_d = din("gmix_full", [128, D])
    gmlpf_d = din("gmlp_full", [128, D])
    sgwT_d = din("sgwT", [128, 8, 128])
    ident_d = din("ident", [128, 128])
    ntri_d = din("ntri", [128, 128])
    nones_d = din("nones", [128, 128])
    masks_d = din("masks", [128, 128])
    y = nc.dram_tensor("y", [BPC, S, D], F32, kind="ExternalOutput").ap()

    with ExitStack() as es:
        k = K(nc, es)

        uid = [0]

        def sb(es_, name, shape, dt):
            uid[0] += 1
            return es_.enter_context(nc.sbuf_tensor("s%d_%s" % (uid[0], name), list(shape), dt))

        def ps(es_, name, shape, dt=F32):
            uid[0] += 1
            return es_.enter_context(nc.psum_tensor("p%d_%s" % (uid[0], name), list(shape), dt))

        ident = sb(es, "ident", [128, 128], BF16)
        ntri = sb(es, "ntri", [128, 128], BF16)
        nones = sb(es, "nones", [128, 128], BF16)
        masks = sb(es, "masks", [128, 128], BF16)
        ones32 = sb(es, "ones32", [128, 1], F32)
        onesb = sb(es, "onesb", [128, 1], BF16)
        eps_t = sb(es, "eps_t", [128, 1], F32)
        one_t = sb(es, "one_t", [128, 1], F32)
        scratch = sb(es, "scratch", [128, 8], F32)
        gmix = sb(es, "gmix", [128, 8], F32)
        gout = sb(es, "gout", [128, 8], F32)
        gmlp = sb(es, "gmlp", [128, 8], F32)
        r_const = Reg("const")

        cch = k.chan()
        cch2 = k.chan()
        for dst, src in ((ident, ident_d), (ntri, ntri_d), (nones, nones_d)):
            k.dma("pool", cch, dst[:], src[:, :], writes=[r_const])
        k.dma("pool", cch, masks[:], masks_d[:, :], writes=[r_const])
        for dst, src in ((gmix, g_mix), (gout, g_out), (gmlp, g_mlp)):
            k.dma("sp", cch2, dst[:], src[:, :], writes=[r_const])
        k.op("dve", lambda e: e.memset(ones32[:], 1.0), writes=[r_const])
        k.op("dve", lambda e: e.memset(onesb[:], 1.0), writes=[r_const])
        k.op("dve", lambda e: e.memset(eps_t[:], EPS), writes=[r_const])
        k.op("dve", lambda e: e.memset(one_t[:], 1.0), writes=[r_const])
        k.barrier(scratch[:, 0:1])

        y_reg = [[Reg("y%d_%d" % (b, i)) for i in range(NT)] for b in range(BPC)]

        with ExitStack() as esA:
            w_in_bf = sb(esA, "w_in_bf", [128, 8, INW], BF16)
            w_out_bf = sb(esA, "w_out_bf", [128, 8, D], BF16)
            sgwT = sb(esA, "sgwT", [128, 8, 128], BF16)
            lng = sb(esA, "lng", [128, 512], F32)
            lnb = sb(esA, "lnb", [128, 512], F32)
            biasf = sb(esA, "biasf", [128, 512], F32)
            gmixf = sb(esA, "gmixf", [128, D], F32)
            QT = sb(esA, "QT", [128, 4, S], BF16)
            KT = sb(esA, "KT", [128, 4, S], BF16)
            Vsb = sb(esA, "Vsb", [128, NT, 512], BF16)
            bT = sb(esA, "bT", [128, 4, S], BF16)
            ssa_sb = sb(esA, "ssa_sb", [128, NT], F32)
            r_w = Reg("wA")

            with ExitStack() as esP:
                stg = [sb(esP, "stg%d" % i, [128, D], F32) for i in range(2)]
                stg_r = [Reg("stg%d" % i) for i in range(2)]
                stg_ch = [k.chan() for _ in range(2)]
                pch = k.chan()
                r_sg = Reg("sgw")
                sgch = k.chan()
                wich = k.chan()
                for c in range(8):
                    k.dma("pool", wich, w_in_bf[:, c, :], w_in[c * 128:(c + 1) * 128, :], writes=[r_w])
                k.dma("pool", sgch, sgwT[:], sgwT_d[:, :, :], writes=[r_sg])
                k.dma("sp", pch, lng[:], lng_d[:, :], writes=[r_w])
                k.dma("sp", pch, lnb[:], lnb_d[:, :], writes=[r_w])
                k.dma("sp", pch, biasf[:], bias_d[:, :], writes=[r_w])
                k.dma("sp", pch, gmixf[:], gmixf_d[:, :], writes=[r_w])
                for c in range(8):
                    sl = c % 2
                    k.dma("sp", stg_ch[sl], stg[sl][:, :], w_out[c * 128:(c + 1) * 128, :],
                          writes=[stg_r[sl]])
                    if c % 2 == 0:
                        k.op("dve", lambda e, c=c, sl=sl: e.tensor_scalar(
                            out=w_out_bf[:, c, :], in0=stg[sl][:, :], scalar1=gout[:, c:c + 1],
                            scalar2=None, op0=ALU.mult), reads=[stg_r[sl], r_const], writes=[r_w])
                    else:
                        k.op("act", lambda e, c=c, sl=sl: e.activation(
                            out=w_out_bf[:, c, :], in_=stg[sl][:, :], func=AF.Copy,
                            scale=gout[:, c:c + 1]), reads=[stg_r[sl], r_const], writes=[r_w])
                k.op("pool", lambda e: e.memset(sgwT[64:128, :, 0:64], 0.0), reads=[r_sg], writes=[r_sg])
                k.barrier(scratch[:, 0:1])

            for b in range(BPC):
                with ExitStack() as e1:
                    NXS = 5
                    xt = [sb(e1, "xt%d" % i, [128, D], F32) for i in range(NXS)]
                    xt_r = [Reg() for _ in range(NXS)]
                    xt_ch = [k.chan() for _ in range(NXS)]
                    junk = sb(e1, "junk", [128, D], BF16)
                    junk_r = Reg()
                    xn = [sb(e1, "xn%d" % i, [128, D], BF16) for i in range(2)]
                    xn_r = [Reg() for _ in range(2)]
                    hT = [sb(e1, "hT%d" % i, [128, 8, 512], BF16) for i in range(2)]
                    hT_r = [[Reg() for _ in range(4)] for _ in range(2)]
                    gu = [sb(e1, "gu%d" % i, [128, 512], F32) for i in range(4)]
                    gv = [sb(e1, "gv%d" % i, [128, 512], F32) for i in range(4)]
                    gu_r = [Reg() for _ in range(4)]
                    gv_r = [Reg() for _ in range(4)]
                    braw = gv
                    braw_r = gv_r
                    t1 = sb(e1, "t1", [128, 512], F32)
                    t1_r = Reg()
                    t2 = sb(e1, "t2", [128, 512], F32)
                    t2_r = Reg()
                    vn = [sb(e1, "vn%d" % i, [128, 512], BF16) for i in range(2)]
                    vn_r = [Reg() for _ in range(2)]
                    bn = [sb(e1, "bn%d" % i, [128, 512], BF16) for i in range(2)]
                    bn_r = [Reg() for _ in range(2)]
                    st = sb(e1, "st", [128, 2, 32], F32)
                    st_r = [[Reg() for _ in range(8)] for _ in range(2)]
                    bst = sb(e1, "bst", [128, 4, 6], F32)
                    bst_r = [Reg() for _ in range(4)]
                    mv = sb(e1, "mv", [128, 4, 2], F32)
                    mv_r = Reg()
                    tp_ps = ps(e1, "tp_ps", [128, 8, 128], BF16)
                    tp_r = Reg()
                    qk_ps = [ps(e1, "qk_ps%d" % i, [128, 512]) for i in range(2)]
                    qk_r = [Reg() for _ in range(2)]
                    v_ps = ps(e1, "v_ps", [128, 512])
                    v_r = Reg()
                    u_ps = ps(e1, "u_ps", [128, 512])
                    u_r = Reg()
                    vg_ps = ps(e1, "vg_ps", [128, 512])
                    vg_r = Reg()
                    mix_ps = ps(e1, "mix_ps", [128, 512])
                    mix_r = Reg()
                    bt_ps = ps(e1, "bt_ps", [128, 4, 128], BF16)
                    bt_r = Reg()
                    QT_r = Reg()
                    KT_r = Reg()
                    V_r = Reg()
                    bT_r = Reg()

                    nload = 0
                    nqk = 0
                    for G in range(4):
                        gp = G % 2
                        R = st_r[gp]
                        slots = []
                        for j in range(4):
                            i = 4 * G + j
                            sl = nload % NXS
                            nload += 1
                            slots.append(sl)
                            k.dma("sp", xt_ch[sl], xt[sl][:, :], x[b, i * 128:(i + 1) * 128, :],
                                  writes=[xt_r[sl]])
                            k.op("act", lambda e, sl=sl, j=j: e.activation(
                                out=junk[:, :], in_=xt[sl][:, :], func=AF.Square,
                                accum_out=st[:, gp, j:j + 1]),
                                reads=[xt_r[sl]], writes=[junk_r, R[0]])
                        k.op("act", lambda e: e.activation(
                            out=st[:, gp, 4:8], in_=st[:, gp, 0:4], func=AF.Sqrt,
                            bias=eps_t[:, 0:1], scale=1.0 / D), reads=[R[0]], writes=[R[1]])
                        k.op("dve", lambda e: e.reciprocal(out=st[:, gp, 8:12], in_=st[:, gp, 4:8]),
                             reads=[R[1]], writes=[R[2]])
                        for j in range(4):
                            sl = slots[j]
                            xs = j % 2
                            k.op("dve", lambda e, sl=sl, xs=xs, j=j: e.scalar_tensor_tensor(
                                out=xn[xs][:, :], in0=xt[sl][:, :], scalar=st[:, gp, 8 + j:9 + j],
                                in1=gmixf[:, :], op0=ALU.mult, op1=ALU.mult),
                                reads=[xt_r[sl], R[2], r_w], writes=[xn_r[xs]])
                            for c in range(8):
                                k.op("pe", lambda e, c=c, xs=xs: e.transpose(
                                    tp_ps[:, c, :], xn[xs][:, c * 128:(c + 1) * 128], ident[:]),
                                    reads=[xn_r[xs]], writes=[tp_r])
                            k.op("act", lambda e, j=j: e.copy(
                                out=hT[gp][:, :, j * 128:(j + 1) * 128], in_=tp_ps[:, :, :]),
                                reads=[tp_r], writes=[hT_r[gp][j]])
                        tc0 = G * 512
                        for eb in range(8):
                            qs = nqk % 2
                            nqk += 1
                            for c in range(8):
                                k.op("pe", lambda e, c=c, eb=eb, qs=qs: e.matmul(
                                    qk_ps[qs][:, :], lhsT=w_in_bf[:, c, eb * 128:(eb + 1) * 128],
                                    rhs=hT[gp][:, c, :], start=(c == 0), stop=(c == 7)),
                                    reads=hT_r[gp] + [r_w], writes=[qk_r[qs]])
                            if eb < 4:
                                k.op("dve", lambda e, eb=eb, qs=qs: e.tensor_scalar(
                                    out=QT[:, eb, tc0:tc0 + 512], in0=qk_ps[qs][:, :], scalar1=0.125,
                                    scalar2=None, op0=ALU.mult), reads=[qk_r[qs]], writes=[QT_r])
                            else:
                                k.op("act", lambda e, eb=eb, qs=qs: e.copy(
                                    out=KT[:, eb - 4, tc0:tc0 + 512], in_=qk_ps[qs][:, :]),
                                    reads=[qk_r[qs]], writes=[KT_r])
                        for j in range(4):
                            i = 4 * G + j
                            for (pst, pr, c0) in ((v_ps, v_r, 1024), (u_ps, u_r, 1536), (vg_ps, vg_r, 2048)):
                                for c in range(8):
                                    k.op("pe", lambda e, c=c, pst=pst, c0=c0, j=j: e.matmul(
                                        pst[:, :], lhsT=hT[gp][:, c, j * 128:(j + 1) * 128],
                                        rhs=w_in_bf[:, c, c0:c0 + 512], start=(c == 0), stop=(c == 7)),
                                        reads=[hT_r[gp][j], r_w], writes=[pr])
                            k.op("dve", lambda e, i=i: e.tensor_copy(out=Vsb[:, i, :], in_=v_ps[:, :]),
                                 reads=[v_r], writes=[V_r])
                            k.op("act", lambda e, j=j: e.activation(
                                out=gu[j][:, :], in_=u_ps[:, :], func=AF.Gelu_apprx_tanh),
                                reads=[u_r], writes=[gu_r[j]])
                            k.op("act", lambda e, j=j: e.activation(
                                out=gv[j][:, :], in_=vg_ps[:, :], func=AF.Gelu_apprx_tanh),
                                reads=[vg_r], writes=[gv_r[j]])
                            k.op("dve", lambda e, j=j: e.bn_stats(out=bst[:, j, :], in_=gv[j][:, :]),
                                 reads=[gv_r[j]], writes=[bst_r[j]])
                            k.op("dve", lambda e, j=j: e.bn_aggr(out=mv[:, j, :], in_=bst[:, j, :]),
                                 reads=[bst_r[j]], writes=[mv_r])
                        k.op("act", lambda e: e.activation(
                            out=st[:, gp, 12:16], in_=mv[:, :, 1], func=AF.Sqrt,
                            bias=eps_t[:, 0:1], scale=1.0), reads=[mv_r], writes=[R[3]])
                        k.op("dve", lambda e: e.reciprocal(out=st[:, gp, 16:20], in_=st[:, gp, 12:16]),
                             reads=[R[3]], writes=[R[4]])
                        for j in range(4):
                            vs = j % 2
                            k.op("dve", lambda e, j=j: e.scalar_tensor_tensor(
                                out=t1[:, :], in0=gv[j][:, :], scalar=mv[:, j, 0:1], in1=lng[:, :],
                                op0=ALU.subtract, op1=ALU.mult),
                                reads=[gv_r[j], mv_r, r_w], writes=[t1_r])
                            k.op("dve", lambda e, j=j, vs=vs: e.scalar_tensor_tensor(
                                out=vn[vs][:, :], in0=t1[:, :], scalar=st[:, gp, 16 + j:17 + j], in1=lnb[:, :],
                                op0=ALU.mult, op1=ALU.add),
                                reads=[t1_r, R[4], r_w], writes=[vn_r[vs]])
                            for g in range(8):
                                k.op("pe", lambda e, g=g, vs=vs: e.matmul(
                                    mix_ps[:, g * 64:(g + 1) * 64], lhsT=sgwT[:, g, :],
                                    rhs=vn[vs][:, g * 64:(g + 1) * 64], start=True, stop=True),
                                    reads=[vn_r[vs], r_w], writes=[mix_r])
                            k.op("dve", lambda e: e.tensor_tensor(
                                out=t2[:, :], in0=mix_ps[:, :], in1=biasf[:, :], op=ALU.add),
                                reads=[mix_r, r_w], writes=[t2_r])
                            k.op("pool", lambda e, j=j: e.tensor_tensor(
                                out=braw[j][:, :], in0=t2[:, :], in1=gu[j][:, :], op=ALU.mult),
                                reads=[t2_r, gu_r[j]], writes=[braw_r[j]])
                            k.op("act", lambda e, j=j: e.activation(
                                out=junk[:, 0:512], in_=braw[j][:, :], func=AF.Square,
                                accum_out=st[:, gp, 20 + j:21 + j]),
                                reads=[braw_r[j]], writes=[junk_r, R[5]])
                        k.op("act", lambda e: e.activation(
                            out=st[:, gp, 24:28], in_=st[:, gp, 20:24], func=AF.Sqrt,
                            bias=eps_t[:, 0:1], scale=1.0 / 512), reads=[R[5]], writes=[R[6]])
                        k.op("dve", lambda e: e.reciprocal(out=st[:, gp, 28:32], in_=st[:, gp, 24:28]),
                             reads=[R[6]], writes=[R[7]])
                        for j in range(4):
                            i = 4 * G + j
                            bs = j % 2
                            k.op("dve", lambda e, j=j, bs=bs: e.tensor_scalar(
                                out=bn[bs][:, :], in0=braw[j][:, :], scalar1=st[:, gp, 28 + j:29 + j],
                                scalar2=None, op0=ALU.mult),
                                reads=[braw_r[j], R[7]], writes=[bn_r[bs]])
                            for c in range(4):
                                k.op("pe", lambda e, c=c, bs=bs: e.transpose(
                                    bt_ps[:, c, :], bn[bs][:, c * 128:(c + 1) * 128], ident[:]),
                                    reads=[bn_r[bs]], writes=[bt_r])
                            k.op("dve", lambda e, i=i: e.tensor_copy(
                                out=bT[:, :, i * 128:(i + 1) * 128], in_=bt_ps[:, :, :]),
                                reads=[bt_r], writes=[bT_r])
                    k.barrier(scratch[:, 0:1])

                with ExitStack() as e2:
                    aT = sb(e2, "aT", [128, 4, S], BF16)
                    aT_r = Reg()
                    with ExitStack() as e3:
                        QTz = [[sb(e3, "QTz%d_%d" % (p, i), [128, 512], BF16) for i in range(2)]
                               for p in range(2)]
                        QTz_r = [[Reg() for _ in range(2)] for _ in range(2)]
                        NE = 3
                        ebuf = [sb(e3, "ebuf%d" % i, [128, 512], F32) for i in range(NE)]
                        ebuf_r = [Reg() for _ in range(NE)]
                        NSP = 6
                        spb = [sb(e3, "spb%d" % i, [128, 512], BF16) for i in range(NSP)]
                        spb_r = [Reg() for _ in range(NSP)]
                        Sb = [sb(e3, "Sb%d" % i, [128, 512], BF16) for i in range(2)]
                        Sb_r = [Reg() for _ in range(2)]
                        NA = 4
                        Ab = [sb(e3, "Ab%d" % i, [128, 512], BF16) for i in range(NA)]
                        Ab_r = [Reg() for _ in range(NA)]
                        sq = [sb(e3, "sq%d" % i, [128, 512], BF16) for i in range(2)]
                        sq_r = [Reg() for _ in range(2)]
                        deferred = []
                        NF = 5
                        f_ps = [ps(e3, "f_ps%d" % i, [128, 512]) for i in range(NF)]
                        f_r = [Reg() for _ in range(NF)]
                        o_ps = [ps(e3, "o_ps%d" % i, [128, 512]) for i in range(2)]
                        o_r = [Reg() for _ in range(2)]
                        ssa_ps = ps(e3, "ssa_ps", [128, 16])
                        ssa_r = Reg()
                        ssa_sb_r = Reg()
                        r_att = Reg()
                        for p in range(2):
                            for i in range(2):
                                k.op("dve", lambda e, p=p, i=i: e.memset(QTz[p][i][:, :], 0.0),
                                     writes=[QTz_r[p][i]])

                        steps = []
                        nhead = 0
                        for qg in range(4):
                            for hp in range(4):
                                for hh in range(2):
                                    nk = 4 * qg + 4
                                    prev_c0 = None
                                    for si in range(nk):
                                        kb = nk - 1 - si
                                        j = kb - 4 * qg
                                        c0 = max(0, j) * 128
                                        steps.append(dict(qg=qg, hp=hp, hh=hh, kb=kb, first=(si == 0),
                                                          last=(si == nk - 1), j=j, c0=c0, pc0=prev_c0,
                                                          hidx=nhead))
                                        prev_c0 = c0
                                    nhead += 1
                        state = {"S": None}

                        head_first = [i for i, s_ in enumerate(steps) if s_["first"]]

                        def qcopy(hn):
                            if hn >= len(head_first):
                                return
                            s_ = steps[head_first[hn]]
                            hp, hh, qg = s_["hp"], s_["hh"], s_["qg"]
                            pb = 64 * hh
                            qb_ = (s_["hidx"] // 2) % 2
                            qz, qzr = QTz[hh][qb_], QTz_r[hh][qb_]
                            k.op("dve", lambda e: e.tensor_copy(
                                out=qz[pb:pb + 64, :], in_=QT[pb:pb + 64, hp, qg * 512:(qg + 1) * 512]),
                                reads=[r_att], writes=[qzr])

                        def s1p(i):
                            s_ = steps[i]
                            hp, hh, kb, qg, c0 = s_["hp"], s_["hh"], s_["kb"], s_["qg"], s_["c0"]
                            qb_ = (s_["hidx"] // 2) % 2
                            qz, qzr = QTz[hh][qb_], QTz_r[hh][qb_]
                            if s_["first"]:
                                qcopy(s_["hidx"] + 1)
                            fb = i % NF
                            k.op("pe", lambda e: e.matmul(
                                f_ps[fb][:, c0:512], lhsT=KT[:, hp, kb * 128:(kb + 1) * 128], rhs=qz[:, c0:512],
                                start=True, stop=True), reads=[qzr, r_att], writes=[f_r[fb]])

                        def s1a(i):
                            s_ = steps[i]
                            c0 = s_["c0"]
                            fb = i % NF
                            eb = i % NE
                            k.op("act", lambda e: e.activation(
                                out=ebuf[eb][:, c0:512], in_=f_ps[fb][:, c0:512], func=AF.Exp),
                                reads=[f_r[fb]], writes=[ebuf_r[eb]])

                        def s1b(i):
                            s_ = steps[i]
                            c0 = s_["c0"]
                            eb = i % NE
                            sb_i = i % NSP
                            k.op("act", lambda e: e.activation(
                                out=spb[sb_i][:, c0:512], in_=ebuf[eb][:, c0:512], func=AF.Ln,
                                bias=one_t[:, 0:1], scale=1.0), reads=[ebuf_r[eb]], writes=[spb_r[sb_i]])
                            if s_["j"] >= 0:
                                k.op("dve", lambda e: e.tensor_tensor(
                                    out=spb[sb_i][:, c0:c0 + 128], in0=spb[sb_i][:, c0:c0 + 128],
                                    in1=masks[:, :], op=ALU.mult),
                                    reads=[spb_r[sb_i]], writes=[spb_r[sb_i]])

                        def s2p(i):
                            s_ = steps[i]
                            fb = i % NF
                            sb_i = i % NSP
                            first = s_["first"]
                            c0, pc0 = s_["c0"], s_["pc0"]
                            k.op("pe", lambda e: e.matmul(
                                f_ps[fb][:, c0:512], lhsT=ntri[:, :], rhs=spb[sb_i][:, c0:512],
                                start=False, stop=first, skip_group_check=True),
                                reads=[spb_r[sb_i]], writes=[f_r[fb]])
                            if not first:
                                S_t, S_rg = state["S"]
                                k.op("pe", lambda e: e.matmul(
                                    f_ps[fb][:, pc0:512], lhsT=nones[:, :], rhs=S_t[:, pc0:512],
                                    start=False, stop=True, skip_group_check=True),
                                    reads=[S_rg], writes=[f_r[fb]])
                            if not s_["last"]:
                                if first:
                                    state["S"] = (spb[sb_i], spb_r[sb_i])
                                    state["Sn"] = 0
                                else:
                                    S_t, S_rg = state["S"]
                                    nb = state["Sn"]
                                    state["Sn"] = 1 - nb
                                    k.op("dve", lambda e: e.tensor_tensor(
                                        out=Sb[nb][:, pc0:512], in0=S_t[:, pc0:512], in1=spb[sb_i][:, pc0:512],
                                        op=ALU.add), reads=[S_rg, spb_r[sb_i]], writes=[Sb_r[nb]])
                                    if c0 < pc0:
                                        k.op("dve", lambda e: e.tensor_copy(
                                            out=Sb[nb][:, c0:pc0], in_=spb[sb_i][:, c0:pc0]),
                                            reads=[spb_r[sb_i]], writes=[Sb_r[nb]])
                                    state["S"] = (Sb[nb], Sb_r[nb])

                        def s2a(i):
                            s_ = steps[i]
                            fb = i % NF
                            c0 = s_["c0"]
                            ab = i % NA
                            k.op("act", lambda e: e.activation(
                                out=Ab[ab][:, c0:512], in_=f_ps[fb][:, c0:512], func=AF.Exp),
                                reads=[f_r[fb]], writes=[Ab_r[ab]])
                            if s_["j"] >= 0:
                                k.op("dve", lambda e: e.tensor_tensor(
                                    out=Ab[ab][:, c0:c0 + 128], in0=Ab[ab][:, c0:c0 + 128],
                                    in1=masks[:, :], op=ALU.mult),
                                    reads=[Ab_r[ab]], writes=[Ab_r[ab]])

                        def s3(i):
                            s_ = steps[i]
                            hp, hh, kb, qg, c0 = s_["hp"], s_["hh"], s_["kb"], s_["qg"], s_["c0"]
                            pb = 64 * hh
                            ab = i % NA
                            ob = s_["hidx"] % 2
                            k.op("pe", lambda e: e.matmul(
                                o_ps[ob][:, c0:512], lhsT=Vsb[:, kb, hp * 128:(hp + 1) * 128], rhs=Ab[ab][:, c0:512],
                                start=s_["first"], stop=s_["last"], skip_group_check=True),
                                reads=[Ab_r[ab], r_att], writes=[o_r[ob]])
                            if s_["last"]:
                                cs = slice(qg * 512, (qg + 1) * 512)
                                k.op("dve", lambda e: e.tensor_copy(
                                    out=aT[pb:pb + 64, hp, cs], in_=o_ps[ob][pb:pb + 64, :]),
                                    reads=[o_r[ob]], writes=[aT_r])
                                sqb = (s_["hidx"] // 2) % 2
                                k.op("dve", lambda e: e.tensor_tensor(
                                    out=sq[sqb][pb:pb + 64, :], in0=aT[pb:pb + 64, hp, cs],
                                    in1=aT[pb:pb + 64, hp, cs], op=ALU.mult), reads=[aT_r], writes=[sq_r[sqb]])
                                if hh == 1:
                                    deferred.append((i + 2, hp, qg, sqb))

                        def run_deferred(i):
                            while deferred and deferred[0][0] <= i:
                                _, hp, qg, sqb = deferred.pop(0)
                                for jj in range(4):
                                    k.op("pe", lambda e, jj=jj: e.matmul(
                                        ssa_ps[:, hp * 4 + jj:hp * 4 + jj + 1],
                                        lhsT=sq[sqb][:, jj * 128:(jj + 1) * 128], rhs=onesb[:, 0:1],
                                        start=True, stop=True), reads=[sq_r[sqb], r_const], writes=[ssa_r])
                                if hp == 3:
                                    k.op("dve", lambda e: e.tensor_reduce(
                                        out=ssa_sb[:, qg * 4:(qg + 1) * 4],
                                        in_=ssa_ps[:, 0:16].rearrange("p (h j) -> p j h", h=4),
                                        axis=AX.X, op=ALU.add), reads=[ssa_r], writes=[ssa_sb_r])

                        nst = len(steps)
                        qcopy(0)
                        s1p(0)
                        for it in range(nst + 3):
                            if it < nst:
                                s1a(it)
                            if 0 <= it - 1 < nst:
                                s2p(it - 1)
                            if 0 <= it - 3 < nst:
                                s3(it - 3)
                            run_deferred(it - 3 if it < nst + 2 else 10 ** 9)
                            if it + 1 < nst:
                                s1p(it + 1)
                            if 0 <= it - 2 < nst:
                                s2a(it - 2)
                            if it < nst:
                                s1b(it)
                        k.barrier(scratch[:, 0:1])

                    with ExitStack() as e4:
                        xr = [sb(e4, "xr%d" % i, [128, D], F32) for i in range(3)]
                        xr_r = [Reg() for _ in range(3)]
                        xr_ch = [k.chan() for _ in range(3)]
                        xo_ch = [k.chan() for _ in range(3)]
                        sda = sb(e4, "sda", [128, NT], F32)
                        rsa = sb(e4, "rsa", [128, NT], F32)
                        rsa_r = Reg()
                        acc = [[ps(e4, "acc%d_%d" % (i, h), [128, 512]) for h in range(4)] for i in range(2)]
                        acc_r = [[Reg() for _ in range(4)] for _ in range(2)]
                        r_all = Reg()
                        k.op("act", lambda e: e.activation(
                            out=sda[:, :], in_=ssa_sb[:, :], func=AF.Sqrt, bias=eps_t[:, 0:1],
                            scale=1.0 / 512), reads=[r_all], writes=[rsa_r])
                        k.op("dve", lambda e: e.reciprocal(out=rsa[:, :], in_=sda[:, :]),
                             reads=[rsa_r], writes=[rsa_r])
                        for i in range(NT):
                            sl = i % 3
                            ab = i % 2
                            k.dma("sp", xr_ch[sl], xr[sl][:, :], x[b, i * 128:(i + 1) * 128, :],
                                  writes=[xr_r[sl]])
                            for hf in range(2):
                                for c in range(4):
                                    k.op("pe", lambda e, c=c, hf=hf: e.matmul(
                                        acc[ab][hf][:, :], lhsT=aT[:, c, i * 128:(i + 1) * 128],
                                        rhs=w_out_bf[:, c, hf * 512:(hf + 1) * 512],
                                        start=(c == 0), stop=(c == 3)), reads=[r_all], writes=[acc_r[ab][hf]])
                                for c in range(4):
                                    k.op("pe", lambda e, c=c, hf=hf: e.matmul(
                                        acc[ab][2 + hf][:, :], lhsT=bT[:, c, i * 128:(i + 1) * 128],
                                        rhs=w_out_bf[:, 4 + c, hf * 512:(hf + 1) * 512],
                                        start=(c == 0), stop=(c == 3)), reads=[r_all], writes=[acc_r[ab][2 + hf]])
                            for hf in range(2):
                                cs = slice(hf * 512, (hf + 1) * 512)
                                k.op("dve", lambda e, hf=hf, cs=cs: e.scalar_tensor_tensor(
                                    out=xr[sl][:, cs], in0=acc[ab][hf][:, :], scalar=rsa[:, i:i + 1],
                                    in1=xr[sl][:, cs], op0=ALU.mult, op1=ALU.add),
                                    reads=[acc_r[ab][hf], rsa_r, xr_r[sl]], writes=[xr_r[sl]])
                                k.op("dve", lambda e, hf=hf, cs=cs: e.tensor_tensor(
                                    out=xr[sl][:, cs], in0=xr[sl][:, cs], in1=acc[ab][2 + hf][:, :], op=ALU.add),
                                    reads=[acc_r[ab][2 + hf], xr_r[sl]], writes=[xr_r[sl]])
                            k.dma("sp", xo_ch[sl], y[b, i * 128:(i + 1) * 128, :], xr[sl][:, :],
                                  reads=[xr_r[sl]], writes=[y_reg[b][i]])
                        k.barrier(scratch[:, 0:1])

        with ExitStack() as esB:
            w_up_bf = sb(esB, "w_up_bf", [128, 8, DFF], BF16)
            w_dn_bf = sb(esB, "w_dn_bf", [128, 32, D], BF16)
            gf = sb(esB, "gf", [128, D], F32)
            gmlpf = sb(esB, "gmlpf", [128, D], F32)
            r_wB = Reg()
            wch2 = k.chan()
            k.dma("sp", wch2, gf[:], gf_d[:, :], writes=[r_wB])
            k.dma("sp", wch2, gmlpf[:], gmlpf_d[:, :], writes=[r_wB])
            wup_ch = k.chan()
            wdn_ch = k.chan()
            r_wup1 = Reg()
            r_wdn1 = Reg()
            for c in range(8):
                k.dma("pool", wup_ch, w_up_bf[:, c, :], w_up[c * 128:(c + 1) * 128, :], writes=[r_wup1])
            for q in range(4):
                k.dma("pool", wdn_ch, w_dn_bf[:, q * 8:(q + 1) * 8, :],
                      w_down[q * 1024:(q + 1) * 1024, :].rearrange("(c p) d -> p c d", p=128),
                      writes=[r_wdn1])
            r_wup = [r_wup1] * 4
            r_wdn = [r_wdn1] * 4

            with ExitStack() as e5:
                NPS = 2
                xp = [sb(e5, "xp%d" % i, [128, D], F32) for i in range(NPS)]
                xp_r = [Reg() for _ in range(NPS)]
                xp_ch = [k.chan() for _ in range(NPS)]
                NES = 2
                xe = [sb(e5, "xe%d" % i, [128, D], F32) for i in range(NES)]
                xe_r = [Reg() for _ in range(NES)]
                xe_ch = [k.chan() for _ in range(NES)]
                xo_ch = [k.chan() for _ in range(NES)]
                xn = [sb(e5, "xnB%d" % i, [128, D], BF16) for i in range(2)]
                xn_r = [Reg() for _ in range(2)]
                h2T = sb(e5, "h2T", [128, 8, 512], BF16)
                h2_r = [Reg() for _ in range(4)]
                actT = sb(e5, "actT", [128, 32, 512], BF16)
                act_r = [Reg() for _ in range(32)]
                rr = [sb(e5, "rr%d" % i, [128, 512], F32) for i in range(2)]
                rr_r = [Reg() for _ in range(2)]
                NST = 4
                stp = sb(e5, "stp", [128, NST, 4], F32)
                stp_r = [Reg() for _ in range(NST)]
                ste = sb(e5, "ste", [128, NST, 4], F32)
                ste_r = [Reg() for _ in range(NST)]
                tp_ps = ps(e5, "tpB", [128, 8, 128], BF16)
                tp_r = Reg()
                up_ps = [ps(e5, "up_ps%d" % i, [128, 512]) for i in range(3)]
                up_r = [Reg() for _ in range(3)]
                dn_ps = [ps(e5, "dn_ps%d" % i, [128, 512]) for i in range(4)]
                dn_r = [Reg() for _ in range(4)]
                junkB = sb(e5, "junkB", [128, D], BF16)
                junkB_r = Reg()
                cnt = {"p": 0, "e": 0, "up": 0, "dn": 0}

                def prologue(blk):
                    b = blk // 4
                    G = blk % 4
                    for j in range(4):
                        i = 4 * G + j
                        n = cnt["p"]
                        cnt["p"] += 1
                        sl = n % NPS
                        xs = n % 2
                        q = n % NST
                        k.dma("sp", xp_ch[sl], xp[sl][:, :], y[b, i * 128:(i + 1) * 128, :],
                              reads=[y_reg[b][i]], writes=[xp_r[sl]])
                        k.op("act", lambda e, sl=sl, xs=xs, q=q: e.activation(
                            out=xn[xs][:, :], in_=xp[sl][:, :], func=AF.Square, accum_out=stp[:, q, 0:1]),
                            reads=[xp_r[sl]], writes=[xn_r[xs], stp_r[q]])
                        k.op("act", lambda e, q=q: e.activation(
                            out=stp[:, q, 1:2], in_=stp[:, q, 0:1], func=AF.Sqrt, bias=eps_t[:, 0:1],
                            scale=1.0 / D), reads=[stp_r[q]], writes=[stp_r[q]])
                        k.op("dve", lambda e, q=q: e.reciprocal(out=stp[:, q, 2:3], in_=stp[:, q, 1:2]),
                             reads=[stp_r[q]], writes=[stp_r[q]])
                        k.op("dve", lambda e, sl=sl, xs=xs, q=q: e.scalar_tensor_tensor(
                            out=xn[xs][:, :], in0=xp[sl][:, :], scalar=stp[:, q, 2:3], in1=gmlpf[:, :],
                            op0=ALU.mult, op1=ALU.mult), reads=[xp_r[sl], stp_r[q], r_wB], writes=[xn_r[xs]])
                        for c in range(8):
                            k.op("pe", lambda e, c=c, xs=xs: e.transpose(
                                tp_ps[:, c, :], xn[xs][:, c * 128:(c + 1) * 128], ident[:]),
                                reads=[xn_r[xs]], writes=[tp_r])
                        k.op("act", lambda e, j=j: e.copy(
                            out=h2T[:, :, j * 128:(j + 1) * 128], in_=tp_ps[:, :, :]),
                            reads=[tp_r], writes=[h2_r[j]])

                def up(blk):
                    for fc in range(32):
                        ub = cnt["up"] % 3
                        rb = cnt["up"] % 2
                        cnt["up"] += 1
                        for c in range(8):
                            k.op("pe", lambda e, c=c, fc=fc, ub=ub: e.matmul(
                                up_ps[ub][:, :], lhsT=w_up_bf[:, c, fc * 128:(fc + 1) * 128],
                                rhs=h2T[:, c, :], start=(c == 0), stop=(c == 7)),
                                reads=h2_r + [r_wup[fc // 8]], writes=[up_r[ub]])
                        k.op("act", lambda e, ub=ub, rb=rb: e.activation(
                            out=rr[rb][:, :], in_=up_ps[ub][:, :], func=AF.Relu),
                            reads=[up_r[ub]], writes=[rr_r[rb]])
                        k.op("pool", lambda e, fc=fc, rb=rb: e.tensor_tensor(
                            out=actT[:, fc, :], in0=rr[rb][:, :], in1=rr[rb][:, :], op=ALU.mult),
                            reads=[rr_r[rb]], writes=[act_r[fc]])

                def down_epi(blk):
                    b = blk // 4
                    G = blk % 4
                    for j in range(4):
                        i = 4 * G + j
                        n = cnt["e"]
                        cnt["e"] += 1
                        sl = n % NES
                        q = n % NST
                        k.dma("sp", xe_ch[sl], xe[sl][:, :], y[b, i * 128:(i + 1) * 128, :],
                              reads=[y_reg[b][i]], writes=[xe_r[sl]])
                        for hf in range(2):
                            db = cnt["dn"] % 4
                            cnt["dn"] += 1
                            cs = slice(hf * 512, (hf + 1) * 512)
                            for fc in range(32):
                                k.op("pe", lambda e, fc=fc, db=db, j=j, hf=hf: e.matmul(
                                    dn_ps[db][:, :], lhsT=actT[:, fc, j * 128:(j + 1) * 128],
                                    rhs=w_dn_bf[:, fc, hf * 512:(hf + 1) * 512],
                                    start=(fc == 0), stop=(fc == 31)),
                                    reads=[act_r[fc], r_wdn[fc // 8]], writes=[dn_r[db]])
                            k.op("dve", lambda e, db=db, sl=sl, cs=cs: e.tensor_tensor(
                                out=xe[sl][:, cs], in0=xe[sl][:, cs], in1=dn_ps[db][:, :], op=ALU.add),
                                reads=[dn_r[db], xe_r[sl]], writes=[xe_r[sl]])
                        xs = n % 2
                        k.op("act", lambda e, sl=sl, q=q: e.activation(
                            out=junkB[:, :], in_=xe[sl][:, :], func=AF.Square,
                            accum_out=ste[:, q, 0:1]),
                            reads=[xe_r[sl]], writes=[junkB_r, ste_r[q]])
                        k.op("act", lambda e, q=q: e.activation(
                            out=ste[:, q, 1:2], in_=ste[:, q, 0:1], func=AF.Sqrt, bias=eps_t[:, 0:1],
                            scale=1.0 / D), reads=[ste_r[q]], writes=[ste_r[q]])
                        k.op("dve", lambda e, q=q: e.reciprocal(out=ste[:, q, 2:3], in_=ste[:, q, 1:2]),
                             reads=[ste_r[q]], writes=[ste_r[q]])
                        k.op("dve", lambda e, sl=sl, q=q: e.scalar_tensor_tensor(
                            out=xe[sl][:, :], in0=xe[sl][:, :], scalar=ste[:, q, 2:3], in1=gf[:, :],
                            op0=ALU.mult, op1=ALU.mult), reads=[xe_r[sl], ste_r[q], r_wB], writes=[xe_r[sl]])
                        k.dma("sp", xo_ch[sl], y[b, i * 128:(i + 1) * 128, :], xe[sl][:, :],
                              reads=[xe_r[sl]], writes=[y_reg[b][i]])

                NBLK = BPC * 4
                prologue(0)
                for blk in range(NBLK):
                    up(blk)
                    if blk + 1 < NBLK:
                        prologue(blk + 1)
                    down_epi(blk)
                k.barrier(scratch[:, 0:1])
    return nc


def _host_consts():
    ident = np.eye(128, dtype=np.float32)
    jj = np.arange(128)[:, None]
    ss = np.arange(128)[None, :]
    ntri = -(jj >= ss).astype(np.float32)
    nones = -np.ones((128, 128), np.float32)
    s_idx = np.arange(128)[:, None]
    t_idx = np.arange(128)[None, :]
    masks = (s_idx < t_idx).astype(np.float32)
    return ident, ntri, nones, masks


def kernel(x, norm_mix_g, w_in, sg_ln_g, sg_ln_b, sg_w, sg_b, out_norm_g, w_out,
           norm_mlp_g, w_up, w_down, norm_final_g):
    f = np.float32
    x = np.ascontiguousarray(np.asarray(x, dtype=f))
    ident, ntri, nones, masks = _host_consts()

    def pc(v):
        return np.ascontiguousarray(np.asarray(v, dtype=f).reshape(8, 128).T)

    shared = {
        "w_in": np.ascontiguousarray(np.asarray(w_in, dtype=f)[0]),
        "w_out": np.ascontiguousarray(np.asarray(w_out, dtype=f)[0]),
        "w_up": np.ascontiguousarray(np.asarray(w_up, dtype=f)[0]),
        "w_down": np.ascontiguousarray(np.asarray(w_down, dtype=f)[0]),
        "g_mix": pc(norm_mix_g[0]),
        "g_out": pc(out_norm_g[0]),
        "g_mlp": pc(norm_mlp_g[0]),
        "lng_full": np.ascontiguousarray(np.broadcast_to(np.asarray(sg_ln_g, dtype=f)[0][None, :], (128, 512))),
        "lnb_full": np.ascontiguousarray(np.broadcast_to(np.asarray(sg_ln_b, dtype=f)[0][None, :], (128, 512))),
        "bias_full": np.ascontiguousarray(
            np.broadcast_to(np.asarray(sg_b, dtype=f)[0].T[:, :, None], (128, 8, 64)).reshape(128, 512)),
        "gmix_full": np.ascontiguousarray(np.broadcast_to(np.asarray(norm_mix_g, dtype=f)[0][None, :], (128, D))),
        "gmlp_full": np.ascontiguousarray(np.broadcast_to(np.asarray(norm_mlp_g, dtype=f)[0][None, :], (128, D))),
        "gf_full": np.ascontiguousarray(np.broadcast_to(np.asarray(norm_final_g, dtype=f)[None, :], (128, D))),
        "sgwT": np.ascontiguousarray(np.transpose(np.asarray(sg_w, dtype=f)[0], (2, 0, 1))),
        "ident": ident, "ntri": ntri, "nones": nones, "masks": masks,
    }
    nc = build()
    in_maps = []
    for c in range(NCORES):
        m = dict(shared)
        m["x"] = np.ascontiguousarray(x[c * BPC:(c + 1) * BPC])
        in_maps.append(m)
    res = run_bass_kernel_spmd(nc, in_maps, core_ids=list(range(NCORES)))
    out = np.concatenate([np.asarray(r["y"]) for r in res.results], axis=0)
    return out.astype(np.float32)
```

```python
import numpy as np
import ml_dtypes
from contextlib import ExitStack

import concourse.bass as bass
import concourse.mybir as mybir
from concourse.bass_utils import run_bass_kernel_spmd

F32 = mybir.dt.float32
BF16 = mybir.dt.bfloat16
AF = mybir.ActivationFunctionType
ALU = mybir.AluOpType
AX = mybir.AxisListType

NCORES = 8
BPC = 2
S = 2048
D = 1024
NT = S // 128
INW = 2560
DFF = 4096
EPS = 1e-6


class Tk:
    __slots__ = ("sem", "val")

    def __init__(self, sem, val):
        self.sem = sem
        self.val = val


class Reg:
    __slots__ = ("w", "r", "name")

    def __init__(self, name=""):
        self.w = None
        self.r = {}
        self.name = name


class Chan:
    def __init__(self, sem):
        self.sem = sem
        self.n = 0


class Eng:
    def __init__(self, eng, sem, name):
        self.eng = eng
        self.sem = sem
        self.n = 0
        self.seen = {}
        self.name = name

    def wait(self, tk):
        if tk is None:
            return
        k = id(tk.sem)
        if self.seen.get(k, 0) >= tk.val:
            return
        self.eng.wait_ge(tk.sem, tk.val)
        self.seen[k] = tk.val


class K:
    def __init__(self, nc, es):
        self.nc = nc
        self.es = es
        self.chans = []
        self.E = {}
        for nm, eng in (("pe", nc.tensor), ("act", nc.scalar), ("dve", nc.vector),
                        ("pool", nc.gpsimd), ("sp", nc.sync)):
            sem = es.enter_context(nc.semaphore("sem_" + nm))
            self.E[nm] = Eng(eng, sem, nm)
        self._nm = 0

    def chan(self):
        self._nm += 1
        c = Chan(self.es.enter_context(self.nc.semaphore("ch%d" % self._nm)))
        self.chans.append(c)
        return c

    def _deps(self, E, reads, writes):
        for r in reads:
            if r.w is not None:
                E.wait(r.w)
        pe = E.name == "pe"
        for w in writes:
            if w.w is not None and not (pe and w.w.sem is E.sem):
                E.wait(w.w)
            for tk in w.r.values():
                if not (pe and tk.sem is E.sem):
                    E.wait(tk)

    def _mark(self, tk, reads, writes):
        for r in reads:
            r.r[id(tk.sem)] = tk
        for w in writes:
            w.w = tk
            w.r = {}

    def op(self, e, fn, reads=(), writes=()):
        E = self.E[e]
        self._deps(E, reads, writes)
        inst = fn(E.eng)
        E.n += 1
        inst.then_inc(E.sem, 1)
        tk = Tk(E.sem, E.n)
        self._mark(tk, reads, writes)
        return tk

    def dma(self, q, ch, out, in_, reads=(), writes=()):
        E = self.E[q]
        self._deps(E, reads, writes)
        inst = E.eng.dma_start(out=out, in_=in_)
        ch.n += 1
        inst.then_inc(ch.sem, 16)
        tk = Tk(ch.sem, 16 * ch.n)
        self._mark(tk, reads, writes)
        return tk

    def barrier(self, scratch_ap):
        V = self.E["dve"]
        for nm, E in self.E.items():
            if E is not V and E.n > 0:
                V.wait(Tk(E.sem, E.n))
        for c in self.chans:
            if c.n > 0:
                V.wait(Tk(c.sem, 16 * c.n))
        if V.n > 0:
            V.wait(Tk(V.sem, V.n))
        inst = V.eng.memset(scratch_ap, 0.0)
        V.n += 1
        inst.then_inc(V.sem, 1)
        tk = Tk(V.sem, V.n)
        for nm, E in self.E.items():
            E.wait(tk)


def build():
    nc = bass.Bass("TRN2", target_bir_lowering=False)

    def din(name, shape):
        return nc.dram_tensor(name, list(shape), F32, kind="ExternalInput").ap()

    x = din("x", [BPC, S, D])
    w_in = din("w_in", [D, INW])
    w_out = din("w_out", [D, D])
    w_up = din("w_up", [D, DFF])
    w_down = din("w_down", [DFF, D])
    g_mix = din("g_mix", [128, 8])
    g_out = din("g_out", [128, 8])
    g_mlp = din("g_mlp", [128, 8])
    lng_d = din("lng_full", [128, 512])
    lnb_d = din("lnb_full", [128, 512])
    bias_d = din("bias_full", [128, 512])
    gf_d = din("gf_full", [128, D])
    gmixf_d = din("gmix_full", [128, D])
    gmlpf_d = din("gmlp_full", [128, D])
    sgwT_d = din("sgwT", [128, 8, 128])
    ident_d = din("ident", [128, 128])
    ntri_d = din("ntri", [128, 128])
    nones_d = din("nones", [128, 128])
    masks_d = din("masks", [128, 128])
    y = nc.dram_tensor("y", [BPC, S, D], F32, kind="ExternalOutput").ap()

    with ExitStack() as es:
        k = K(nc, es)

        uid = [0]

        def sb(es_, name, shape, dt):
            uid[0] += 1
            return es_.enter_context(nc.sbuf_tensor("s%d_%s" % (uid[0], name), list(shape), dt))

        def ps(es_, name, shape, dt=F32):
            uid[0] += 1
            return es_.enter_context(nc.psum_tensor("p%d_%s" % (uid[0], name), list(shape), dt))

        ident = sb(es, "ident", [128, 128], BF16)
        ntri = sb(es, "ntri", [128, 128], BF16)
        nones = sb(es, "nones", [128, 128], BF16)
        masks = sb(es, "masks", [128, 128], BF16)
        ones32 = sb(es, "ones32", [128, 1], F32)
        onesb = sb(es, "onesb", [128, 1], BF16)
        eps_t = sb(es, "eps_t", [128, 1], F32)
        one_t = sb(es, "one_t", [128, 1], F32)
        scratch = sb(es, "scratch", [128, 8], F32)
        gmix = sb(es, "gmix", [128, 8], F32)
        gout = sb(es, "gout", [128, 8], F32)
        gmlp = sb(es, "gmlp", [128, 8], F32)
        r_const = Reg("const")

        cch = k.chan()
        cch2 = k.chan()
        for dst, src in ((ident, ident_d), (ntri, ntri_d), (nones, nones_d)):
            k.dma("pool", cch, dst[:], src[:, :], writes=[r_const])
        k.dma("pool", cch, masks[:], masks_d[:, :], writes=[r_const])
        for dst, src in ((gmix, g_mix), (gout, g_out), (gmlp, g_mlp)):
            k.dma("sp", cch2, dst[:], src[:, :], writes=[r_const])
        k.op("dve", lambda e: e.memset(ones32[:], 1.0), writes=[r_const])
        k.op("dve", lambda e: e.memset(onesb[:], 1.0), writes=[r_const])
        k.op("dve", lambda e: e.memset(eps_t[:], EPS), writes=[r_const])
        k.op("dve", lambda e: e.memset(one_t[:], 1.0), writes=[r_const])
        k.barrier(scratch[:, 0:1])

        y_reg = [[Reg("y%d_%d" % (b, i)) for i in range(NT)] for b in range(BPC)]

        with ExitStack() as esA:
            w_in_bf = sb(esA, "w_in_bf", [128, 8, INW], BF16)
            w_out_bf = sb(esA, "w_out_bf", [128, 8, D], BF16)
            sgwT = sb(esA, "sgwT", [128, 8, 128], BF16)
            lng = sb(esA, "lng", [128, 512], F32)
            lnb = sb(esA, "lnb", [128, 512], F32)
            biasf = sb(esA, "biasf", [128, 512], F32)
            gmixf = sb(esA, "gmixf", [128, D], F32)
            QT = sb(esA, "QT", [128, 4, S], BF16)
            KT = sb(esA, "KT", [128, 4, S], BF16)
            Vsb = sb(esA, "Vsb", [128, NT, 512], BF16)
            bT = sb(esA, "bT", [128, 4, S], BF16)
            ssa_sb = sb(esA, "ssa_sb", [128, NT], F32)
            r_w = Reg("wA")

            with ExitStack() as esP:
                stg = [sb(esP, "stg%d" % i, [128, D], F32) for i in range(2)]
                stg_r = [Reg("stg%d" % i) for i in range(2)]
                stg_ch = [k.chan() for _ in range(2)]
                pch = k.chan()
                r_sg = Reg("sgw")
                sgch = k.chan()
                wich = k.chan()
                for c in range(8):
                    k.dma("pool", wich, w_in_bf[:, c, :], w_in[c * 128:(c + 1) * 128, :], writes=[r_w])
                k.dma("pool", sgch, sgwT[:], sgwT_d[:, :, :], writes=[r_sg])
                k.dma("sp", pch, lng[:], lng_d[:, :], writes=[r_w])
                k.dma("sp", pch, lnb[:], lnb_d[:, :], writes=[r_w])
                k.dma("sp", pch, biasf[:], bias_d[:, :], writes=[r_w])
                k.dma("sp", pch, gmixf[:], gmixf_d[:, :], writes=[r_w])
                for c in range(8):
                    sl = c % 2
                    k.dma("sp", stg_ch[sl], stg[sl][:, :], w_out[c * 128:(c + 1) * 128, :],
                          writes=[stg_r[sl]])
                    if c % 2 == 0:
                        k.op("dve", lambda e, c=c, sl=sl: e.tensor_scalar(
                            out=w_out_bf[:, c, :], in0=stg[sl][:, :], scalar1=gout[:, c:c + 1],
                            scalar2=None, op0=ALU.mult), reads=[stg_r[sl], r_const], writes=[r_w])
                    else:
                        k.op("act", lambda e, c=c, sl=sl: e.activation(
                            out=w_out_bf[:, c, :], in_=stg[sl][:, :], func=AF.Copy,
                            scale=gout[:, c:c + 1]), reads=[stg_r[sl], r_const], writes=[r_w])
                k.op("pool", lambda e: e.memset(sgwT[64:128, :, 0:64], 0.0), reads=[r_sg], writes=[r_sg])
                k.barrier(scratch[:, 0:1])

            for b in range(BPC):
                with ExitStack() as e1:
                    NXS = 4
                    xt = [sb(e1, "xt%d" % i, [128, D], F32) for i in range(NXS)]
                    xt_r = [Reg() for _ in range(NXS)]
                    xt_ch = [k.chan() for _ in range(NXS)]
                    junk = sb(e1, "junk", [128, D], BF16)
                    junk_r = Reg()
                    xn = [sb(e1, "xn%d" % i, [128, D], BF16) for i in range(2)]
                    xn_r = [Reg() for _ in range(2)]
                    hT = [sb(e1, "hT%d" % i, [128, 8, 512], BF16) for i in range(2)]
                    hT_r = [[Reg() for _ in range(4)] for _ in range(2)]
                    gu = [sb(e1, "gu%d" % i, [128, 512], F32) for i in range(4)]
                    gv = [sb(e1, "gv%d" % i, [128, 512], F32) for i in range(4)]
                    gu_r = [Reg() for _ in range(4)]
                    gv_r = [Reg() for _ in range(4)]
                    braw = [sb(e1, "braw%d" % i, [128, 512], F32) for i in range(4)]
                    braw_r = [Reg() for _ in range(4)]
                    t1 = sb(e1, "t1", [128, 512], F32)
                    t1_r = Reg()
                    t2 = sb(e1, "t2", [128, 512], F32)
                    t2_r = Reg()
                    vn = [sb(e1, "vn%d" % i, [128, 512], BF16) for i in range(2)]
                    vn_r = [Reg() for _ in range(2)]
                    bn = [sb(e1, "bn%d" % i, [128, 512], BF16) for i in range(2)]
                    bn_r = [Reg() for _ in range(2)]
                    st = sb(e1, "st", [128, 2, 32], F32)
                    st_r = [[Reg() for _ in range(8)] for _ in range(2)]
                    bst = sb(e1, "bst", [128, 4, 6], F32)
                    bst_r = [Reg() for _ in range(4)]
                    mv = sb(e1, "mv", [128, 2, 4, 2], F32)
                    tp_ps = ps(e1, "tp_ps", [128, 8, 128], BF16)
                    tp_r = Reg()
                    qk_ps = [ps(e1, "qk_ps%d" % i, [128, 512]) for i in range(2)]
                    qk_r = [Reg() for _ in range(2)]
                    v_ps = ps(e1, "v_ps", [128, 512])
                    v_r = Reg()
                    u_ps = ps(e1, "u_ps", [128, 512])
                    u_r = Reg()
                    vg_ps = ps(e1, "vg_ps", [128, 512])
                    vg_r = Reg()
                    mix_ps = ps(e1, "mix_ps", [128, 512])
                    mix_r = Reg()
                    bt_ps = ps(e1, "bt_ps", [128, 4, 128], BF16)
                    bt_r = Reg()
                    QT_r = Reg()
                    KT_r = Reg()
                    V_r = Reg()
                    bT_r = Reg()

                    cnt = {"load": 0, "qk": 0}
                    mv_rj = [Reg() for _ in range(4)]

                    def stage_A(G):
                        gp = G % 2
                        R = st_r[gp]
                        slots = []
                        for j in range(4):
                            i = 4 * G + j
                            sl = cnt["load"] % NXS
                            cnt["load"] += 1
                            slots.append(sl)
                            k.dma("sp", xt_ch[sl], xt[sl][:, :], x[b, i * 128:(i + 1) * 128, :],
                                  writes=[xt_r[sl]])
                            k.op("act", lambda e, sl=sl, j=j: e.activation(
                                out=junk[:, :], in_=xt[sl][:, :], func=AF.Square,
                                accum_out=st[:, gp, j:j + 1]),
                                reads=[xt_r[sl]], writes=[junk_r, R[0]])
                        k.op("act", lambda e: e.activation(
                            out=st[:, gp, 4:8], in_=st[:, gp, 0:4], func=AF.Sqrt,
                            bias=eps_t[:, 0:1], scale=1.0 / D), reads=[R[0]], writes=[R[1]])
                        k.op("dve", lambda e: e.reciprocal(out=st[:, gp, 8:12], in_=st[:, gp, 4:8]),
                             reads=[R[1]], writes=[R[2]])
                        for j in range(4):
                            sl = slots[j]
                            xs = j % 2
                            k.op("dve", lambda e, sl=sl, xs=xs, j=j: e.scalar_tensor_tensor(
                                out=xn[xs][:, :], in0=xt[sl][:, :], scalar=st[:, gp, 8 + j:9 + j],
                                in1=gmixf[:, :], op0=ALU.mult, op1=ALU.mult),
                                reads=[xt_r[sl], R[2], r_w], writes=[xn_r[xs]])
                            for c in range(8):
                                k.op("pe", lambda e, c=c, xs=xs: e.transpose(
                                    tp_ps[:, c, :], xn[xs][:, c * 128:(c + 1) * 128], ident[:]),
                                    reads=[xn_r[xs]], writes=[tp_r])
                            k.op("act", lambda e, j=j: e.copy(
                                out=hT[gp][:, :, j * 128:(j + 1) * 128], in_=tp_ps[:, :, :]),
                                reads=[tp_r], writes=[hT_r[gp][j]])
                        tc0 = G * 512
                        for eb in range(8):
                            qs = cnt["qk"] % 2
                            cnt["qk"] += 1
                            for c in range(8):
                                k.op("pe", lambda e, c=c, eb=eb, qs=qs: e.matmul(
                                    qk_ps[qs][:, :], lhsT=w_in_bf[:, c, eb * 128:(eb + 1) * 128],
                                    rhs=hT[gp][:, c, :], start=(c == 0), stop=(c == 7)),
                                    reads=hT_r[gp] + [r_w], writes=[qk_r[qs]])
                            if eb < 4:
                                k.op("dve", lambda e, eb=eb, qs=qs: e.tensor_scalar(
                                    out=QT[:, eb, tc0:tc0 + 512], in0=qk_ps[qs][:, :], scalar1=0.125,
                                    scalar2=None, op0=ALU.mult), reads=[qk_r[qs]], writes=[QT_r])
                            else:
                                k.op("act", lambda e, eb=eb, qs=qs: e.copy(
                                    out=KT[:, eb - 4, tc0:tc0 + 512], in_=qk_ps[qs][:, :]),
                                    reads=[qk_r[qs]], writes=[KT_r])

                    def stage_VUV(G, j):
                        gp = G % 2
                        i = 4 * G + j
                        for (pst, pr, c0) in ((v_ps, v_r, 1024), (u_ps, u_r, 1536), (vg_ps, vg_r, 2048)):
                            for c in range(8):
                                k.op("pe", lambda e, c=c, pst=pst, c0=c0: e.matmul(
                                    pst[:, :], lhsT=hT[gp][:, c, j * 128:(j + 1) * 128],
                                    rhs=w_in_bf[:, c, c0:c0 + 512], start=(c == 0), stop=(c == 7)),
                                    reads=[hT_r[gp][j], r_w], writes=[pr])
                        k.op("dve", lambda e: e.tensor_copy(out=Vsb[:, i, :], in_=v_ps[:, :]),
                             reads=[v_r], writes=[V_r])
                        k.op("act", lambda e: e.activation(
                            out=gu[j][:, :], in_=u_ps[:, :], func=AF.Gelu_apprx_tanh),
                            reads=[u_r], writes=[gu_r[j]])
                        k.op("act", lambda e: e.activation(
                            out=gv[j][:, :], in_=vg_ps[:, :], func=AF.Gelu_apprx_tanh),
                            reads=[vg_r], writes=[gv_r[j]])
                        k.op("dve", lambda e: e.bn_stats(out=bst[:, j, :], in_=gv[j][:, :]),
                             reads=[gv_r[j]], writes=[bst_r[j]])
                        k.op("dve", lambda e: e.bn_aggr(out=mv[:, gp, j, :], in_=bst[:, j, :]),
                             reads=[bst_r[j]], writes=[mv_rj[j]])

                    def stage_LNS(G):
                        gp = G % 2
                        R = st_r[gp]
                        k.op("act", lambda e: e.activation(
                            out=st[:, gp, 12:16], in_=mv[:, gp, :, 1], func=AF.Sqrt,
                            bias=eps_t[:, 0:1], scale=1.0), reads=mv_rj, writes=[R[3]])
                        k.op("dve", lambda e: e.reciprocal(out=st[:, gp, 16:20], in_=st[:, gp, 12:16]),
                             reads=[R[3]], writes=[R[4]])

                    def stage_SGU2(G, j):
                        gp = G % 2
                        R = st_r[gp]
                        vs = j % 2
                        k.op("dve", lambda e: e.scalar_tensor_tensor(
                            out=t1[:, :], in0=gv[j][:, :], scalar=mv[:, gp, j, 0:1], in1=lng[:, :],
                            op0=ALU.subtract, op1=ALU.mult),
                            reads=[gv_r[j], mv_rj[j], r_w], writes=[t1_r])
                        k.op("dve", lambda e: e.scalar_tensor_tensor(
                            out=vn[vs][:, :], in0=t1[:, :], scalar=st[:, gp, 16 + j:17 + j], in1=lnb[:, :],
                            op0=ALU.mult, op1=ALU.add),
                            reads=[t1_r, R[4], r_w], writes=[vn_r[vs]])
                        for g in range(8):
                            k.op("pe", lambda e, g=g: e.matmul(
                                mix_ps[:, g * 64:(g + 1) * 64], lhsT=sgwT[:, g, :],
                                rhs=vn[vs][:, g * 64:(g + 1) * 64], start=True, stop=True),
                                reads=[vn_r[vs], r_w], writes=[mix_r])
                        k.op("dve", lambda e: e.tensor_tensor(
                            out=t2[:, :], in0=mix_ps[:, :], in1=biasf[:, :], op=ALU.add),
                            reads=[mix_r, r_w], writes=[t2_r])
                        k.op("pool", lambda e: e.tensor_tensor(
                            out=braw[j][:, :], in0=t2[:, :], in1=gu[j][:, :], op=ALU.mult),
                            reads=[t2_r, gu_r[j]], writes=[braw_r[j]])
                        k.op("act", lambda e: e.activation(
                            out=junk[:, 0:512], in_=braw[j][:, :], func=AF.Square,
                            accum_out=st[:, gp, 20 + j:21 + j]),
                            reads=[braw_r[j]], writes=[junk_r, R[5]])

                    def stage_S3S(G):
                        gp = G % 2
                        R = st_r[gp]
                        k.op("act", lambda e: e.activation(
                            out=st[:, gp, 24:28], in_=st[:, gp, 20:24], func=AF.Sqrt,
                            bias=eps_t[:, 0:1], scale=1.0 / 512), reads=[R[5]], writes=[R[6]])
                        k.op("dve", lambda e: e.reciprocal(out=st[:, gp, 28:32], in_=st[:, gp, 24:28]),
                             reads=[R[6]], writes=[R[7]])

                    def stage_SGU3(G, j):
                        gp = G % 2
                        R = st_r[gp]
                        i = 4 * G + j
                        bs = j % 2
                        k.op("dve", lambda e: e.tensor_scalar(
                            out=bn[bs][:, :], in0=braw[j][:, :], scalar1=st[:, gp, 28 + j:29 + j],
                            scalar2=None, op0=ALU.mult),
                            reads=[braw_r[j], R[7]], writes=[bn_r[bs]])
                        for c in range(4):
                            k.op("pe", lambda e, c=c: e.transpose(
                                bt_ps[:, c, :], bn[bs][:, c * 128:(c + 1) * 128], ident[:]),
                                reads=[bn_r[bs]], writes=[bt_r])
                        k.op("dve", lambda e: e.tensor_copy(
                            out=bT[:, :, i * 128:(i + 1) * 128], in_=bt_ps[:, :, :]),
                            reads=[bt_r], writes=[bT_r])

                    stage_A(0)
                    for j in range(4):
                        stage_VUV(0, j)
                    stage_LNS(0)
                    for G in range(1, 4):
                        stage_A(G)
                        for j in range(4):
                            stage_SGU2(G - 1, j)
                            stage_VUV(G, j)
                        stage_LNS(G)
                        stage_S3S(G - 1)
                        for j in range(4):
                            stage_SGU3(G - 1, j)
                    for j in range(4):
                        stage_SGU2(3, j)
                    stage_S3S(3)
                    for j in range(4):
                        stage_SGU3(3, j)
                    k.barrier(scratch[:, 0:1])

                with ExitStack() as e2:
                    aT = sb(e2, "aT", [128, 4, S], BF16)
                    aT_r = Reg()
                    with ExitStack() as e3:
                        QTz = [[sb(e3, "QTz%d_%d" % (p, i), [128, 512], BF16) for i in range(2)]
                               for p in range(2)]
                        QTz_r = [[Reg() for _ in range(2)] for _ in range(2)]
                        NE = 3
                        ebuf = [sb(e3, "ebuf%d" % i, [128, 512], F32) for i in range(NE)]
                        ebuf_r = [Reg() for _ in range(NE)]
                        NSP = 6
                        spb = [sb(e3, "spb%d" % i, [128, 512], BF16) for i in range(NSP)]
                        spb_r = [Reg() for _ in range(NSP)]
                        Sb = [sb(e3, "Sb%d" % i, [128, 512], BF16) for i in range(2)]
                        Sb_r = [Reg() for _ in range(2)]
                        NA = 4
                        Ab = [sb(e3, "Ab%d" % i, [128, 512], BF16) for i in range(NA)]
                        Ab_r = [Reg() for _ in range(NA)]
                        sq = [sb(e3, "sq%d" % i, [128, 512], BF16) for i in range(2)]
                        sq_r = [Reg() for _ in range(2)]
                        deferred = []
                        NF = 5
                        f_ps = [ps(e3, "f_ps%d" % i, [128, 512]) for i in range(NF)]
                        f_r = [Reg() for _ in range(NF)]
                        o_ps = [ps(e3, "o_ps%d" % i, [128, 512]) for i in range(2)]
                        o_r = [Reg() for _ in range(2)]
                        ssa_ps = ps(e3, "ssa_ps", [128, 16])
                        ssa_r = Reg()
                        ssa_sb_r = Reg()
                        r_att = Reg()
                        for p in range(2):
                            for i in range(2):
                                k.op("dve", lambda e, p=p, i=i: e.memset(QTz[p][i][:, :], 0.0),
                                     writes=[QTz_r[p][i]])

                        steps = []
                        nhead = 0
                        for qg in range(4):
                            for hp in range(4):
                                for hh in range(2):
                                    nk = 4 * qg + 4
                                    prev_c0 = None
                                    for si in range(nk):
                                        kb = nk - 1 - si
                                        j = kb - 4 * qg
                                        c0 = max(0, j) * 128
                                        steps.append(dict(qg=qg, hp=hp, hh=hh, kb=kb, first=(si == 0),
                                                          last=(si == nk - 1), j=j, c0=c0, pc0=prev_c0,
                                                          hidx=nhead))
                                        prev_c0 = c0
                                    nhead += 1
                        state = {"S": None}

                        head_first = [i for i, s_ in enumerate(steps) if s_["first"]]

                        def qcopy(hn):
                            if hn >= len(head_first):
                                return
                            s_ = steps[head_first[hn]]
                            hp, hh, qg = s_["hp"], s_["hh"], s_["qg"]
                            pb = 64 * hh
                            qb_ = (s_["hidx"] // 2) % 2
                            qz, qzr = QTz[hh][qb_], QTz_r[hh][qb_]
                            k.op("dve", lambda e: e.tensor_copy(
                                out=qz[pb:pb + 64, :], in_=QT[pb:pb + 64, hp, qg * 512:(qg + 1) * 512]),
                                reads=[r_att], writes=[qzr])

                        def s1p(i):
                            s_ = steps[i]
                            hp, hh, kb, qg, c0 = s_["hp"], s_["hh"], s_["kb"], s_["qg"], s_["c0"]
                            qb_ = (s_["hidx"] // 2) % 2
                            qz, qzr = QTz[hh][qb_], QTz_r[hh][qb_]
                            if s_["first"]:
                                qcopy(s_["hidx"] + 1)
                            fb = i % NF
                            k.op("pe", lambda e: e.matmul(
                                f_ps[fb][:, c0:512], lhsT=KT[:, hp, kb * 128:(kb + 1) * 128], rhs=qz[:, c0:512],
                                start=True, stop=True), reads=[qzr, r_att], writes=[f_r[fb]])

                        def s1a(i):
                            s_ = steps[i]
                            c0 = s_["c0"]
                            fb = i % NF
                            eb = i % NE
                            k.op("act", lambda e: e.activation(
                                out=ebuf[eb][:, c0:512], in_=f_ps[fb][:, c0:512], func=AF.Exp),
                                reads=[f_r[fb]], writes=[ebuf_r[eb]])

                        def s1b(i):
                            s_ = steps[i]
                            c0 = s_["c0"]
                            eb = i % NE
                            sb_i = i % NSP
                            k.op("act", lambda e: e.activation(
                                out=spb[sb_i][:, c0:512], in_=ebuf[eb][:, c0:512], func=AF.Ln,
                                bias=one_t[:, 0:1], scale=1.0), reads=[ebuf_r[eb]], writes=[spb_r[sb_i]])
                            if s_["j"] >= 0:
                                k.op("dve", lambda e: e.tensor_tensor(
                                    out=spb[sb_i][:, c0:c0 + 128], in0=spb[sb_i][:, c0:c0 + 128],
                                    in1=masks[:, :], op=ALU.mult),
                                    reads=[spb_r[sb_i]], writes=[spb_r[sb_i]])

                        def s2p(i):
                            s_ = steps[i]
                            fb = i % NF
                            sb_i = i % NSP
                            first = s_["first"]
                            c0, pc0 = s_["c0"], s_["pc0"]
                            k.op("pe", lambda e: e.matmul(
                                f_ps[fb][:, c0:512], lhsT=ntri[:, :], rhs=spb[sb_i][:, c0:512],
                                start=False, stop=first, skip_group_check=True),
                                reads=[spb_r[sb_i]], writes=[f_r[fb]])
                            if not first:
                                S_t, S_rg = state["S"]
                                k.op("pe", lambda e: e.matmul(
                                    f_ps[fb][:, pc0:512], lhsT=nones[:, :], rhs=S_t[:, pc0:512],
                                    start=False, stop=True, skip_group_check=True),
                                    reads=[S_rg], writes=[f_r[fb]])
                            if not s_["last"]:
                                if first:
                                    state["S"] = (spb[sb_i], spb_r[sb_i])
                                    state["Sn"] = 0
                                else:
                                    S_t, S_rg = state["S"]
                                    nb = state["Sn"]
                                    state["Sn"] = 1 - nb
                                    k.op("dve", lambda e: e.tensor_tensor(
                                        out=Sb[nb][:, pc0:512], in0=S_t[:, pc0:512], in1=spb[sb_i][:, pc0:512],
                                        op=ALU.add), reads=[S_rg, spb_r[sb_i]], writes=[Sb_r[nb]])
                                    if c0 < pc0:
                                        k.op("dve", lambda e: e.tensor_copy(
                                            out=Sb[nb][:, c0:pc0], in_=spb[sb_i][:, c0:pc0]),
                                            reads=[spb_r[sb_i]], writes=[Sb_r[nb]])
                                    state["S"] = (Sb[nb], Sb_r[nb])

                        def s2a(i):
                            s_ = steps[i]
                            fb = i % NF
                            c0 = s_["c0"]
                            ab = i % NA
                            k.op("act", lambda e: e.activation(
                                out=Ab[ab][:, c0:512], in_=f_ps[fb][:, c0:512], func=AF.Exp),
                                reads=[f_r[fb]], writes=[Ab_r[ab]])
                            if s_["j"] >= 0:
                                k.op("dve", lambda e: e.tensor_tensor(
                                    out=Ab[ab][:, c0:c0 + 128], in0=Ab[ab][:, c0:c0 + 128],
                                    in1=masks[:, :], op=ALU.mult),
                                    reads=[Ab_r[ab]], writes=[Ab_r[ab]])

                        def s3(i):
                            s_ = steps[i]
                            hp, hh, kb, qg, c0 = s_["hp"], s_["hh"], s_["kb"], s_["qg"], s_["c0"]
                            pb = 64 * hh
                            ab = i % NA
                            ob = s_["hidx"] % 2
                            k.op("pe", lambda e: e.matmul(
                                o_ps[ob][:, c0:512], lhsT=Vsb[:, kb, hp * 128:(hp + 1) * 128], rhs=Ab[ab][:, c0:512],
                                start=s_["first"], stop=s_["last"], skip_group_check=True),
                                reads=[Ab_r[ab], r_att], writes=[o_r[ob]])
                            if s_["last"]:
                                cs = slice(qg * 512, (qg + 1) * 512)
                                k.op("dve", lambda e: e.tensor_copy(
                                    out=aT[pb:pb + 64, hp, cs], in_=o_ps[ob][pb:pb + 64, :]),
                                    reads=[o_r[ob]], writes=[aT_r])
                                sqb = (s_["hidx"] // 2) % 2
                                k.op("dve", lambda e: e.tensor_tensor(
                                    out=sq[sqb][pb:pb + 64, :], in0=aT[pb:pb + 64, hp, cs],
                                    in1=aT[pb:pb + 64, hp, cs], op=ALU.mult), reads=[aT_r], writes=[sq_r[sqb]])
                                if hh == 1:
                                    deferred.append((i + 2, hp, qg, sqb))

                        def run_deferred(i):
                            while deferred and deferred[0][0] <= i:
                                _, hp, qg, sqb = deferred.pop(0)
                                for jj in range(4):
                                    k.op("pe", lambda e, jj=jj: e.matmul(
                                        ssa_ps[:, hp * 4 + jj:hp * 4 + jj + 1],
                                        lhsT=sq[sqb][:, jj * 128:(jj + 1) * 128], rhs=onesb[:, 0:1],
                                        start=True, stop=True), reads=[sq_r[sqb], r_const], writes=[ssa_r])
                                if hp == 3:
                                    k.op("dve", lambda e: e.tensor_reduce(
                                        out=ssa_sb[:, qg * 4:(qg + 1) * 4],
                                        in_=ssa_ps[:, 0:16].rearrange("p (h j) -> p j h", h=4),
                                        axis=AX.X, op=ALU.add), reads=[ssa_r], writes=[ssa_sb_r])

                        nst = len(steps)
                        qcopy(0)
                        s1p(0)
                        for it in range(nst + 3):
                            if it < nst:
                                s1a(it)
                            if 0 <= it - 1 < nst:
                                s2p(it - 1)
                            if 0 <= it - 3 < nst:
                                s3(it - 3)
                            run_deferred(it - 3 if it < nst + 2 else 10 ** 9)
                            if it + 1 < nst:
                                s1p(it + 1)
                            if 0 <= it - 2 < nst:
                                s2a(it - 2)
                            if it < nst:
                                s1b(it)
                        k.barrier(scratch[:, 0:1])

                    with ExitStack() as e4:
                        xr = [sb(e4, "xr%d" % i, [128, D], F32) for i in range(3)]
                        xr_r = [Reg() for _ in range(3)]
                        xr_ch = [k.chan() for _ in range(3)]
                        xo_ch = [k.chan() for _ in range(3)]
                        sda = sb(e4, "sda", [128, NT], F32)
                        rsa = sb(e4, "rsa", [128, NT], F32)
                        rsa_r = Reg()
                        acc = [[ps(e4, "acc%d_%d" % (i, h), [128, 512]) for h in range(4)] for i in range(2)]
                        acc_r = [[Reg() for _ in range(4)] for _ in range(2)]
                        r_all = Reg()
                        k.op("act", lambda e: e.activation(
                            out=sda[:, :], in_=ssa_sb[:, :], func=AF.Sqrt, bias=eps_t[:, 0:1],
                            scale=1.0 / 512), reads=[r_all], writes=[rsa_r])
                        k.op("dve", lambda e: e.reciprocal(out=rsa[:, :], in_=sda[:, :]),
                             reads=[rsa_r], writes=[rsa_r])
                        for i in range(NT):
                            sl = i % 3
                            ab = i % 2
                            k.dma("sp", xr_ch[sl], xr[sl][:, :], x[b, i * 128:(i + 1) * 128, :],
                                  writes=[xr_r[sl]])
                            for hf in range(2):
                                for c in range(4):
                                    k.op("pe", lambda e, c=c, hf=hf: e.matmul(
                                        acc[ab][hf][:, :], lhsT=aT[:, c, i * 128:(i + 1) * 128],
                                        rhs=w_out_bf[:, c, hf * 512:(hf + 1) * 512],
                                        start=(c == 0), stop=(c == 3)), reads=[r_all], writes=[acc_r[ab][hf]])
                                for c in range(4):
                                    k.op("pe", lambda e, c=c, hf=hf: e.matmul(
                                        acc[ab][2 + hf][:, :], lhsT=bT[:, c, i * 128:(i + 1) * 128],
                                        rhs=w_out_bf[:, 4 + c, hf * 512:(hf + 1) * 512],
                                        start=(c == 0), stop=(c == 3)), reads=[r_all], writes=[acc_r[ab][2 + hf]])
                            for hf in range(2):
                                cs = slice(hf * 512, (hf + 1) * 512)
                                k.op("dve", lambda e, hf=hf, cs=cs: e.scalar_tensor_tensor(
                                    out=xr[sl][:, cs], in0=acc[ab][hf][:, :], scalar=rsa[:, i:i + 1],
                                    in1=xr[sl][:, cs], op0=ALU.mult, op1=ALU.add),
                                    reads=[acc_r[ab][hf], rsa_r, xr_r[sl]], writes=[xr_r[sl]])
                                k.op("dve", lambda e, hf=hf, cs=cs: e.tensor_tensor(
                                    out=xr[sl][:, cs], in0=xr[sl][:, cs], in1=acc[ab][2 + hf][:, :], op=ALU.add),
                                    reads=[acc_r[ab][2 + hf], xr_r[sl]], writes=[xr_r[sl]])
                            k.dma("sp", xo_ch[sl], y[b, i * 128:(i + 1) * 128, :], xr[sl][:, :],
                                  reads=[xr_r[sl]], writes=[y_reg[b][i]])
                        k.barrier(scratch[:, 0:1])

        with ExitStack() as esB:
            w_up_bf = sb(esB, "w_up_bf", [128, 8, DFF], BF16)
            w_dn_bf = sb(esB, "w_dn_bf", [128, 32, D], BF16)
            gf = sb(esB, "gf", [128, D], F32)
            gmlpf = sb(esB, "gmlpf", [128, D], F32)
            r_wB = Reg()
            wch2 = k.chan()
            k.dma("sp", wch2, gf[:], gf_d[:, :], writes=[r_wB])
            k.dma("sp", wch2, gmlpf[:], gmlpf_d[:, :], writes=[r_wB])
            wup_ch = k.chan()
            wdn_ch = k.chan()
            r_wup1 = Reg()
            r_wdn1 = Reg()
            for c in range(8):
                k.dma("pool", wup_ch, w_up_bf[:, c, :], w_up[c * 128:(c + 1) * 128, :], writes=[r_wup1])
            for q in range(4):
                k.dma("pool", wdn_ch, w_dn_bf[:, q * 8:(q + 1) * 8, :],
                      w_down[q * 1024:(q + 1) * 1024, :].rearrange("(c p) d -> p c d", p=128),
                      writes=[r_wdn1])
            r_wup = [r_wup1] * 4
            r_wdn = [r_wdn1] * 4

            with ExitStack() as e5:
                NPS = 2
                xp = [sb(e5, "xp%d" % i, [128, D], F32) for i in range(NPS)]
                xp_r = [Reg() for _ in range(NPS)]
                xp_ch = [k.chan() for _ in range(NPS)]
                NES = 2
                xe = [sb(e5, "xe%d" % i, [128, D], F32) for i in range(NES)]
                xe_r = [Reg() for _ in range(NES)]
                xe_ch = [k.chan() for _ in range(NES)]
                xo_ch = [k.chan() for _ in range(NES)]
                xn = [sb(e5, "xnB%d" % i, [128, D], BF16) for i in range(2)]
                xn_r = [Reg() for _ in range(2)]
                h2T = sb(e5, "h2T", [128, 8, 512], BF16)
                h2_r = [Reg() for _ in range(4)]
                actT = sb(e5, "actT", [128, 32, 512], BF16)
                act_r = [Reg() for _ in range(32)]
                rr = [sb(e5, "rr%d" % i, [128, 512], F32) for i in range(2)]
                rr_r = [Reg() for _ in range(2)]
                NST = 4
                stp = sb(e5, "stp", [128, NST, 4], F32)
                stp_r = [Reg() for _ in range(NST)]
                ste = sb(e5, "ste", [128, NST, 4], F32)
                ste_r = [Reg() for _ in range(NST)]
                tp_ps = ps(e5, "tpB", [128, 8, 128], BF16)
                tp_r = Reg()
                up_ps = [ps(e5, "up_ps%d" % i, [128, 512]) for i in range(3)]
                up_r = [Reg() for _ in range(3)]
                dn_ps = [ps(e5, "dn_ps%d" % i, [128, 512]) for i in range(4)]
                dn_r = [Reg() for _ in range(4)]
                junkB = sb(e5, "junkB", [128, D], BF16)
                junkB_r = Reg()
                cnt = {"p": 0, "e": 0, "up": 0, "dn": 0}

                def prologue(blk):
                    b = blk // 4
                    G = blk % 4
                    for j in range(4):
                        i = 4 * G + j
                        n = cnt["p"]
                        cnt["p"] += 1
                        sl = n % NPS
                        xs = n % 2
                        q = n % NST
                        k.dma("sp", xp_ch[sl], xp[sl][:, :], y[b, i * 128:(i + 1) * 128, :],
                              reads=[y_reg[b][i]], writes=[xp_r[sl]])
                        k.op("act", lambda e, sl=sl, xs=xs, q=q: e.activation(
                            out=xn[xs][:, :], in_=xp[sl][:, :], func=AF.Square, accum_out=stp[:, q, 0:1]),
                            reads=[xp_r[sl]], writes=[xn_r[xs], stp_r[q]])
                        k.op("act", lambda e, q=q: e.activation(
                            out=stp[:, q, 1:2], in_=stp[:, q, 0:1], func=AF.Sqrt, bias=eps_t[:, 0:1],
                            scale=1.0 / D), reads=[stp_r[q]], writes=[stp_r[q]])
                        k.op("dve", lambda e, q=q: e.reciprocal(out=stp[:, q, 2:3], in_=stp[:, q, 1:2]),
                             reads=[stp_r[q]], writes=[stp_r[q]])
                        k.op("dve", lambda e, sl=sl, xs=xs, q=q: e.scalar_tensor_tensor(
                            out=xn[xs][:, :], in0=xp[sl][:, :], scalar=stp[:, q, 2:3], in1=gmlpf[:, :],
                            op0=ALU.mult, op1=ALU.mult), reads=[xp_r[sl], stp_r[q], r_wB], writes=[xn_r[xs]])
                        for c in range(8):
                            k.op("pe", lambda e, c=c, xs=xs: e.transpose(
                                tp_ps[:, c, :], xn[xs][:, c * 128:(c + 1) * 128], ident[:]),
                                reads=[xn_r[xs]], writes=[tp_r])
                        k.op("act", lambda e, j=j: e.copy(
                            out=h2T[:, :, j * 128:(j + 1) * 128], in_=tp_ps[:, :, :]),
                            reads=[tp_r], writes=[h2_r[j]])

                def up(blk):
                    for fc in range(32):
                        ub = cnt["up"] % 3
                        rb = cnt["up"] % 2
                        cnt["up"] += 1
                        for c in range(8):
                            k.op("pe", lambda e, c=c, fc=fc, ub=ub: e.matmul(
                                up_ps[ub][:, :], lhsT=w_up_bf[:, c, fc * 128:(fc + 1) * 128],
                                rhs=h2T[:, c, :], start=(c == 0), stop=(c == 7)),
                                reads=h2_r + [r_wup[fc // 8]], writes=[up_r[ub]])
                        k.op("act", lambda e, ub=ub, rb=rb: e.activation(
                            out=rr[rb][:, :], in_=up_ps[ub][:, :], func=AF.Relu),
                            reads=[up_r[ub]], writes=[rr_r[rb]])
                        k.op("pool", lambda e, fc=fc, rb=rb: e.tensor_tensor(
                            out=actT[:, fc, :], in0=rr[rb][:, :], in1=rr[rb][:, :], op=ALU.mult),
                            reads=[rr_r[rb]], writes=[act_r[fc]])

                def down_epi(blk):
                    b = blk // 4
                    G = blk % 4
                    for j in range(4):
                        i = 4 * G + j
                        n = cnt["e"]
                        cnt["e"] += 1
                        sl = n % NES
                        q = n % NST
                        k.dma("sp", xe_ch[sl], xe[sl][:, :], y[b, i * 128:(i + 1) * 128, :],
                              reads=[y_reg[b][i]], writes=[xe_r[sl]])
                        for hf in range(2):
                            db = cnt["dn"] % 4
                            cnt["dn"] += 1
                            cs = slice(hf * 512, (hf + 1) * 512)
                            for fc in range(32):
                                k.op("pe", lambda e, fc=fc, db=db, j=j, hf=hf: e.matmul(
                                    dn_ps[db][:, :], lhsT=actT[:, fc, j * 128:(j + 1) * 128],
                                    rhs=w_dn_bf[:, fc, hf * 512:(hf + 1) * 512],
                                    start=(fc == 0), stop=(fc == 31)),
                                    reads=[act_r[fc], r_wdn[fc // 8]], writes=[dn_r[db]])
                            k.op("dve", lambda e, db=db, sl=sl, cs=cs: e.tensor_tensor(
                                out=xe[sl][:, cs], in0=xe[sl][:, cs], in1=dn_ps[db][:, :], op=ALU.add),
                                reads=[dn_r[db], xe_r[sl]], writes=[xe_r[sl]])
                        xs = n % 2
                        k.op("act", lambda e, sl=sl, q=q: e.activation(
                            out=junkB[:, :], in_=xe[sl][:, :], func=AF.Square,
                            accum_out=ste[:, q, 0:1]),
                            reads=[xe_r[sl]], writes=[junkB_r, ste_r[q]])
                        k.op("act", lambda e, q=q: e.activation(
                            out=ste[:, q, 1:2], in_=ste[:, q, 0:1], func=AF.Sqrt, bias=eps_t[:, 0:1],
                            scale=1.0 / D), reads=[ste_r[q]], writes=[ste_r[q]])
                        k.op("dve", lambda e, q=q: e.reciprocal(out=ste[:, q, 2:3], in_=ste[:, q, 1:2]),
                             reads=[ste_r[q]], writes=[ste_r[q]])
                        k.op("dve", lambda e, sl=sl, q=q: e.scalar_tensor_tensor(
                            out=xe[sl][:, :], in0=xe[sl][:, :], scalar=ste[:, q, 2:3], in1=gf[:, :],
                            op0=ALU.mult, op1=ALU.mult), reads=[xe_r[sl], ste_r[q], r_wB], writes=[xe_r[sl]])
                        k.dma("sp", xo_ch[sl], y[b, i * 128:(i + 1) * 128, :], xe[sl][:, :],
                              reads=[xe_r[sl]], writes=[y_reg[b][i]])

                NBLK = BPC * 4
                prologue(0)
                for blk in range(NBLK):
                    up(blk)
                    if blk + 1 < NBLK:
                        prologue(blk + 1)
                    down_epi(blk)
                k.barrier(scratch[:, 0:1])
    return nc


def _host_consts():
    ident = np.eye(128, dtype=np.float32)
    jj = np.arange(128)[:, None]
    ss = np.arange(128)[None, :]
    ntri = -(jj >= ss).astype(np.float32)
    nones = -np.ones((128, 128), np.float32)
    s_idx = np.arange(128)[:, None]
    t_idx = np.arange(128)[None, :]
    masks = (s_idx < t_idx).astype(np.float32)
    return ident, ntri, nones, masks


def kernel(x, norm_mix_g, w_in, sg_ln_g, sg_ln_b, sg_w, sg_b, out_norm_g, w_out,
           norm_mlp_g, w_up, w_down, norm_final_g):
    f = np.float32
    x = np.ascontiguousarray(np.asarray(x, dtype=f))
    ident, ntri, nones, masks = _host_consts()

    def pc(v):
        return np.ascontiguousarray(np.asarray(v, dtype=f).reshape(8, 128).T)

    shared = {
        "w_in": np.ascontiguousarray(np.asarray(w_in, dtype=f)[0]),
        "w_out": np.ascontiguousarray(np.asarray(w_out, dtype=f)[0]),
        "w_up": np.ascontiguousarray(np.asarray(w_up, dtype=f)[0]),
        "w_down": np.ascontiguousarray(np.asarray(w_down, dtype=f)[0]),
        "g_mix": pc(norm_mix_g[0]),
        "g_out": pc(out_norm_g[0]),
        "g_mlp": pc(norm_mlp_g[0]),
        "lng_full": np.ascontiguousarray(np.broadcast_to(np.asarray(sg_ln_g, dtype=f)[0][None, :], (128, 512))),
        "lnb_full": np.ascontiguousarray(np.broadcast_to(np.asarray(sg_ln_b, dtype=f)[0][None, :], (128, 512))),
        "bias_full": np.ascontiguousarray(
            np.broadcast_to(np.asarray(sg_b, dtype=f)[0].T[:, :, None], (128, 8, 64)).reshape(128, 512)),
        "gmix_full": np.ascontiguousarray(np.broadcast_to(np.asarray(norm_mix_g, dtype=f)[0][None, :], (128, D))),
        "gmlp_full": np.ascontiguousarray(np.broadcast_to(np.asarray(norm_mlp_g, dtype=f)[0][None, :], (128, D))),
        "gf_full": np.ascontiguousarray(np.broadcast_to(np.asarray(norm_final_g, dtype=f)[None, :], (128, D))),
        "sgwT": np.ascontiguousarray(np.transpose(np.asarray(sg_w, dtype=f)[0], (2, 0, 1))),
        "ident": ident, "ntri": ntri, "nones": nones, "masks": masks,
    }
    nc = build()
    in_maps = []
    for c in range(NCORES):
        m = dict(shared)
        m["x"] = np.ascontiguousarray(x[c * BPC:(c + 1) * BPC])
        in_maps.append(m)
    res = run_bass_kernel_spmd(nc, in_maps, core_ids=list(range(NCORES)))
    out = np.concatenate([np.asarray(r["y"]) for r in res.results], axis=0)
    return out.astype(np.float32)
```
